# Optimizing a Trainium2 kernel written in Bass

```python
import numpy as np
import jax
import jax.numpy as jnp
from jax import lax

D_MODEL = 2048
BATCH = 4
SEQ = 8192
DEPTH = 2

GRID_W = 64
CTX_LEN = 256
D_MIX = D_MODEL
NA_HEAD_DIM = 128
NA_HEADS = (D_MIX // 2) // NA_HEAD_DIM
NA_WIDTH = NA_HEADS * NA_HEAD_DIM
NA_ROWS = 8
NA_COLS = 16
NA_KEY_COLS = 2 * NA_COLS
HG_WIDTH = D_MIX // 4
HG_EXPAND = 128
HG_HEADS = HG_WIDTH // HG_EXPAND
HG_HEAD_V = HG_WIDTH // HG_HEADS
GLA_WIDTH = D_MIX - NA_WIDTH - HG_WIDTH
GLA_HEADS = 4
GLA_KEY_WIDTH = GLA_WIDTH // 2
GLA_HEAD_K = GLA_KEY_WIDTH // GLA_HEADS
GLA_HEAD_V = GLA_WIDTH // GLA_HEADS
GLA_RANK = 16
GLA_GATE_NORM = 16.0
CHUNK = 64
D_FF = 5632
CONV_WIDTH = 3
ROPE_BASE = 10000.0
EPS = 1e-6
SPLIT_SIZES = (NA_WIDTH, NA_WIDTH, NA_WIDTH,
               HG_WIDTH, HG_WIDTH, HG_WIDTH, HG_WIDTH, HG_WIDTH,
               GLA_KEY_WIDTH, GLA_KEY_WIDTH, GLA_WIDTH, GLA_WIDTH, GLA_RANK, GLA_RANK)
IN_COLS = sum(SPLIT_SIZES)

kernel_name = "hybrid_na_hgrn2_gla_prefix_dit"


def rms_norm(x, g):
    xf = x.astype(jnp.float32)
    y = xf * lax.rsqrt(jnp.mean(xf * xf, axis=-1, keepdims=True) + EPS)
    return (y * g.astype(jnp.float32)).astype(x.dtype)


def modulate(h, shift, scale):
    return h * (1.0 + scale) + shift


def heads(a, n):
    return a.reshape(*a.shape[:-1], n, a.shape[-1] // n)


def split_points():
    return [int(v) for v in np.cumsum(SPLIT_SIZES)[:-1]]


def axial_rope(x, rows, cols):
    half = x.shape[-1] // 2
    inv = ROPE_BASE ** (-jnp.arange(0, half, 2, dtype=jnp.float32) / half)

    def rot(xp, pos):
        ang = pos.astype(jnp.float32)[:, None] * inv
        cos = jnp.cos(ang)[None, :, None, :]
        sin = jnp.sin(ang)[None, :, None, :]
        x1, x2 = jnp.split(xp.astype(jnp.float32), 2, axis=-1)
        return jnp.concatenate([x1 * cos - x2 * sin, x1 * sin + x2 * cos], axis=-1)

    return jnp.concatenate([rot(x[..., :half], rows), rot(x[..., half:], cols)], axis=-1).astype(x.dtype)


def hgrn_lower_bounds(p):
    s = jax.nn.softmax(p.astype(jnp.float32), axis=1)
    return jnp.cumsum(s, axis=1) - s[:, :1]


def hgrn2_forget(z, lb):
    zf = z.astype(jnp.float32)
    f = lb + (1.0 - lb) * jax.nn.sigmoid(zf)
    return (1.0 - lb) * jax.nn.sigmoid(-zf), jnp.log(f)


def gla_log_gate(r, up, b):
    return jax.nn.log_sigmoid((r @ up + b).astype(jnp.float32)) / GLA_GATE_NORM


def chunk_scan(q, k, v, log_g, s0):
    b_, t_, h_, _ = q.shape
    dv = v.shape[-1]
    n = t_ // CHUNK

    def to_chunks(a):
        return a.astype(jnp.float32).reshape(b_, n, CHUNK, h_, a.shape[-1]).transpose(1, 0, 3, 2, 4)

    lower = jnp.tril(jnp.ones((CHUNK, CHUNK), dtype=bool))[:, :, None]

    def step(s, blk):
        qc, kc, vc, gc = blk
        cum = jnp.cumsum(gc, axis=2)
        o_inter = jnp.einsum('bhtd,bhde->bhte', qc * jnp.exp(cum), s)
        rel = cum[:, :, :, None, :] - cum[:, :, None, :, :]
        decay = jnp.exp(jnp.where(lower, rel, -jnp.inf))
        att = jnp.einsum('bhtd,bhsd,bhtsd->bhts', qc, kc, decay)
        o = o_inter + jnp.einsum('bhts,bhse->bhte', att, vc)
        last = cum[:, :, -1:, :]
        s_new = jnp.exp(last[:, :, 0, :, None]) * s + jnp.einsum('bhsd,bhse->bhde', kc * jnp.exp(last - cum), vc)
        return s_new, o

    s_fin, o = lax.scan(step, s0, (to_chunks(q), to_chunks(k), to_chunks(v), to_chunks(log_g)))
    o = o.transpose(1, 0, 3, 2, 4).reshape(b_, t_, h_, dv)
    return o.astype(v.dtype), s_fin


def bidir_prefix_scan(q, v, k_f, g_f, k_b, g_b, cq, cv, ck_f, cg_f, ck_b, cg_b):
    b_, _, h_, dk = q.shape
    s0 = jnp.zeros((b_, h_, dk, v.shape[-1]), jnp.float32)
    rev = lambda a: a[:, ::-1]
    c_of, s_f = chunk_scan(cq, ck_f, cv, cg_f, s0)
    c_ob, s_b = chunk_scan(rev(cq), rev(ck_b), rev(cv), rev(cg_b), s0)
    o_f, _ = chunk_scan(q, k_f, v, g_f, s_f)
    o_b, _ = chunk_scan(rev(q), rev(k_b), rev(v), rev(g_b), s_b)
    return o_f + rev(o_b), c_of + rev(c_ob)


def neighborhood_attention(q, k, v, k_ctx, v_ctx, rpb):
    b_, t_, h_, d = q.shape
    rows = t_ // GRID_W
    kr = min(NA_ROWS, rows)
    n_cb = GRID_W // NA_COLS
    scale = d ** -0.5
    to_grid = lambda a: a.transpose(0, 2, 1, 3).reshape(b_, h_, rows, GRID_W, d)
    qg, kg, vg = to_grid(q), to_grid(k), to_grid(v)
    kc = k_ctx.transpose(0, 2, 1, 3)
    vc = v_ctx.transpose(0, 2, 1, 3)
    qcol = np.arange(GRID_W).reshape(n_cb, NA_COLS)
    cstart = np.clip(qcol - NA_COLS // 2, 0, GRID_W - NA_COLS)
    kstart = np.clip(np.arange(n_cb) * NA_COLS - NA_COLS // 2, 0, GRID_W - NA_KEY_COLS)
    kcol = kstart[:, None] + np.arange(NA_KEY_COLS)
    col_ok = (kcol[:, None, :] >= cstart[:, :, None]) & (kcol[:, None, :] < cstart[:, :, None] + NA_COLS)
    col_ok = jnp.asarray(col_ok[:, :, None, :])
    dc_idx = np.clip(kcol[:, None, :] - qcol[:, :, None], -(NA_COLS - 1), NA_COLS - 1) + NA_COLS - 1
    rpb_g = rpb[:, :, dc_idx].transpose(0, 2, 3, 1, 4)
    n_loc = kr * NA_KEY_COLS

    def row_fn(r):
        rs = jnp.clip(r - kr // 2, 0, rows - kr)
        q_r = lax.dynamic_index_in_dim(qg, r, axis=2, keepdims=False).reshape(b_, h_, n_cb, NA_COLS, d)
        k_blk = lax.dynamic_slice_in_dim(kg, rs, kr, axis=2)[:, :, :, kcol]
        v_blk = lax.dynamic_slice_in_dim(vg, rs, kr, axis=2)[:, :, :, kcol]
        bias = jnp.take(rpb_g, rs + jnp.arange(kr) - r + NA_ROWS - 1, axis=3)
        s_loc = jnp.einsum('bhjqd,bhrjkd->bhjqrk', q_r, k_blk).astype(jnp.float32) * scale + bias
        s_loc = jnp.where(col_ok, s_loc, -jnp.inf)
        s_ctx = jnp.einsum('bhjqd,bhld->bhjql', q_r, kc).astype(jnp.float32) * scale
        s = jnp.concatenate([s_loc.reshape(b_, h_, n_cb, NA_COLS, n_loc), s_ctx], axis=-1)
        p = jax.nn.softmax(s, axis=-1).astype(v.dtype)
        p_loc = p[..., :n_loc].reshape(b_, h_, n_cb, NA_COLS, kr, NA_KEY_COLS)
        o = (jnp.einsum('bhjqrk,bhrjkd->bhjqd', p_loc, v_blk)
             + jnp.einsum('bhjql,bhld->bhjqd', p[..., n_loc:], vc))
        return o.reshape(b_, h_, GRID_W, d)

    out = lax.map(row_fn, jnp.arange(rows))
    return out.transpose(1, 0, 3, 2, 4).reshape(b_, t_, h_, d)


def context_attention(q, k, v):
    s = jnp.einsum('blhd,bmhd->bhlm', q, k).astype(jnp.float32) * (q.shape[-1] ** -0.5)
    p = jax.nn.softmax(s, axis=-1).astype(v.dtype)
    return jnp.einsum('bhlm,bmhd->blhd', p, v)


def mixer_features(h, w_in, lb_f, lb_b, gate_up, gate_b):
    p = h @ w_in
    (na_q, na_k, na_v, hg_q, hg_i, hg_zf, hg_zb, hg_g,
     gl_q, gl_k, gl_v, gl_g, gl_rf, gl_rb) = jnp.split(p, split_points(), axis=-1)
    hg_kf, hg_lf = hgrn2_forget(hg_zf, lb_f)
    hg_kb, hg_lb = hgrn2_forget(hg_zb, lb_b)
    return dict(
        na_q=heads(na_q, NA_HEADS), na_k=heads(na_k, NA_HEADS), na_v=heads(na_v, NA_HEADS),
        hg_q=heads(jax.nn.silu(hg_q), HG_HEADS), hg_v=heads(hg_i, HG_HEADS),
        hg_kf=heads(hg_kf, HG_HEADS), hg_lf=heads(hg_lf, HG_HEADS),
        hg_kb=heads(hg_kb, HG_HEADS), hg_lb=heads(hg_lb, HG_HEADS), hg_g=hg_g,
        gl_q=heads(gl_q * (GLA_HEAD_K ** -0.5), GLA_HEADS), gl_k=heads(gl_k, GLA_HEADS),
        gl_v=heads(gl_v, GLA_HEADS), gl_g=gl_g,
        gl_lf=heads(gla_log_gate(gl_rf, gate_up[0], gate_b[0]), GLA_HEADS),
        gl_lb=heads(gla_log_gate(gl_rb, gate_up[1], gate_b[1]), GLA_HEADS),
    )


def norm_gate(o, norm_g, g):
    o = rms_norm(o, norm_g)
    return o.reshape(*o.shape[:2], -1) * jax.nn.silu(g)


def conv_ffn(h, w_up, conv_w, conv_b, w_down):
    u = h @ w_up
    u = lax.conv_general_dilated(u, conv_w[:, None, :], window_strides=(1,),
                                 padding=((CONV_WIDTH // 2, CONV_WIDTH // 2),),
                                 dimension_numbers=('NWC', 'WIO', 'NWC'),
                                 feature_group_count=u.shape[-1]) + conv_b
    a, b = jnp.split(u, 2, axis=-1)
    return (jax.nn.silu(a) * b) @ w_down


def trunk_layer(x, xc, mod, mod_c, norm1_g, w_in, rpb, lb_f, lb_b, gate_up, gate_b,
                hg_norm_g, gla_norm_g, w_out, norm2_g, w_up, conv_w, conv_b, w_down, update_ctx):
    sh1, sc1, gt1, sh2, sc2, gt2 = [m[:, None] for m in jnp.split(mod, 6, axis=-1)]
    csh1, csc1, cgt1, csh2, csc2, cgt2 = jnp.split(mod_c, 6, axis=-1)

    lat = mixer_features(modulate(rms_norm(x, norm1_g), sh1, sc1), w_in, lb_f, lb_b, gate_up, gate_b)
    cf = mixer_features(modulate(rms_norm(xc, norm1_g), csh1, csc1), w_in, lb_f, lb_b, gate_up, gate_b)

    t_ = x.shape[1]
    pos = jnp.arange(t_)
    rows, cols = pos // GRID_W, pos % GRID_W
    gl_q = axial_rope(lat['gl_q'], rows, cols)
    gl_k = axial_rope(lat['gl_k'], rows, cols)

    na_o = neighborhood_attention(lat['na_q'], lat['na_k'], lat['na_v'], cf['na_k'], cf['na_v'], rpb)
    hg_o, hg_co = bidir_prefix_scan(lat['hg_q'], lat['hg_v'], lat['hg_kf'], lat['hg_lf'], lat['hg_kb'], lat['hg_lb'],
                                    cf['hg_q'], cf['hg_v'], cf['hg_kf'], cf['hg_lf'], cf['hg_kb'], cf['hg_lb'])
    gl_o, gl_co = bidir_prefix_scan(gl_q, lat['gl_v'], gl_k, lat['gl_lf'], gl_k, lat['gl_lb'],
                                    cf['gl_q'], cf['gl_v'], cf['gl_k'], cf['gl_lf'], cf['gl_k'], cf['gl_lb'])

    mix = jnp.concatenate([na_o.reshape(*na_o.shape[:2], NA_WIDTH),
                           norm_gate(hg_o, hg_norm_g, lat['hg_g']),
                           norm_gate(gl_o, gla_norm_g, lat['gl_g'])], axis=-1) @ w_out
    x = x + gt1 * mix
    x = x + gt2 * conv_ffn(modulate(rms_norm(x, norm2_g), sh2, sc2), w_up, conv_w, conv_b, w_down)

    if update_ctx:
        c_na = context_attention(cf['na_q'], cf['na_k'], cf['na_v'])
        c_mix = jnp.concatenate([c_na.reshape(*c_na.shape[:2], NA_WIDTH),
                                 norm_gate(hg_co, hg_norm_g, cf['hg_g']),
                                 norm_gate(gl_co, gla_norm_g, cf['gl_g'])], axis=-1) @ w_out
        xc = xc + cgt1 * c_mix
        xc = xc + cgt2 * conv_ffn(modulate(rms_norm(xc, norm2_g), csh2, csc2), w_up, conv_w, conv_b, w_down)
    return x, xc


def setup_inputs(seed: int = 0) -> dict:
    key = jax.random.key(seed)
    ks = jax.random.split(key, 24)
    f32 = jnp.float32
    nrm = lambda k, shape, s: (jax.random.normal(k, shape, f32) * s)
    d = D_MODEL
    return {
        "x": nrm(ks[0], (BATCH, SEQ, d), 1.0),
        "c": nrm(ks[1], (BATCH, d), 1.0),
        "ctx": nrm(ks[2], (BATCH, CTX_LEN, d), 1.0),
        "c_ctx": nrm(ks[3], (d,), 1.0),
        "ada_w": nrm(ks[4], (DEPTH, d, 6 * d), d ** -0.5),
        "ada_b": nrm(ks[5], (DEPTH, 6 * d), 0.01),
        "norm1_g": 1.0 + nrm(ks[6], (DEPTH, d), 0.05),
        "w_in": nrm(ks[7], (DEPTH, d, IN_COLS), d ** -0.5),
        "na_rpb": nrm(ks[8], (DEPTH, NA_HEADS, 2 * NA_ROWS - 1, 2 * NA_COLS - 1), 0.1),
        "hg_lower_bounds": nrm(ks[9], (2, DEPTH, HG_WIDTH), 1.0),
        "hg_norm_g": 1.0 + nrm(ks[10], (DEPTH, HG_HEAD_V), 0.05),
        "gla_gate_up": nrm(ks[11], (DEPTH, 2, GLA_RANK, GLA_KEY_WIDTH), GLA_RANK ** -0.5),
        "gla_gate_b": nrm(ks[12], (DEPTH, 2, GLA_KEY_WIDTH), 0.1),
        "gla_norm_g": 1.0 + nrm(ks[13], (DEPTH, GLA_HEAD_V), 0.05),
        "w_out": nrm(ks[14], (DEPTH, D_MIX, d), D_MIX ** -0.5),
        "norm2_g": 1.0 + nrm(ks[15], (DEPTH, d), 0.05),
        "w_up": nrm(ks[16], (DEPTH, d, 2 * D_FF), d ** -0.5),
        "conv_w": nrm(ks[17], (DEPTH, CONV_WIDTH, 2 * D_FF), CONV_WIDTH ** -0.5),
        "conv_b": nrm(ks[18], (DEPTH, 2 * D_FF), 0.01),
        "w_down": nrm(ks[19], (DEPTH, D_FF, d), D_FF ** -0.5),
        "final_g": 1.0 + nrm(ks[20], (d,), 0.05),
    }


def reference(x, c, ctx, c_ctx, ada_w, ada_b, norm1_g, w_in, na_rpb, hg_lower_bounds, hg_norm_g,
              gla_gate_up, gla_gate_b, gla_norm_g, w_out, norm2_g, w_up, conv_w, conv_b, w_down, final_g):
    lb = hgrn_lower_bounds(hg_lower_bounds)
    sc = jax.nn.silu(c)
    scc = jax.nn.silu(c_ctx)
    xc = ctx
    for l in range(DEPTH):
        mod = sc @ ada_w[l] + ada_b[l]
        mod_c = scc @ ada_w[l] + ada_b[l]
        x, xc = trunk_layer(x, xc, mod, mod_c, norm1_g[l], w_in[l], na_rpb[l], lb[0, l], lb[1, l],
                            gla_gate_up[l], gla_gate_b[l], hg_norm_g[l], gla_norm_g[l], w_out[l],
                            norm2_g[l], w_up[l], conv_w[l], conv_b[l], w_down[l], l < DEPTH - 1)
    return rms_norm(x, final_g)
```

```python
import numpy as np
from contextlib import ExitStack
import concourse.bass as bass
import concourse.mybir as mybir
from concourse.bass_utils import run_bass_kernel_spmd

F32 = mybir.dt.float32
BF16 = mybir.dt.bfloat16
AF = mybir.ActivationFunctionType
ALU = mybir.AluOpType

D = 2048
KC = 16
CTX = 256
DFF = 5632
NFB = DFF // 128
GW = 64
CH = 32
EPS = 1e-6
NEG = -30000.0
DEPTH = 2


class Buf:
    __slots__ = ("w", "r")

    def __init__(self):
        self.w = {}
        self.r = {}


class T:
    def __init__(self, h):
        self.h = h
        self.b = Buf()
        self.dsem = None

    def __getitem__(self, k):
        return self.h[k]


class TV(T):
    def __init__(self, base, idx):
        self.h = base.h
        self.idx = idx
        self.b = Buf()
        self.dsem = None

    def __getitem__(self, k):
        if not isinstance(k, tuple):
            k = (k,)
        return self.h[(k[0], self.idx) + tuple(k[1:])]


class KB:
    def __init__(self, NL, debug=False):
        self.NL = NL
        self.NT = CTX + NL
        self.debug = debug
        nc = bass.Bass("TRN2", target_bir_lowering=False)
        self.nc = nc
        self.eng = {"pe": nc.tensor, "act": nc.scalar, "dve": nc.vector, "pool": nc.gpsimd, "sp": nc.sync}
        self.gstack = ExitStack()
        self.sem = {}
        self.cnt = {}
        for e in ("pe", "act", "dve", "pool"):
            self.sem[e] = self.gstack.enter_context(nc.semaphore("s_" + e))
            self.cnt[e] = 0
        self.known = {e: {} for e in self.eng}
        self.ccsem = self.gstack.enter_context(nc.semaphore('s_cc'))
        self.cccount = 0
        self.free_dsems = []
        self.all_dsems = []
        self.semcount = {}
        self.phase_tiles = []
        self.stack = None
        self.uid = 0
        self.dram = {}

    def name(self, n):
        self.uid += 1
        return f"{n}_{self.uid}"

    def sb(self, n, shape, dt):
        t = T(self.stack.enter_context(self.nc.sbuf_tensor(self.name(n), list(shape), dt)))
        self.phase_tiles.append(t)
        return t

    def ps(self, n, shape, dt=F32):
        t = T(self.stack.enter_context(self.nc.psum_tensor(self.name(n), list(shape), dt)))
        self.phase_tiles.append(t)
        return t

    def pool_tiles(self, n, shape, dt, k, psum=False):
        ts = [(self.ps if psum else self.sb)(n, shape, dt) for _ in range(k)]
        return Rot(ts)

    def dr(self, n, shape, dt, kind="Internal"):
        if self.debug and kind == "Internal" and n in self.debug:
            kind = "ExternalOutput"
        t = self.nc.dram_tensor(n, list(shape), dt, kind=kind).ap()
        self.dram[n] = t
        return t

    def get_dsem(self, t):
        if t.dsem is None:
            if self.free_dsems:
                t.dsem = self.free_dsems.pop()
            else:
                s = self.gstack.enter_context(self.nc.semaphore(self.name("d")))
                self.all_dsems.append(s)
                self.semcount[s] = 0
                t.dsem = s
        return t.dsem

    def _deps(self, e, reads, writes):
        need = {}
        for t in reads:
            for s, c in t.b.w.items():
                if need.get(s, 0) < c:
                    need[s] = c
        for t in writes:
            for s, c in t.b.w.items():
                if need.get(s, 0) < c:
                    need[s] = c
            for s, c in t.b.r.items():
                if need.get(s, 0) < c:
                    need[s] = c
        kn = self.known[e]
        for s, c in need.items():
            if kn.get(s, 0) < c:
                self.eng[e].wait_ge(s, c)
                kn[s] = c

    def op(self, e, fn, reads=(), writes=(), acc=False, sig=True):
        self._deps(e, reads, () if acc else writes)
        ins = fn(self.eng[e])
        s = self.sem[e]
        if sig:
            self.cnt[e] += 1
            ins.then_inc(s, 1)
            c = self.cnt[e]
        else:
            c = self.cnt[e] + 1
        for t in reads:
            t.b.r[s] = c
        for t in writes:
            if acc:
                t.b.w[s] = c
            else:
                t.b.w = {s: c}
                t.b.r = {}
        return ins

    def dma(self, q, out, in_, owner, reads=(), writes=(), slow=False):
        s = self.get_dsem(owner)
        self._deps(q, reads, writes)
        ins = self.eng[q].dma_start(out=out, in_=in_, allow_slow_non_contiguous=True) if slow else self.eng[q].dma_start(out=out, in_=in_)
        self.semcount[s] += 16
        ins.then_inc(s, 16)
        c = self.semcount[s]
        for t in reads:
            t.b.r[s] = c
        for t in writes:
            t.b.w = {s: c}
            t.b.r = {}
        return ins

    def barrier(self):
        targets = [(self.sem[e], self.cnt[e]) for e in self.sem if self.cnt[e] > 0]
        targets += [(s, self.semcount[s]) for s in self.all_dsems if self.semcount[s] > 0]
        for e in self.eng:
            kn = self.known[e]
            for s, c in targets:
                if kn.get(s, 0) < c:
                    self.eng[e].wait_ge(s, c)
                    kn[s] = c

    def begin(self):
        self.stack = ExitStack()
        self.phase_tiles = []

    def end(self):
        self.barrier()
        for t in self.phase_tiles:
            if t.dsem is not None:
                self.free_dsems.append(t.dsem)
                t.dsem = None
        self.stack.close()
        self.stack = None


class Rot:
    def __init__(self, ts):
        self.ts = ts
        self.i = 0

    def next(self):
        t = self.ts[self.i % len(self.ts)]
        self.i += 1
        return t


def fm_blocks():
    bl = []
    for h in range(8):
        bl.append(("naq", h))
    for h in range(8):
        bl.append(("nak", h))
    for h in range(4):
        bl += [("hgq", h), ("hgz1", h), ("hgz2", h)]
    for h in range(4):
        bl.append(("hgg", h))
    for h in range(4):
        bl += [("glq", h), ("glqs", h), ("glk", h), ("glks", h)]
    for h in range(4):
        bl.append(("glg", h))
    return bl


def _interleave(bl):
    na = [b for b in bl if b[0] in ("naq", "nak")]
    rest = [b for b in bl if b[0] not in ("naq", "nak")]
    out = []
    i = 0
    groups = []
    k = 0
    while k < len(rest):
        if rest[k][0] == "hgq":
            groups.append(rest[k:k + 3]); k += 3
        elif rest[k][0] == "glq":
            groups.append(rest[k:k + 4]); k += 4
        else:
            groups.append(rest[k:k + 1]); k += 1
    for g_ in groups:
        out += g_
        if len(g_) > 1 and i < len(na):
            out += na[i:i + 2]; i += 2
    out += na[i:]
    return out


FMB_ORDER = _interleave(fm_blocks())
FMB = fm_blocks()
NFMB = len(FMB)


def token_tiles(NL):
    tl = [(0, CTX)]
    for s in range(0, NL, 512):
        tl.append((CTX + s, min(512, NL - s)))
    return tl


def na_geometry(NL, SEQ):
    nqt = NL // 128
    nkt = nqt + 2
    rows = SEQ // GW
    plan = []
    for t in range(nqt):
        lo = min(max(t - 2, 0), nkt - 5)
        plan.append(lo)
    def struct(flip):
        out = np.full((nqt, 5, 128, 128), -1, np.int32)
        base = (rows // 2) * GW if flip else 0
        def glob(loc):
            loc = np.asarray(loc)
            own = loc < NL
            a = (loc - NL) // 128
            w = (loc - NL) % 128
            pl = (nqt - 1 - a) * 128 + w
            if not flip:
                g_own = loc
                g_par = SEQ - 1 - pl
            else:
                g_own = SEQ - 1 - loc
                g_par = pl
            return np.where(own, g_own, g_par)
        for t in range(nqt):
            qg = glob(t * 128 + np.arange(128))
            qr, qc = qg // GW, qg % GW
            rs = np.clip(qr - 4, 0, rows - 8)
            cs_ = np.clip(qc - 8, 0, GW - 16)
            for j in range(5):
                kg = glob((plan[t] + j) * 128 + np.arange(128))
                kr, kc = kg // GW, kg % GW
                ok = ((kr[:, None] >= rs[None, :]) & (kr[:, None] < rs[None, :] + 8) &
                      (kc[:, None] >= cs_[None, :]) & (kc[:, None] < cs_[None, :] + 16))
                dr = kr[:, None] - qr[None, :] + 7
                dc = kc[:, None] - qc[None, :] + 15
                idx = dr * 31 + dc
                out[t, j] = np.where(ok, idx, -1)
        return out
    s0, s1 = struct(False), struct(True)
    classes = []
    cls_of = []
    for t in range(nqt):
        key = (s0[t].tobytes(), s1[t].tobytes())
        if key in classes:
            cls_of.append(classes.index(key))
        else:
            classes.append(key)
            cls_of.append(len(classes) - 1)
    reps = [cls_of.index(c) for c in range(len(classes))]
    return plan, cls_of, reps, s0, s1


def build(NL, SEQ, ncls, plan, cls_of, debug=(), stop_after=None):
    kb = KB(NL, debug=set(debug))
    nc = kb.nc
    NT = kb.NT
    NCH = NT // CH
    NCC = CTX // CH
    TT = token_tiles(NL)
    nqt = NL // 128
    EI = "ExternalInput"
    xT = kb.dr("xT", [D, NT], F32, EI)
    cs_in = kb.dr("cs", [128, KC, 2], F32, EI)
    ada_w = kb.dr("ada_w", [DEPTH, D, 6 * D], F32, EI)
    ada_bt = kb.dr("ada_bt", [DEPTH, 128, 96], F32, EI)
    vecs = kb.dr("vecs", [128, 5, KC], F32, EI)
    hnorm = kb.dr("hnorm", [128, 2, DEPTH], F32, EI)
    lbp = kb.dr("lbp", [128, 2, DEPTH, 4], F32, EI)
    gbias = kb.dr("gbias", [128, DEPTH, 2, 4], F32, EI)
    cw_in = kb.dr("cw", [128, DEPTH, 3, 2 * NFB], F32, EI)
    cb_in = kb.dr("cb", [128, DEPTH, 2 * NFB], F32, EI)
    gup_in = kb.dr("gup", [16, DEPTH, 2, 4, 128], F32, EI)
    w_fm = kb.dr("w_fm", [DEPTH, NFMB, 128, KC, 128], F32, EI)
    w_r = kb.dr("w_r", [DEPTH, 2, 128, KC, 16], F32, EI)
    w_tm = kb.dr("w_tm", [DEPTH, 4, 128, KC, 512], F32, EI)
    w_out = kb.dr("w_out", [DEPTH, KC, 128, KC, 128], F32, EI)
    w_up = kb.dr("w_up", [DEPTH, 2 * NFB, 128, KC, 128], F32, EI)
    w_dn = kb.dr("w_dn", [DEPTH, KC, 128, NFB, 128], F32, EI)
    ropec = kb.dr("ropec", [128, NT], F32, EI)
    ropes = kb.dr("ropes", [128, NT], F32, EI)
    nabias = kb.dr("nabias", [DEPTH, ncls, 128, 8, 5, 128], F32, EI)
    pmask = kb.dr("pmask", [128, 2], F32, EI)
    outT = kb.dr("outT", [D, NL], F32, "ExternalOutput")

    xa = kb.dr("xa", [D, NT], F32)
    xb = kb.dr("xb", [D, NT], F32)
    qT = kb.dr("qT", [1024, NT], BF16)
    kT = kb.dr("kT", [1024, NT + 256], BF16)
    vN = kb.dr("vN", [NT + 256, 1024], BF16)
    sv = {"hg": kb.dr("hgv", [NT, 512], BF16), "gl": kb.dr("glv", [NT, 512], BF16)}
    gs = {"hg": kb.dr("hggs", [512, NT], BF16), "gl": kb.dr("glgs", [512, NT], BF16)}
    qt_d, kt_d, dec_d, o_d = {}, {}, {}, {}
    for kd in ("hg", "gl"):
        for s_ in (1, 2):
            qt_d[kd, s_] = kb.dr(f"qt_{kd}{s_}", [512, NT], BF16)
            kt_d[kd, s_] = kb.dr(f"kt_{kd}{s_}", [512, NT], BF16)
            dec_d[kd, s_] = kb.dr(f"dec_{kd}{s_}", [512, 3, NCH], F32)
            o_d[kd, s_] = kb.dr(f"o_{kd}{s_}", [512, NT], F32)
    mixT = kb.dr("mixT", [D, NT], BF16)
    h2T = kb.dr("h2T", [D, NL + 2], BF16)
    h2c = kb.dr("h2c", [D, CTX + 2], BF16)
    xs_st = kb.dr("xs_st", [128, 2, 512], F32)
    xg_st = kb.dr("xg_st", [256, 2, 512], F32)
    xs_kv = kb.dr("xs_kv", [2048, 256], BF16)
    xg_kv = kb.dr("xg_kv", [4096, 256], BF16)
    xs_h = kb.dr("xs_h", [128, KC], BF16)
    xg_h = kb.dr("xg_h", [256, KC], BF16)
    RG = [[0, 1], [2, 3], [4, 5], [6, 7]]

    def fm(ap, c0=0, n=None):
        return ap.rearrange("(c p) n -> p c n", p=128)

    kb.begin()
    pstack = kb.stack
    ident = kb.sb("ident", [128, 128], BF16)
    ones = kb.sb("ones", [128, 128], BF16)
    identf = kb.sb("identf", [128, 128], F32)
    modv = kb.sb("modv", [128, DEPTH, 96, 2], F32)
    Asc = kb.sb("Asc", [128, DEPTH, 2, KC, 2], F32)
    vec_t = kb.sb("vec_t", [128, 5, KC], F32)
    hn_t = kb.sb("hn_t", [128, 2, DEPTH], F32)
    lb_t = kb.sb("lb_t", [128, 2, 4], F32)
    oml_t = kb.sb("oml_t", [128, 2, 4], F32)
    nml_t = kb.sb("nml_t", [128, 2, 4], F32)
    zero_t = kb.sb("zero_t", [128, 8], F32)
    one_t = kb.sb("one_t", [128, 8], F32)
    gb_t = kb.sb("gb_t", [128, DEPTH, 2, 4], F32)
    cw_t = kb.sb("cw_t", [128, DEPTH, 3, 2 * NFB], F32)
    cb_t = kb.sb("cb_t", [128, DEPTH, 2 * NFB], F32)
    gup_t = kb.sb("gup_t", [16, DEPTH, 2, 4, 128], BF16)
    pm_t = kb.sb("pm_t", [128, 2], F32)
    rmask = kb.sb("rmask", [128, 512], F32)
    tri = {1: kb.sb("tri1", [32, 32], F32), 2: kb.sb("tri2", [32, 32], F32)}
    zbf = kb.sb("zbf", [128, KC], BF16)
    mEL = {0: kb.sb("mE", [128, 512], BF16), 1: kb.sb("mL", [128, 512], BF16)}
    persistent = list(kb.phase_tiles)
    kb.stack = None

    def setup():
        kb.begin()
        lbraw = kb.sb("lbraw", [128, 2, DEPTH, 4], F32)
        cst = kb.sb("cst", [128, KC, 2], F32)
        csb = kb.sb("csb", [128, KC, 2], BF16)
        abt = kb.sb("abt", [128, DEPTH, 96], F32)
        tmp = kb.sb("tmp", [128, 2, 4], F32)
        kb.op("pool", lambda e: e.memset(ones[:], 1.0), writes=[ones])
        kb.op("pool", lambda e: e.memset(identf[:], 0.0), writes=[identf])
        kb.op("pool", lambda e: e.affine_select(out=identf[:], in_=identf[:], pattern=[[-1, 128]],
                                                compare_op=ALU.not_equal, fill=1.0, base=0, channel_multiplier=1),
              reads=[identf], writes=[identf])
        kb.op("dve", lambda e: e.tensor_copy(out=ident[:], in_=identf[:]), reads=[identf], writes=[ident])
        kb.op("pool", lambda e: e.memset(zero_t[:], 0.0), writes=[zero_t])
        kb.op("pool", lambda e: e.memset(one_t[:], 1.0), writes=[one_t])
        kb.op("pool", lambda e: e.memset(zbf[:], 0.0), writes=[zbf])
        for hf in range(2):
            kb.op("pool", lambda e: e.memset(mEL[hf][:], 0.0), writes=[mEL[hf]])
            kb.op("pool", lambda e: e.memset(mEL[hf][:].rearrange("p (c t) -> p c t", t=CH)[:, :, hf * (CH // 2):(hf + 1) * (CH // 2)], 1.0),
                  reads=[mEL[hf]], writes=[mEL[hf]])
        kb.op("pool", lambda e: e.memset(rmask[:], 1.0), writes=[rmask])
        kb.op("pool", lambda e: e.memset(rmask[:].rearrange("p (c t) -> p c t", t=CH)[:, :, 0:1], 0.0),
              reads=[rmask], writes=[rmask])
        for k_, pstep, cmul in ((1, 1, -1), (2, -1, 1)):
            kb.op("pool", lambda e: e.memset(tri[k_][:], 1.0), writes=[tri[k_]])
            kb.op("pool", lambda e: e.affine_select(out=tri[k_][:], in_=tri[k_][:], pattern=[[pstep, 32]],
                                                    compare_op=ALU.is_ge, fill=0.0, base=0, channel_multiplier=cmul),
                  reads=[tri[k_]], writes=[tri[k_]])
        for (dst, src) in ((vec_t, vecs), (hn_t, hnorm), (lbraw, lbp), (gb_t, gbias), (cw_t, cw_in),
                           (cb_t, cb_in), (pm_t, pmask), (cst, cs_in), (abt, ada_bt.rearrange("l p c -> p l c"))):
            kb.dma("sp", dst[:], src, dst, writes=[dst])
        kb.dma("pool", gup_t[:], gup_in, gup_t, writes=[gup_t])
        kb.op("dve", lambda e: e.tensor_tensor(out=tmp[:], in0=lbraw[:, :, 1, :], in1=lbraw[:, :, 0, :], op=ALU.subtract),
              reads=[lbraw], writes=[tmp])
        kb.op("act", lambda e: e.activation(out=lb_t[:], in_=tmp[:], func=AF.Sigmoid), reads=[tmp], writes=[lb_t])
        kb.op("dve", lambda e: e.tensor_scalar(out=oml_t[:], in0=lb_t[:], scalar1=-1.0, scalar2=1.0, op0=ALU.mult, op1=ALU.add),
              reads=[lb_t], writes=[oml_t])
        kb.op("dve", lambda e: e.tensor_scalar(out=nml_t[:], in0=lb_t[:], scalar1=1.0, scalar2=-1.0, op0=ALU.mult, op1=ALU.add),
              reads=[lb_t], writes=[nml_t])
        kb.op("act", lambda e: e.activation(out=csb[:], in_=cst[:], func=AF.Silu), reads=[cst], writes=[csb])
        wpool = kb.pool_tiles("adaw", [128, KC, 512], BF16, 3)
        pp = kb.pool_tiles("adaps", [128, 4, 2], F32, 2, psum=True)
        for l in range(DEPTH):
            for fb in range(24):
                wt = wpool.next()
                kb.dma("pool", wt[:], ada_w[l].rearrange("(kc p) f -> p kc f", p=128)[:, :, fb * 512:(fb + 1) * 512],
                       wt, writes=[wt])
                pt = pp.next()
                for m in range(4):
                    for kc in range(KC):
                        kb.op("pe", lambda e: e.matmul(pt[:, m, :], lhsT=wt[:, kc, m * 128:(m + 1) * 128], rhs=csb[:, kc, :],
                                                       start=(kc == 0), stop=(kc == KC - 1)),
                              reads=[wt, csb], writes=[pt], acc=not (kc == 0 and m == 0), sig=(kc == KC - 1))
                kb.op("dve", lambda e: e.tensor_tensor(out=modv[:, l, fb * 4:(fb + 1) * 4, :], in0=pt[:],
                                                       in1=abt[:, l, fb * 4:(fb + 1) * 4].to_broadcast([128, 4, 2]) if False else
                                                       abt[:, l, fb * 4:(fb + 1) * 4].rearrange("p (c o) -> p c o", o=1).to_broadcast([128, 4, 2]),
                                                       op=ALU.add),
                      reads=[pt, abt], writes=[modv])
        for l in range(DEPTH):
            for wi in range(2):
                gvec = vec_t[:, l + 2 * wi, :]
                scl = modv[:, l, (3 * wi + 1) * KC:(3 * wi + 2) * KC, :]
                kb.op("dve", lambda e: e.tensor_scalar(out=Asc[:, l, wi, :, :], in0=scl, scalar1=1.0, scalar2=float(np.sqrt(D)),
                                                       op0=ALU.add, op1=ALU.mult), reads=[modv], writes=[Asc])
                kb.op("dve", lambda e: e.tensor_tensor(out=Asc[:, l, wi, :, :], in0=Asc[:, l, wi, :, :],
                                                       in1=gvec.rearrange("p (c o) -> p c o", o=1).to_broadcast([128, KC, 2]), op=ALU.mult),
                      reads=[Asc, vec_t], writes=[Asc])
        kb.phase_tiles += []
        kb.end()

    def modsel(l, part, c, j):
        return modv[:, l, part * KC + c, j:j + 1]

    def norm_mod(src, t0, n, l, wi, j, h_out, xpool, sqpool, sspool, rspool):
        xt = xpool.next()
        kb.dma("sp", xt[:, :, 0:n], fm(src)[:, :, t0:t0 + n], xt, writes=[xt])
        sq = sqpool.next()
        kb.op("act", lambda e: e.activation(out=sq[:, :, 0:n], in_=xt[:, :, 0:n], func=AF.Square), reads=[xt], writes=[sq])
        ss = sspool.next()
        for c in range(KC):
            kb.op("pe", lambda e: e.matmul(ss[:, 0:n], lhsT=ones[:], rhs=sq[:, c, 0:n], start=(c == 0), stop=(c == KC - 1)),
                  reads=[ones, sq], writes=[ss], acc=(c > 0), sig=(c == KC - 1))
        rs = rspool.next()
        kb.op("act", lambda e: e.activation(out=rs[:, 0:n], in_=ss[:, 0:n], func=AF.Ln, bias=float(D * EPS)), reads=[ss], writes=[rs])
        kb.op("act", lambda e: e.activation(out=rs[:, 0:n], in_=rs[:, 0:n], func=AF.Exp, scale=-0.5), reads=[rs], writes=[rs])
        xm = xt
        kb.op("dve", lambda e: e.tensor_tensor(out=xm[:, :, 0:n], in0=xt[:, :, 0:n],
                                               in1=rs[:, 0:n].rearrange("p (o n) -> p o n", o=1).to_broadcast([128, KC, n]), op=ALU.mult),
              reads=[rs], writes=[xm])
        for c in range(KC):
            kb.op("act", lambda e: e.activation(out=h_out[:, c, 0:n], in_=xm[:, c, 0:n], func=AF.Identity,
                                                scale=Asc[:, l, wi, c, j:j + 1], bias=modsel(l, 3 * wi, c, j)),
                  reads=[xm, Asc, modv], writes=[h_out])
        return xt

    def phase_A(l, src):
        kb.begin()
        xpool = kb.pool_tiles("xA", [128, KC, 512], F32, 1)
        sqpool = kb.pool_tiles("sqA", [128, KC, 512], BF16, 1)
        sspool = kb.pool_tiles("ssA", [128, 512], F32, 1, psum=True)
        rspool = kb.pool_tiles("rsA", [128, 512], F32, 1)
        hpool = kb.pool_tiles("hA", [128, KC, 512], BF16, 2)
        wfm = kb.pool_tiles("wfm", [128, KC, 128], BF16, 4)
        wtm = kb.pool_tiles("wtm", [128, KC, 512], BF16, 2)
        wr = kb.pool_tiles("wr", [128, KC, 16], BF16, 2)
        pfm = kb.pool_tiles("pfm", [128, 512], F32, 5, psum=True)
        ptm = kb.pool_tiles("ptm", [128, 512], F32, 2, psum=True)
        obf = kb.pool_tiles("obf", [128, 512], BF16, 4)
        of32 = kb.pool_tiles("of32", [128, 512], F32, 10)
        rT = [kb.sb("rT1", [16, 512], BF16), kb.sb("rT2", [16, 512], BF16)]
        keep = {k_: kb.sb("keep_" + k_, [128, 512], F32) for k_ in ("q", "k1", "k2", "lg1", "lg2", "qs")}
        decp = kb.pool_tiles("decp", [128, 3, 16], F32, 3)
        ropc = kb.sb("ropc", [128, 512], F32)
        rops = kb.sb("rops", [128, 512], F32)
        lbz = (l == 0)

        def store(q, dst, t, n, dtile):
            kb.dma(q, dst, t[:, 0:n], t, reads=[t])

        def gate_pipe(kd, h, s_, lg, qsrc, ksrc, t0, n):
            nch = n // CH
            c0 = t0 // CH
            cum = of32.next()
            kb.op("dve", lambda e: e.tensor_tensor_scan(out=cum[:, 0:n], data0=rmask[:, 0:n], data1=lg[:, 0:n], initial=0.0,
                                                        op0=ALU.mult, op1=ALU.add), reads=[rmask, lg], writes=[cum])
            c3 = cum[:, 0:n].rearrange("p (c t) -> p c t", t=CH)
            if s_ == 2:
                cb_ = of32.next()
                kb.op("dve", lambda e: e.tensor_tensor(out=cb_[:, 0:n].rearrange("p (c t) -> p c t", t=CH),
                                                       in0=c3[:, :, CH - 1:CH].to_broadcast([128, nch, CH]), in1=c3, op=ALU.subtract),
                      reads=[cum], writes=[cb_])
                kb.op("dve", lambda e: e.tensor_tensor(out=cb_[:, 0:n], in0=cb_[:, 0:n], in1=lg[:, 0:n], op=ALU.add),
                      reads=[cb_, lg], writes=[cb_])
                cum = cb_
                c3 = cum[:, 0:n].rearrange("p (c t) -> p c t", t=CH)
                mi, li = CH // 2, 0
            else:
                mi, li = CH // 2 - 1, CH - 1
            dc = decp.next()
            kb.op("act", lambda e: e.activation(out=dc[:, 0, 0:nch], in_=c3[:, :, li], func=AF.Exp), reads=[cum], writes=[dc])
            kb.op("act", lambda e: e.activation(out=dc[:, 2, 0:nch], in_=c3[:, :, mi], func=AF.Exp), reads=[cum], writes=[dc])
            kb.op("dve", lambda e: e.tensor_tensor(out=dc[:, 1, 0:nch], in0=c3[:, :, li], in1=c3[:, :, mi], op=ALU.subtract),
                  reads=[cum], writes=[dc])
            kb.op("act", lambda e: e.activation(out=dc[:, 1, 0:nch], in_=dc[:, 1, 0:nch], func=AF.Exp), reads=[dc], writes=[dc])
            kb.dma("act", dec_d[kd, s_][h * 128:(h + 1) * 128, :, c0:c0 + nch], dc[:, :, 0:nch], dc, reads=[dc])
            d1 = of32.next()
            kb.op("dve", lambda e: e.tensor_tensor(out=d1[:, 0:n].rearrange("p (c t) -> p c t", t=CH), in0=c3,
                                                   in1=c3[:, :, mi:mi + 1].to_broadcast([128, nch, CH]), op=ALU.subtract),
                  reads=[cum], writes=[d1])
            e1 = of32.next()
            kb.op("act", lambda e: e.activation(out=e1[:, 0:n], in_=d1[:, 0:n], func=AF.Exp), reads=[d1], writes=[e1])
            kb.op("act", lambda e: e.activation(out=d1[:, 0:n], in_=d1[:, 0:n], func=AF.Exp, scale=-1.0), reads=[d1], writes=[d1])
            qo = obf.next()
            kb.op("dve", lambda e: e.tensor_tensor(out=qo[:, 0:n], in0=qsrc[:, 0:n], in1=e1[:, 0:n], op=ALU.mult),
                  reads=[qsrc, e1], writes=[qo])
            kb.dma("sp", qt_d[kd, s_][h * 128:(h + 1) * 128, t0:t0 + n], qo[:, 0:n], qo, reads=[qo])
            ko = obf.next()
            kb.op("dve", lambda e: e.tensor_tensor(out=ko[:, 0:n], in0=ksrc[:, 0:n], in1=d1[:, 0:n], op=ALU.mult),
                  reads=[ksrc, d1], writes=[ko])
            kb.dma("sp", kt_d[kd, s_][h * 128:(h + 1) * 128, t0:t0 + n], ko[:, 0:n], ko, reads=[ko])

        hts = {}

        def do_norm(ti):
            t0_, n_ = TT[ti]
            hts[ti] = hpool.next()
            norm_mod(src, t0_, n_, l, 0, 1 if t0_ == 0 else 0, hts[ti], xpool, sqpool, sspool, rspool)

        do_norm(0)
        for ti, (t0, n) in enumerate(TT):
            j = 1 if t0 == 0 else 0
            ht = hts.pop(ti)
            if ti + 1 < len(TT):
                do_norm(ti + 1)
            for tb in range(4):
                wt = wtm.next()
                kb.dma("pool", wt[:], w_tm[l, tb], wt, writes=[wt])
                for m in range(n // 128):
                    pt = ptm.next()
                    for kc in range(KC):
                        kb.op("pe", lambda e: e.matmul(pt[:], lhsT=ht[:, kc, m * 128:(m + 1) * 128], rhs=wt[:, kc, :],
                                                       start=(kc == 0), stop=(kc == KC - 1)),
                              reads=[ht, wt], writes=[pt], acc=(kc > 0), sig=(kc == KC - 1))
                    ot = obf.next()
                    kb.op("act", lambda e: e.copy(out=ot[:], in_=pt[:]), reads=[pt], writes=[ot])
                    r0 = t0 + m * 128
                    if tb < 2:
                        dst = vN[r0:r0 + 128, tb * 512:(tb + 1) * 512]
                    else:
                        dst = sv["hg" if tb == 2 else "gl"][r0:r0 + 128, :]
                    kb.dma("act", dst, ot[:], ot, reads=[ot])
            for s_ in range(2):
                wt = wr.next()
                kb.dma("pool", wt[:], w_r[l, s_], wt, writes=[wt])
                pt = pfm.next()
                for kc in range(KC):
                    kb.op("pe", lambda e: e.matmul(pt[0:16, 0:n], lhsT=wt[:, kc, :], rhs=ht[:, kc, 0:n],
                                                   start=(kc == 0), stop=(kc == KC - 1)),
                          reads=[ht, wt], writes=[pt], acc=(kc > 0), sig=(kc == KC - 1))
                kb.op("act", lambda e: e.copy(out=rT[s_][:, 0:n], in_=pt[0:16, 0:n]), reads=[pt], writes=[rT[s_]])
            kb.dma("sp", ropc[:, 0:n], ropec[:, t0:t0 + n], ropc, writes=[ropc])
            kb.dma("sp", rops[:, 0:n], ropes[:, t0:t0 + n], rops, writes=[rops])
            for (kind, h) in FMB_ORDER:
                bi = FMB.index((kind, h))
                wt = wfm.next()
                kb.dma("pool", wt[:], w_fm[l, bi], wt, writes=[wt])
                pt = pfm.next()
                for kc in range(KC):
                    kb.op("pe", lambda e: e.matmul(pt[:, 0:n], lhsT=wt[:, kc, :], rhs=ht[:, kc, 0:n],
                                                   start=(kc == 0), stop=(kc == KC - 1)),
                          reads=[ht, wt], writes=[pt], acc=(kc > 0), sig=(kc == KC - 1))
                if kind == "naq":
                    ot = obf.next()
                    kb.op("act", lambda e: e.activation(out=ot[:, 0:n], in_=pt[:, 0:n], func=AF.Copy, scale=float(128 ** -0.5)),
                          reads=[pt], writes=[ot])
                    kb.dma("act", qT[h * 128:(h + 1) * 128, t0:t0 + n], ot[:, 0:n], ot, reads=[ot])
                elif kind == "nak":
                    ot = obf.next()
                    kb.op("act", lambda e: e.copy(out=ot[:, 0:n], in_=pt[:, 0:n]), reads=[pt], writes=[ot])
                    kb.dma("act", kT[h * 128:(h + 1) * 128, t0:t0 + n], ot[:, 0:n], ot, reads=[ot])
                elif kind in ("hgg", "glg"):
                    ot = obf.next()
                    kb.op("act", lambda e: e.activation(out=ot[:, 0:n], in_=pt[:, 0:n], func=AF.Silu), reads=[pt], writes=[ot])
                    kb.dma("act", gs[kind[:2]][h * 128:(h + 1) * 128, t0:t0 + n], ot[:, 0:n], ot, reads=[ot])
                elif kind == "hgq":
                    kb.op("act", lambda e: e.activation(out=keep["q"][:, 0:n], in_=pt[:, 0:n], func=AF.Silu),
                          reads=[pt], writes=[keep["q"]])
                elif kind in ("hgz1", "hgz2"):
                    s_ = 1 if kind == "hgz1" else 2
                    sg = of32.next()
                    kb.op("act", lambda e: e.activation(out=sg[:, 0:n], in_=pt[:, 0:n], func=AF.Sigmoid), reads=[pt], writes=[sg])
                    kk, lg = keep["k%d" % s_], keep["lg%d" % s_]
                    if lbz:
                        kb.op("dve", lambda e: e.tensor_scalar(out=kk[:, 0:n], in0=sg[:, 0:n], scalar1=-1.0, scalar2=1.0,
                                                               op0=ALU.mult, op1=ALU.add), reads=[sg], writes=[kk])
                        kb.op("act", lambda e: e.activation(out=lg[:, 0:n], in_=sg[:, 0:n], func=AF.Ln), reads=[sg], writes=[lg])
                    else:
                        kb.op("dve", lambda e: e.tensor_scalar(out=kk[:, 0:n], in0=sg[:, 0:n], scalar1=nml_t[:, s_ - 1, h:h + 1],
                                                               scalar2=oml_t[:, s_ - 1, h:h + 1], op0=ALU.mult, op1=ALU.add),
                              reads=[sg, nml_t, oml_t], writes=[kk])
                        kb.op("act", lambda e: e.activation(out=lg[:, 0:n], in_=sg[:, 0:n], func=AF.Ln,
                                                            scale=oml_t[:, s_ - 1, h:h + 1], bias=lb_t[:, s_ - 1, h:h + 1]),
                              reads=[sg, oml_t, lb_t], writes=[lg])
                    if s_ == 2:
                        for s2 in (1, 2):
                            gate_pipe("hg", h, s2, keep["lg%d" % s2], keep["q"], keep["k%d" % s2], t0, n)
                elif kind in ("glq", "glk"):
                    kb.op("act", lambda e: e.activation(out=keep["qs"][:, 0:n], in_=pt[:, 0:n], func=AF.Copy,
                                                        scale=(0.125 if kind == "glq" else 1.0)), reads=[pt], writes=[keep["qs"]])
                elif kind in ("glqs", "glks"):
                    dstk = keep["q"] if kind == "glqs" else keep["k1"]
                    tmp = of32.next()
                    kb.op("dve", lambda e: e.tensor_tensor(out=tmp[:, 0:n], in0=pt[:, 0:n], in1=rops[:, 0:n], op=ALU.mult),
                          reads=[pt, rops], writes=[tmp])
                    kb.op("dve", lambda e: e.tensor_tensor(out=dstk[:, 0:n], in0=keep["qs"][:, 0:n], in1=ropc[:, 0:n], op=ALU.mult),
                          reads=[keep["qs"], ropc], writes=[dstk])
                    kb.op("dve", lambda e: e.scalar_tensor_tensor(out=dstk[:, 0:n], in0=tmp[:, 0:n],
                                                                  scalar=(0.125 if kind == "glqs" else 1.0), in1=dstk[:, 0:n],
                                                                  op0=ALU.mult, op1=ALU.add), reads=[tmp, dstk], writes=[dstk])
                    if kind == "glks":
                        for s_ in (1, 2):
                            pg = pfm.next()
                            kb.op("pe", lambda e: e.matmul(pg[:, 0:n], lhsT=gup_t[:, l, s_ - 1, h, :], rhs=rT[s_ - 1][:, 0:n],
                                                           start=True, stop=True), reads=[gup_t, rT[s_ - 1]], writes=[pg])
                            sg = of32.next()
                            kb.op("act", lambda e: e.activation(out=sg[:, 0:n], in_=pg[:, 0:n], func=AF.Sigmoid,
                                                                bias=gb_t[:, l, s_ - 1, h:h + 1]), reads=[pg, gb_t], writes=[sg])
                            lg = keep["lg%d" % s_]
                            kb.op("act", lambda e: e.activation(out=lg[:, 0:n], in_=sg[:, 0:n], func=AF.Ln), reads=[sg], writes=[lg])
                            kb.op("dve", lambda e: e.tensor_scalar(out=lg[:, 0:n], in0=lg[:, 0:n], scalar1=1.0 / 16.0, scalar2=None,
                                                                   op0=ALU.mult), reads=[lg], writes=[lg])
                            gate_pipe("gl", h, s_, lg, keep["q"], keep["k1"], t0, n)
        kb.end()

    def scan_pass(l, s_, do_ctx_out):
        kb.begin()
        S = {kd: kb.sb("S" + kd, [128, 4, 128], F32) for kd in ("hg", "gl")}
        Sb = {kd: kb.sb("Sb" + kd, [128, 4, 128], BF16) for kd in ("hg", "gl")}
        dec = {kd: kb.sb("dec" + kd, [128, 4, 3, NCH], F32) for kd in ("hg", "gl")}
        qpool = kb.pool_tiles("sq", [128, 4, 256], BF16, 4)
        kpool = kb.pool_tiles("sk", [128, 4, 256], BF16, 4)
        kSp = kb.pool_tiles("skS", [128, 4, 256], BF16, 4)
        kRp = kb.pool_tiles("skR", [128, 4, 256], BF16, 4)
        qRp = kb.pool_tiles("sqR", [128, 4, 256], BF16, 4)
        vpool = kb.pool_tiles("svv", [32, 8, 512], BF16, 4)
        opool = kb.pool_tiles("so", [128, 4, 256], F32, 4)
        ktm = kb.pool_tiles("ktm", [32, 512], BF16, 4)
        attm = kb.pool_tiles("attm", [32, 4, 32], BF16, 4)
        tmpS = kb.pool_tiles("tmpS", [128, 4, 128], F32, 4)
        p_tr = kb.pool_tiles("p_tr", [32, 512], BF16, 2, psum=True)
        p_at = kb.pool_tiles("p_at", [32, 4, 32], F32, 2, psum=True)
        p_o = kb.pool_tiles("p_o", [128, 4, 32], F32, 2, psum=True)
        pend = {}
        p_u = kb.pool_tiles("p_u", [128, 4, 128], F32, 2, psum=True)
        for kd in ("hg", "gl"):
            kb.dma("sp", dec[kd][:], dec_d[kd, s_].rearrange("(h p) a c -> p h a c", p=128), dec[kd], writes=[dec[kd]])

        def init_state(zero):
            for ki, kd in enumerate(("hg", "gl")):
                if zero:
                    kb.op("pool", lambda e: e.memset(S[kd][:], 0.0), writes=[S[kd]])
                else:
                    g0 = tmpS.next()
                    g1 = tmpS.next()
                    kb.dma("sp", g0[:].rearrange("p h d -> p (h d)"), xg_st[0:128, ki, :], g0, writes=[g0])
                    kb.dma("sp", g1[:].rearrange("p h d -> p (h d)"), xg_st[128:256, ki, :], g1, writes=[g1])
                    kb.op("dve", lambda e: e.tensor_scalar(out=S[kd][:], in0=g0[:], scalar1=pm_t[:, 0:1], scalar2=None, op0=ALU.mult),
                          reads=[g0, pm_t], writes=[S[kd]])
                    kb.op("dve", lambda e: e.scalar_tensor_tensor(out=S[kd][:], in0=g1[:], scalar=pm_t[:, 1:2], in1=S[kd][:],
                                                                  op0=ALU.mult, op1=ALU.add), reads=[g1, pm_t, S[kd]], writes=[S[kd]])

        def run_blocks(blocks, rev):
            steps = []
            for bi, (b0, bn) in enumerate(blocks):
                cl = list(range(bn // CH))
                if rev:
                    cl = cl[::-1]
                for k_, ci in enumerate(cl):
                    steps.append((bi, ci, b0 // CH + ci, k_ == len(cl) - 1))
            loaded = {}

            def load_block(bi):
                if bi >= len(blocks) or bi in loaded:
                    return
                b0, bn = blocks[bi]
                tiles = {}
                for kd in ("hg", "gl"):
                    qt_ = qpool.next(); kt_ = kpool.next(); vt_ = vpool.next(); ot_ = opool.next()
                    kb.dma("sp", qt_[:, :, 0:bn], qt_d[kd, s_].rearrange("(h p) n -> p h n", p=128)[:, :, b0:b0 + bn], qt_, writes=[qt_])
                    kb.dma("sp", kt_[:, :, 0:bn], kt_d[kd, s_].rearrange("(h p) n -> p h n", p=128)[:, :, b0:b0 + bn], kt_, writes=[kt_])
                    kb.dma("sp", vt_[:, 0:bn // CH, :], sv[kd][b0:b0 + bn, :].rearrange("(c p) f -> p c f", p=CH), vt_, writes=[vt_])
                    safe, risky = (0, 1) if s_ == 1 else (1, 0)
                    kS = kSp.next(); kR = kRp.next(); qR = qRp.next()
                    for (dst_, src_, mk) in ((kS, kt_, safe), (kR, kt_, risky), (qR, qt_, risky)):
                        kb.op("dve", lambda e: e.tensor_tensor(out=dst_[:, :, 0:bn], in0=src_[:, :, 0:bn],
                                                                in1=mEL[mk][:, 0:bn].rearrange("p (o n) -> p o n", o=1).to_broadcast([128, 4, bn]),
                                                                op=ALU.mult), reads=[src_, mEL[mk]], writes=[dst_])
                    tiles[kd] = (qt_, kt_, vt_, ot_, kS, kR, qR)
                loaded[bi] = tiles

            def prep(st, kd):
                bi, ci, gc, _ = st
                qt_, kt_, vt_, ot_, kS, kR, qR = loaded[bi][kd]
                cs = slice(ci * CH, (ci + 1) * CH)
                ptr = p_tr.next()
                for h in range(4):
                    kb.op("pe", lambda e: e.transpose(ptr[:, h * 128:(h + 1) * 128], kt_[:, h, cs], ident[:]),
                          reads=[kt_, ident], writes=[ptr], acc=(h > 0), sig=(h == 3))
                km = ktm.next()
                kb.op("act", lambda e: e.copy(out=km[:], in_=ptr[:]), reads=[ptr], writes=[km])
                pa = p_at.next()
                for h in range(4):
                    kb.op("pe", lambda e: e.matmul(pa[:, h, :], lhsT=kS[:, h, cs], rhs=qt_[:, h, cs], start=True, stop=False),
                          reads=[kS, qt_], writes=[pa], acc=(h > 0), sig=False)
                    kb.op("pe", lambda e: e.matmul(pa[:, h, :], lhsT=kR[:, h, cs], rhs=qR[:, h, cs], start=False, stop=True),
                          reads=[kR, qR], writes=[pa], acc=True, sig=(h == 3))
                am = attm.next()
                kb.op("dve", lambda e: e.tensor_tensor(out=am[:], in0=pa[:],
                                                       in1=tri[s_][:].rearrange("s (o t) -> s o t", o=1).to_broadcast([32, 4, 32]),
                                                       op=ALU.mult), reads=[pa, tri[s_]], writes=[am])
                pu = p_u.next()
                for h in range(4):
                    kb.op("pe", lambda e: e.matmul(pu[:, h, :], lhsT=km[:, h * 128:(h + 1) * 128], rhs=vt_[:, ci, h * 128:(h + 1) * 128],
                                                   start=True, stop=True), reads=[km, vt_], writes=[pu], acc=(h > 0), sig=(h == 3))
                tu = tmpS.next()
                for h in range(4):
                    kb.op("act", lambda e: e.activation(out=tu[:, h, :], in_=pu[:, h, :], func=AF.Copy, scale=dec[kd][:, h, 1, gc:gc + 1]),
                          reads=[pu, dec[kd]], writes=[tu])
                return am, tu

            def state(st, kd, am, tu):
                bi, ci, gc, _ = st
                qt_, kt_, vt_, ot_, kS, kR, qR = loaded[bi][kd]
                cs = slice(ci * CH, (ci + 1) * CH)
                kb.op("pool", lambda e: e.tensor_tensor(out=Sb[kd][:], in0=S[kd][:],
                                                        in1=dec[kd][:, :, 2, gc:gc + 1].to_broadcast([128, 4, 128]), op=ALU.mult),
                      reads=[S[kd], dec[kd]], writes=[Sb[kd]])
                flush(kd)
                po = p_o.next()
                for h in range(4):
                    kb.op("pe", lambda e: e.matmul(po[:, h, :], lhsT=vt_[:, ci, h * 128:(h + 1) * 128], rhs=am[:, h, :],
                                                   start=True, stop=False), reads=[vt_, am], writes=[po], acc=(h > 0), sig=False)
                    kb.op("pe", lambda e: e.matmul(po[:, h, :], lhsT=Sb[kd][:, h, :], rhs=qt_[:, h, cs],
                                                   start=False, stop=True), reads=[Sb[kd], qt_], writes=[po], acc=True, sig=(h == 3))
                pend[kd] = (po, ot_, cs)
                for h in range(4):
                    kb.op("dve", lambda e: e.scalar_tensor_tensor(out=S[kd][:, h, :], in0=S[kd][:, h, :], scalar=dec[kd][:, h, 0, gc:gc + 1],
                                                                  in1=tu[:, h, :], op0=ALU.mult, op1=ALU.add),
                          reads=[S[kd], dec[kd], tu], writes=[S[kd]])

            def flush(kd):
                if kd in pend:
                    po_, ot2, cs2 = pend.pop(kd)
                    kb.op("act", lambda e: e.copy(out=ot2[:, :, cs2], in_=po_[:]), reads=[po_], writes=[ot2])

            load_block(0)
            load_block(1)
            pre = {kd: prep(steps[0], kd) for kd in ("hg", "gl")}
            for i, st in enumerate(steps):
                nxt = None
                if i + 1 < len(steps):
                    nxt = {kd: prep(steps[i + 1], kd) for kd in ("hg", "gl")}
                for kd in ("hg", "gl"):
                    state(st, kd, *pre[kd])
                if st[3]:
                    b0, bn = blocks[st[0]]
                    for kd in ("hg", "gl"):
                        flush(kd)
                        ot_ = loaded[st[0]][kd][3]
                        kb.dma("act", o_d[kd, s_].rearrange("(h p) n -> p h n", p=128)[:, :, b0:b0 + bn], ot_[:, :, 0:bn], ot_, reads=[ot_])
                    load_block(st[0] + 2)
                pre = nxt

        SB = [(b0, 256) for b0 in range(0, NT, 256)]
        if s_ == 1:
            init_state(True)
            run_blocks(SB, False)
            for ki, kd in enumerate(("hg", "gl")):
                kb.dma("pool", xs_st[:, ki, :], S[kd][:].rearrange("p h d -> p (h d)"), S[kd], reads=[S[kd]])
        else:
            init_state(False)
            run_blocks(SB[1:][::-1], True)
            init_state(True)
            run_blocks(SB[:1], True)
        kb.end()

    def exchange1():
        kb.begin()
        a = kb.sb("xa_k", [128, 8, 256], BF16)
        b_ = kb.sb("xa_v", [128, 2, 1024], BF16)
        kb.dma("pool", a[:], kT.rearrange("(h p) n -> p h n", p=128)[:, :, NT - 256:NT], a, writes=[a])
        kb.dma("pool", xs_kv[0:1024, :].rearrange("(h p) n -> p h n", p=128), a[:], a, reads=[a])
        kb.dma("pool", b_[:], vN[NT - 256:NT, :].rearrange("(c p) f -> p c f", p=128), b_, writes=[b_])
        kb.dma("pool", xs_kv[1024:2048, :].rearrange("(c p x) n -> p c (x n)", p=128, x=4), b_[:], b_, reads=[b_])
        kb.barrier()
        g = nc.gpsimd
        cc = kb.ccsem
        for (src, dst) in ((xs_kv, xg_kv), (xs_st.rearrange("p k f -> p (k f)"), xg_st.rearrange("p k f -> p (k f)"))):
            g.collective_compute("AllGather", ALU.bypass, replica_groups=RG, ins=[src], outs=[dst]).then_inc(cc)
            kb.cccount += 1
        g.wait_ge(cc, kb.cccount)
        for r in range(2):
            ka = kb.sb("ka%d" % r, [128, 8, 256], BF16)
            va = kb.sb("va%d" % r, [128, 2, 1024], BF16)
            kb.dma("pool", ka[:], xg_kv[r * 2048:r * 2048 + 1024, :].rearrange("(h p) n -> p h n", p=128), ka, writes=[ka])
            kb.dma("pool", va[:], xg_kv[r * 2048 + 1024:r * 2048 + 2048, :].rearrange("(c p x) n -> p c (x n)", p=128, x=4), va, writes=[va])
            if r == 0:
                ksel = kb.sb("ksel", [128, 8, 256], BF16)
                vsel = kb.sb("vsel", [128, 2, 1024], BF16)
                kb.op("dve", lambda e: e.tensor_scalar(out=ksel[:], in0=ka[:], scalar1=pm_t[:, 0:1], scalar2=None, op0=ALU.mult),
                      reads=[ka, pm_t], writes=[ksel])
                kb.op("dve", lambda e: e.tensor_scalar(out=vsel[:], in0=va[:], scalar1=pm_t[:, 0:1], scalar2=None, op0=ALU.mult),
                      reads=[va, pm_t], writes=[vsel])
            else:
                kb.op("dve", lambda e: e.scalar_tensor_tensor(out=ksel[:], in0=ka[:], scalar=pm_t[:, 1:2], in1=ksel[:],
                                                              op0=ALU.mult, op1=ALU.add), reads=[ka, pm_t, ksel], writes=[ksel])
                kb.op("dve", lambda e: e.scalar_tensor_tensor(out=vsel[:], in0=va[:], scalar=pm_t[:, 1:2], in1=vsel[:],
                                                              op0=ALU.mult, op1=ALU.add), reads=[va, pm_t, vsel], writes=[vsel])
        for a_ in range(2):
            kb.dma("sp", kT.rearrange("(h p) n -> p h n", p=128)[:, :, NT + a_ * 128:NT + (a_ + 1) * 128],
                   ksel[:, :, (1 - a_) * 128:(2 - a_) * 128], ksel, reads=[ksel])
            kb.dma("sp", vN[NT + a_ * 128:NT + (a_ + 1) * 128, :], vsel[:, 1 - a_, :], vsel, reads=[vsel])
        kb.end()

    def attention(l, with_ctx_self):
        kb.begin()
        kc_t = kb.sb("kc_t", [128, 8, CTX], BF16)
        vc_t = kb.sb("vc_t", [128, 2, 1024], BF16)
        kb.dma("sp", kc_t[:], kT.rearrange("(h p) n -> p h n", p=128)[:, :, 0:CTX], kc_t, writes=[kc_t])
        kb.dma("sp", vc_t[:], vN[0:CTX, :].rearrange("(c p) f -> p c f", p=128), vc_t, writes=[vc_t])
        bias_t = [kb.sb("bias%d" % c, [128, 8, 5, 128], BF16) for c in range(ncls)]
        for c in range(ncls):
            kb.dma("pool", bias_t[c][:], nabias[l, c], bias_t[c], writes=[bias_t[c]])
        qpool = kb.pool_tiles("aq", [128, 8, 128], BF16, 2)
        kpool = kb.pool_tiles("ak", [128, 8, 640], BF16, 2)
        vpool = kb.pool_tiles("av", [128, 5, 1024], BF16, 2)
        ppool = kb.pool_tiles("ap", [128, 7, 128], BF16, 3)
        opool = kb.pool_tiles("ao", [128, 8, 128], BF16, 2)
        rpool = kb.pool_tiles("ar", [128, 4, 128], F32, 2)
        p_s = kb.pool_tiles("p_s", [128, 8, 128], F32, 2, psum=True)
        p_n = kb.pool_tiles("p_n", [128, 4, 128], F32, 1, psum=True)
        p_d = kb.pool_tiles("p_d", [128, 4, 128], F32, 1, psum=True)
        qtiles = [(CTX + t * 128, plan[t], cls_of[t]) for t in range(nqt)]
        if with_ctx_self:
            qtiles = [(0, None, None), (128, None, None)] + qtiles
        for (q0, lo, cls) in qtiles:
            qt_ = qpool.next()
            kb.dma("sp", qt_[:], qT.rearrange("(h p) n -> p h n", p=128)[:, :, q0:q0 + 128], qt_, writes=[qt_])
            nloc = 0
            if lo is not None:
                nloc = 5
                kt_ = kpool.next(); vt_ = vpool.next()
                k0 = CTX + lo * 128
                kb.dma("sp", kt_[:], kT.rearrange("(h p) n -> p h n", p=128)[:, :, k0:k0 + 640], kt_, writes=[kt_])
                kb.dma("sp", vt_[:], vN[k0:k0 + 640, :].rearrange("(c p) f -> p c f", p=128), vt_, writes=[vt_])
            nk = nloc + 2
            ot_ = opool.next()
            for hg_ in range(2):
                pn = p_n.next(); pd = p_d.next()
                for hh in range(4):
                    h = hg_ * 4 + hh
                    ps_ = p_s.next()
                    for j in range(nk):
                        if j < nloc:
                            kap = kt_[:, h, j * 128:(j + 1) * 128]
                        else:
                            kap = kc_t[:, h, (j - nloc) * 128:(j - nloc + 1) * 128]
                        hasb = j < nloc
                        kb.op("pe", lambda e: e.matmul(ps_[:, j, :], lhsT=kap, rhs=qt_[:, h, :], start=True, stop=not hasb),
                              reads=[qt_, kc_t] + ([kt_] if nloc else []), writes=[ps_], acc=(j > 0), sig=False)
                        if hasb:
                            kb.op("pe", lambda e: e.matmul(ps_[:, j, :], lhsT=ident[:], rhs=bias_t[cls][:, h, j, :], start=False, stop=True),
                                  reads=[ident, bias_t[cls]], writes=[ps_], acc=True, sig=False)
                    kb.op("pe", lambda e: e.matmul(ps_[:, 7, 0:2], lhsT=ident[:], rhs=ident[:, 0:2], start=True, stop=True),
                          reads=[ident], writes=[ps_], acc=True, sig=True)
                    pt_ = ppool.next()
                    kb.op("act", lambda e: e.activation(out=pt_[:, 0:nk, :], in_=ps_[:, 0:nk, :], func=AF.Exp), reads=[ps_], writes=[pt_])
                    for j in range(nk):
                        if j < nloc:
                            vap = vt_[:, j, h * 128:(h + 1) * 128]
                        else:
                            vap = vc_t[:, j - nloc, h * 128:(h + 1) * 128]
                        kb.op("pe", lambda e: e.matmul(pn[:, hh, :], lhsT=vap, rhs=pt_[:, j, :], start=(j == 0), stop=(j == nk - 1)),
                              reads=[pt_, vc_t] + ([vt_] if nloc else []), writes=[pn], acc=not (j == 0 and hh == 0), sig=False)
                    for j in range(nk):
                        kb.op("pe", lambda e: e.matmul(pd[:, hh, :], lhsT=ones[:], rhs=pt_[:, j, :], start=(j == 0), stop=(j == nk - 1)),
                              reads=[pt_, ones], writes=[pd], acc=not (j == 0 and hh == 0), sig=(j == nk - 1))
                rt_ = rpool.next()
                kb.op("dve", lambda e: e.reciprocal(out=rt_[:], in_=pd[:]), reads=[pd], writes=[rt_])
                kb.op("dve", lambda e: e.tensor_tensor(out=ot_[:, hg_ * 4:(hg_ + 1) * 4, :], in0=pn[:], in1=rt_[:], op=ALU.mult),
                      reads=[pn, rt_], writes=[ot_])
            kb.dma("act", mixT[0:1024, :].rearrange("(h p) n -> p h n", p=128)[:, :, q0:q0 + 128], ot_[:], ot_, reads=[ot_])
        kb.end()

    def post_scan(l, t_list):
        kb.begin()
        o1p = kb.pool_tiles("o1p", [128, 512], F32, 4)
        o2p = kb.pool_tiles("o2p", [128, 512], F32, 4)
        gp = kb.pool_tiles("gp", [128, 512], BF16, 4)
        sqp = kb.pool_tiles("sqp", [128, 512], BF16, 3)
        rsp = kb.pool_tiles("rsp", [128, 512], F32, 3)
        mp = kb.pool_tiles("mp", [128, 512], BF16, 4)
        pss = kb.pool_tiles("pss", [128, 512], F32, 4, psum=True)
        for (t0, n) in t_list:
            for ki, kd in enumerate(("hg", "gl")):
                for h in range(4):
                    o1 = o1p.next(); o2 = o2p.next(); g_ = gp.next()
                    rows = slice(h * 128, (h + 1) * 128)
                    kb.dma("sp", o1[:, 0:n], o_d[kd, 1][rows, t0:t0 + n], o1, writes=[o1])
                    kb.dma("sp", o2[:, 0:n], o_d[kd, 2][rows, t0:t0 + n], o2, writes=[o2])
                    kb.dma("sp", g_[:, 0:n], gs[kd][rows, t0:t0 + n], g_, writes=[g_])
                    kb.op("dve", lambda e: e.tensor_tensor(out=o1[:, 0:n], in0=o1[:, 0:n], in1=o2[:, 0:n], op=ALU.add),
                          reads=[o1, o2], writes=[o1])
                    sq = sqp.next()
                    kb.op("act", lambda e: e.activation(out=sq[:, 0:n], in_=o1[:, 0:n], func=AF.Square), reads=[o1], writes=[sq])
                    ps_ = pss.next()
                    kb.op("pe", lambda e: e.matmul(ps_[:, 0:n], lhsT=ones[:], rhs=sq[:, 0:n], start=True, stop=True),
                          reads=[ones, sq], writes=[ps_])
                    rs = rsp.next()
                    kb.op("act", lambda e: e.activation(out=rs[:, 0:n], in_=ps_[:, 0:n], func=AF.Ln, bias=float(128 * EPS)),
                          reads=[ps_], writes=[rs])
                    kb.op("act", lambda e: e.activation(out=rs[:, 0:n], in_=rs[:, 0:n], func=AF.Exp, scale=-0.5), reads=[rs], writes=[rs])
                    kb.op("dve", lambda e: e.scalar_tensor_tensor(out=o1[:, 0:n], in0=o1[:, 0:n], scalar=hn_t[:, ki, l:l + 1], in1=rs[:, 0:n],
                                                                  op0=ALU.mult, op1=ALU.mult), reads=[o1, rs, hn_t], writes=[o1])
                    m_ = mp.next()
                    kb.op("dve", lambda e: e.scalar_tensor_tensor(out=m_[:, 0:n], in0=o1[:, 0:n], scalar=float(np.sqrt(128.0)), in1=g_[:, 0:n],
                                                                  op0=ALU.mult, op1=ALU.mult), reads=[o1, g_], writes=[m_])
                    r0 = 1024 + ki * 512 + h * 128
                    kb.dma("act", mixT[r0:r0 + 128, t0:t0 + n], m_[:, 0:n], m_, reads=[m_])
        kb.end()

    def out_proj(l, src, dst, t_list):
        kb.begin()
        mpool = kb.pool_tiles("mxt", [128, KC, 512], BF16, 2)
        xpool = kb.pool_tiles("xr", [128, 512], F32, 6)
        wpool = kb.pool_tiles("wo", [128, KC, 128], BF16, 5)
        pp = kb.pool_tiles("po", [128, 512], F32, 4, psum=True)
        for (t0, n) in t_list:
            j = 1 if t0 == 0 else 0
            mt = mpool.next()
            kb.dma("sp", mt[:, :, 0:n], fm(mixT)[:, :, t0:t0 + n], mt, writes=[mt])
            for ob in range(KC):
                wt = wpool.next()
                kb.dma("pool", wt[:], w_out[l, ob], wt, writes=[wt])
                xt = xpool.next()
                kb.dma("sp", xt[:, 0:n], src[ob * 128:(ob + 1) * 128, t0:t0 + n], xt, writes=[xt])
                pt = pp.next()
                for kc in range(KC):
                    kb.op("pe", lambda e: e.matmul(pt[:, 0:n], lhsT=wt[:, kc, :], rhs=mt[:, kc, 0:n], start=(kc == 0), stop=(kc == KC - 1)),
                          reads=[wt, mt], writes=[pt], acc=(kc > 0), sig=(kc == KC - 1))
                kb.op("dve", lambda e: e.scalar_tensor_tensor(out=xt[:, 0:n], in0=pt[:, 0:n], scalar=modsel(l, 2, ob, j), in1=xt[:, 0:n],
                                                              op0=ALU.mult, op1=ALU.add), reads=[pt, xt, modv], writes=[xt])
                kb.dma("act", dst[ob * 128:(ob + 1) * 128, t0:t0 + n], xt[:, 0:n], xt, reads=[xt])
        kb.end()

    def norm2_phase(l, src, t_list):
        kb.begin()
        xpool = kb.pool_tiles("xN", [128, KC, 512], F32, 2)
        sqpool = kb.pool_tiles("sqN", [128, KC, 512], BF16, 1)
        sspool = kb.pool_tiles("ssN", [128, 512], F32, 1, psum=True)
        rspool = kb.pool_tiles("rsN", [128, 512], F32, 1)
        hpool = kb.pool_tiles("hN", [128, KC, 512], BF16, 3)
        for (t0, n) in t_list:
            j = 1 if t0 == 0 else 0
            ht = hpool.next()
            norm_mod(src, t0, n, l, 1, j, ht, xpool, sqpool, sspool, rspool)
            if t0 == 0:
                kb.dma("act", fm(h2c)[:, :, 1:1 + n], ht[:, :, 0:n], ht, reads=[ht])
            else:
                c0 = t0 - CTX + 1
                kb.dma("act", fm(h2T)[:, :, c0:c0 + n], ht[:, :, 0:n], ht, reads=[ht])
                if t0 + n == NT:
                    kb.dma("pool", xs_h, ht[:, :, n - 1], ht, reads=[ht], slow=True)
        kb.dma("sp", fm(h2T)[:, :, 0], zbf[:], zbf, reads=[zbf], slow=True)
        kb.dma("sp", fm(h2c)[:, :, 0], zbf[:], zbf, reads=[zbf], slow=True)
        kb.dma("sp", fm(h2c)[:, :, CTX + 1], zbf[:], zbf, reads=[zbf], slow=True)
        kb.barrier()
        g = nc.gpsimd
        g.collective_compute("AllGather", ALU.bypass, replica_groups=RG, ins=[xs_h], outs=[xg_h]).then_inc(kb.ccsem)
        kb.cccount += 1
        g.wait_ge(kb.ccsem, kb.cccount)
        h0 = kb.sb("hx0", [128, KC], BF16)
        h1 = kb.sb("hx1", [128, KC], BF16)
        hs = kb.sb("hxs", [128, KC], BF16)
        kb.dma("pool", h0[:], xg_h[0:128, :], h0, writes=[h0])
        kb.dma("pool", h1[:], xg_h[128:256, :], h1, writes=[h1])
        kb.op("dve", lambda e: e.tensor_scalar(out=hs[:], in0=h0[:], scalar1=pm_t[:, 0:1], scalar2=None, op0=ALU.mult),
              reads=[h0, pm_t], writes=[hs])
        kb.op("dve", lambda e: e.scalar_tensor_tensor(out=hs[:], in0=h1[:], scalar=pm_t[:, 1:2], in1=hs[:], op0=ALU.mult, op1=ALU.add),
              reads=[h1, pm_t, hs], writes=[hs])
        kb.dma("sp", fm(h2T)[:, :, NL + 1], hs[:], hs, reads=[hs], slow=True)
        kb.end()

    def ffn(l, mid, dst, do_ctx):
        kb.begin()
        wins = []
        if do_ctx:
            wins.append((h2c, 0, CTX + 2, 0))
        s = 0
        while s < NL:
            w = min(512, NL + 2 - s)
            wins.append((h2T, s, w, CTX + s))
            s += w - 2
        hpool = kb.pool_tiles("hF", [128, KC, 512], BF16, 2)
        gpool = kb.pool_tiles("gF", [128, NFB, 512], BF16, 1)
        wup = kb.pool_tiles("wup", [128, KC, 128], BF16, 4)
        wdn = kb.pool_tiles("wdn", [128, NFB, 128], BF16, 2)
        pab = kb.pool_tiles("pab", [128, 512], F32, 4, psum=True)
        pdn = kb.pool_tiles("pdn", [128, 512], F32, 2, psum=True)
        cvp = kb.pool_tiles("cvp", [128, 512], F32, 6)
        xpool = kb.pool_tiles("xF", [128, 512], F32, 3)
        for (hsrc, s0, w, tok0) in wins:
            j = 1 if tok0 == 0 else 0
            no = w - 2
            ht = hpool.next()
            kb.dma("sp", ht[:, :, 0:w], fm(hsrc)[:, :, s0:s0 + w], ht, writes=[ht])
            gt = gpool.next()
            for jb in range(NFB):
                cv = []
                for half in range(2):
                    blk = half * NFB + jb
                    wt = wup.next()
                    kb.dma("pool", wt[:], w_up[l, blk], wt, writes=[wt])
                    pt = pab.next()
                    for kc in range(KC):
                        kb.op("pe", lambda e: e.matmul(pt[:, 0:w], lhsT=wt[:, kc, :], rhs=ht[:, kc, 0:w], start=(kc == 0), stop=(kc == KC - 1)),
                              reads=[wt, ht], writes=[pt], acc=(kc > 0), sig=(kc == KC - 1))
                    c_ = cvp.next()
                    kb.op("act", lambda e: e.activation(out=c_[:, 0:no], in_=pt[:, 1:1 + no], func=AF.Identity,
                                                        scale=cw_t[:, l, 1, blk:blk + 1], bias=cb_t[:, l, blk:blk + 1]),
                          reads=[pt, cw_t, cb_t], writes=[c_])
                    kb.op("dve", lambda e: e.scalar_tensor_tensor(out=c_[:, 0:no], in0=pt[:, 0:no], scalar=cw_t[:, l, 0, blk:blk + 1],
                                                                  in1=c_[:, 0:no], op0=ALU.mult, op1=ALU.add), reads=[pt, cw_t, c_], writes=[c_])
                    kb.op("dve", lambda e: e.scalar_tensor_tensor(out=c_[:, 0:no], in0=pt[:, 2:2 + no], scalar=cw_t[:, l, 2, blk:blk + 1],
                                                                  in1=c_[:, 0:no], op0=ALU.mult, op1=ALU.add), reads=[pt, cw_t, c_], writes=[c_])
                    cv.append(c_)
                sa = cvp.next()
                kb.op("act", lambda e: e.activation(out=sa[:, 0:no], in_=cv[0][:, 0:no], func=AF.Silu), reads=[cv[0]], writes=[sa])
                kb.op("dve", lambda e: e.tensor_tensor(out=gt[:, jb, 0:no], in0=sa[:, 0:no], in1=cv[1][:, 0:no], op=ALU.mult),
                      reads=[sa, cv[1]], writes=[gt])
            for ob in range(KC):
                wt = wdn.next()
                kb.dma("pool", wt[:], w_dn[l, ob], wt, writes=[wt])
                xt = xpool.next()
                kb.dma("sp", xt[:, 0:no], mid[ob * 128:(ob + 1) * 128, tok0:tok0 + no], xt, writes=[xt])
                pt = pdn.next()
                for kc in range(NFB):
                    kb.op("pe", lambda e: e.matmul(pt[:, 0:no], lhsT=wt[:, kc, :], rhs=gt[:, kc, 0:no], start=(kc == 0), stop=(kc == NFB - 1)),
                          reads=[wt, gt], writes=[pt], acc=(kc > 0), sig=(kc == NFB - 1))
                kb.op("dve", lambda e: e.scalar_tensor_tensor(out=xt[:, 0:no], in0=pt[:, 0:no], scalar=modsel(l, 5, ob, j), in1=xt[:, 0:no],
                                                              op0=ALU.mult, op1=ALU.add), reads=[pt, xt, modv], writes=[xt])
                kb.dma("act", dst[ob * 128:(ob + 1) * 128, tok0:tok0 + no], xt[:, 0:no], xt, reads=[xt])
        kb.end()

    def final_norm(src):
        kb.begin()
        xpool = kb.pool_tiles("xZ", [128, KC, 512], F32, 2)
        sqpool = kb.pool_tiles("sqZ", [128, KC, 512], BF16, 1)
        sspool = kb.pool_tiles("ssZ", [128, 512], F32, 1, psum=True)
        rspool = kb.pool_tiles("rsZ", [128, 512], F32, 1)
        fa = kb.sb("fa", [128, KC], F32)
        kb.op("dve", lambda e: e.tensor_scalar(out=fa[:], in0=vec_t[:, 4, :], scalar1=float(np.sqrt(D)), scalar2=None, op0=ALU.mult),
              reads=[vec_t], writes=[fa])
        for (t0, n) in TT[1:]:
            xt = xpool.next()
            kb.dma("sp", xt[:, :, 0:n], fm(src)[:, :, t0:t0 + n], xt, writes=[xt])
            sq = sqpool.next()
            kb.op("act", lambda e: e.activation(out=sq[:, :, 0:n], in_=xt[:, :, 0:n], func=AF.Square), reads=[xt], writes=[sq])
            ss = sspool.next()
            for c in range(KC):
                kb.op("pe", lambda e: e.matmul(ss[:, 0:n], lhsT=ones[:], rhs=sq[:, c, 0:n], start=(c == 0), stop=(c == KC - 1)),
                      reads=[ones, sq], writes=[ss], acc=(c > 0), sig=(c == KC - 1))
            rs = rspool.next()
            kb.op("act", lambda e: e.activation(out=rs[:, 0:n], in_=ss[:, 0:n], func=AF.Ln, bias=float(D * EPS)), reads=[ss], writes=[rs])
            kb.op("act", lambda e: e.activation(out=rs[:, 0:n], in_=rs[:, 0:n], func=AF.Exp, scale=-0.5), reads=[rs], writes=[rs])
            kb.op("dve", lambda e: e.tensor_tensor(out=xt[:, :, 0:n], in0=xt[:, :, 0:n],
                                                   in1=rs[:, 0:n].rearrange("p (o n) -> p o n", o=1).to_broadcast([128, KC, n]), op=ALU.mult),
                  reads=[rs], writes=[xt])
            for c in range(KC):
                kb.op("act", lambda e: e.activation(out=xt[:, c, 0:n], in_=xt[:, c, 0:n], func=AF.Copy, scale=fa[:, c:c + 1]),
                      reads=[xt, fa], writes=[xt])
            kb.dma("act", fm(outT)[:, :, t0 - CTX:t0 - CTX + n], xt[:, :, 0:n], xt, reads=[xt])
        kb.end()

    setup()
    bufs = [xT, xa, xb]
    src = xT
    for l in range(DEPTH):
        mid, dst = xa, xb
        phase_A(l, src)
        if stop_after == ("A", l):
            break
        scan_pass(l, 1, l == 0)
        exchange1()
        scan_pass(l, 2, l == 0)
        if stop_after == ("S", l):
            break
        attention(l, l == 0)
        tl = TT if l == 0 else TT[1:]
        post_scan(l, tl)
        if stop_after == ("M", l):
            break
        out_proj(l, src, mid, tl)
        norm2_phase(l, mid, tl)
        ffn(l, mid, dst, l == 0)
        src = dst
        if stop_after == ("L", l):
            break
    if stop_after is None:
        final_norm(xb)
    kb.stack = pstack
    kb.phase_tiles = persistent
    kb.end()
    kb.gstack.close()
    return nc


def _pc(v, nchunk):
    return np.ascontiguousarray(v.reshape(nchunk, 128).T)


def _tile_w(w, cols):
    cols = np.asarray(cols)
    sub = w[:, np.where(cols < 0, 0, cols)]
    if (cols < 0).any():
        sub = sub.copy()
        sub[:, cols < 0] = 0.0
    k = w.shape[0] // 128
    return np.ascontiguousarray(sub.reshape(k, 128, len(cols)).transpose(1, 0, 2))


def prep(inp, NL, SEQ, B):
    f32 = np.float32
    g = {k: np.asarray(v, dtype=f32) for k, v in inp.items()}
    plan, cls_of, reps, s0, s1 = na_geometry(NL, SEQ)
    ncls = len(reps)
    NT = CTX + NL
    ar = np.arange
    swap64 = np.concatenate([ar(16, 32), ar(0, 16), ar(48, 64), ar(32, 48)])
    pad = -np.ones(64, np.int64)
    shared = {}
    for half in range(2):
        zf, zb = (4096, 4608) if half == 0 else (4608, 4096)
        rf, rb = (7168, 7184) if half == 0 else (7184, 7168)
        d1, d2 = (0, 1) if half == 0 else (1, 0)
        w_fm = np.zeros((DEPTH, NFMB, 128, KC, 128), f32)
        w_r = np.zeros((DEPTH, 2, 128, KC, 16), f32)
        w_tm = np.zeros((DEPTH, 4, 128, KC, 512), f32)
        for l in range(DEPTH):
            w = g["w_in"][l]
            for bi, (kind, h) in enumerate(FMB):
                if kind == "naq": cols = h * 128 + ar(128)
                elif kind == "nak": cols = 1024 + h * 128 + ar(128)
                elif kind == "hgq": cols = 3072 + h * 128 + ar(128)
                elif kind == "hgz1": cols = zf + h * 128 + ar(128)
                elif kind == "hgz2": cols = zb + h * 128 + ar(128)
                elif kind == "hgg": cols = 5120 + h * 128 + ar(128)
                elif kind == "glq": cols = np.concatenate([5632 + h * 64 + ar(64), pad])
                elif kind == "glqs": cols = np.concatenate([5632 + h * 64 + swap64, pad])
                elif kind == "glk": cols = np.concatenate([5888 + h * 64 + ar(64), pad])
                elif kind == "glks": cols = np.concatenate([5888 + h * 64 + swap64, pad])
                elif kind == "glg": cols = 6656 + h * 128 + ar(128)
                w_fm[l, bi] = _tile_w(w, cols)
            w_r[l, 0] = _tile_w(w, rf + ar(16))
            w_r[l, 1] = _tile_w(w, rb + ar(16))
            for tb, c0 in enumerate((2048, 2560, 3584, 6144)):
                w_tm[l, tb] = _tile_w(w, c0 + ar(512))
        lbp = np.zeros((128, 2, DEPTH, 4), f32)
        gbias = np.zeros((128, DEPTH, 2, 4), f32)
        gup = np.zeros((16, DEPTH, 2, 4, 128), f32)
        cw = np.zeros((128, DEPTH, 3, 2 * NFB), f32)
        for l in range(DEPTH):
            for si, dd in enumerate((d1, d2)):
                lbp[:, si, l, :] = _pc(g["hg_lower_bounds"][dd, l], 4)
                for h in range(4):
                    gbias[0:64, l, si, h] = g["gla_gate_b"][l, dd, h * 64:(h + 1) * 64]
                    gup[:, l, si, h, 0:64] = g["gla_gate_up"][l, dd][:, h * 64:(h + 1) * 64]
            for tap in range(3):
                cw[:, l, tap, :] = _pc(g["conv_w"][l, (2 - tap) if half else tap], 2 * NFB)
        shared[half] = dict(w_fm=w_fm, w_r=w_r, w_tm=w_tm, lbp=lbp, gbias=gbias, gup=gup, cw=cw,
                            pmask=np.tile(np.array([[0.0, 1.0]] if half == 0 else [[1.0, 0.0]], f32), (128, 1)))
        st = s1 if half else s0
        nb = np.zeros((DEPTH, ncls, 128, 8, 5, 128), f32)
        for l in range(DEPTH):
            tab = g["na_rpb"][l].reshape(8, -1)
            for c, t in enumerate(reps):
                idx = st[t]
                val = np.where(idx[None] >= 0, tab[:, np.maximum(idx, 0)], f32(NEG))
                nb[l, c] = val.transpose(2, 0, 1, 3)
        shared[half]["nabias"] = nb
        cosT = np.ones((128, NT), f32)
        sinT = np.zeros((128, NT), f32)
        i = ar(NL)
        gpos = (SEQ - 1 - i) if half else i
        inv = (10000.0 ** (-ar(0, 32, 2, dtype=f32) / f32(32))).astype(f32)
        for base, pos in ((0, gpos // GW), (32, gpos % GW)):
            ang = pos.astype(f32)[None, :] * inv[:, None]
            cosT[base:base + 16, CTX:] = np.cos(ang); cosT[base + 16:base + 32, CTX:] = np.cos(ang)
            sinT[base:base + 16, CTX:] = -np.sin(ang); sinT[base + 16:base + 32, CTX:] = np.sin(ang)
        shared[half]["ropec"] = cosT
        shared[half]["ropes"] = sinT
    common = dict(
        ada_w=g["ada_w"],
        ada_bt=np.ascontiguousarray(g["ada_b"].reshape(DEPTH, 96, 128).transpose(0, 2, 1)),
        vecs=np.ascontiguousarray(np.stack([_pc(g["norm1_g"][0], KC), _pc(g["norm1_g"][1], KC), _pc(g["norm2_g"][0], KC),
                                            _pc(g["norm2_g"][1], KC), _pc(g["final_g"], KC)], axis=1)),
        hnorm=np.ascontiguousarray(np.stack([g["hg_norm_g"].T, g["gla_norm_g"].T], axis=1)),
        cb=np.ascontiguousarray(np.stack([_pc(g["conv_b"][l], 2 * NFB) for l in range(DEPTH)], axis=1)),
        w_out=np.ascontiguousarray(np.stack([np.stack([_tile_w(g["w_out"][l], ob * 128 + ar(128)) for ob in range(KC)]) for l in range(DEPTH)])),
        w_up=np.ascontiguousarray(np.stack([np.stack([_tile_w(g["w_up"][l], bk * 128 + ar(128)) for bk in range(2 * NFB)]) for l in range(DEPTH)])),
        w_dn=np.ascontiguousarray(np.stack([np.stack([_tile_w(g["w_down"][l], ob * 128 + ar(128)) for ob in range(KC)]) for l in range(DEPTH)])),
    )
    in_maps = []
    for r in range(2 * B):
        b, half = r // 2, r % 2
        xl = g["x"][b, half * NL:(half + 1) * NL]
        cx = g["ctx"][b]
        if half:
            xl, cx = xl[::-1], cx[::-1]
        m = dict(common)
        m.update(shared[half])
        m["xT"] = np.ascontiguousarray(np.concatenate([cx, xl], axis=0).T)
        m["cs"] = np.ascontiguousarray(np.stack([_pc(g["c"][b], KC), _pc(g["c_ctx"], KC)], axis=2))
        in_maps.append(m)
    return in_maps, ncls, plan, cls_of


_CACHE = {}


def run(inp, NL, SEQ, B, debug=(), stop_after=None):
    in_maps, ncls, plan, cls_of = prep(inp, NL, SEQ, B)
    nc = build(NL, SEQ, ncls, plan, cls_of, debug=debug, stop_after=stop_after)
    res = run_bass_kernel_spmd(nc, in_maps, core_ids=list(range(2 * B)))
    return res


def kernel(**inputs):
    B, SEQ, _ = inputs["x"].shape
    NL = SEQ // 2
    res = run(inputs, NL, SEQ, B)
    out = np.zeros((B, SEQ, D), np.float32)
    for r in range(2 * B):
        b, half = r // 2, r % 2
        o = np.asarray(res.results[r]["outT"]).T
        if half:
            out[b, NL:] = o[::-1]
        else:
            out[b, :NL] = o
    return out
```

```python
import numpy as np
from contextlib import ExitStack
import concourse.bass as bass
import concourse.mybir as mybir
from concourse.bass_utils import run_bass_kernel_spmd

F32 = mybir.dt.float32
BF16 = mybir.dt.bfloat16
AF = mybir.ActivationFunctionType
ALU = mybir.AluOpType

D = 2048
KC = 16
CTX = 256
DFF = 5632
NFB = DFF // 128
GW = 64
CH = 32
EPS = 1e-6
NEG = -30000.0
DEPTH = 2


class Buf:
    __slots__ = ("w", "r")

    def __init__(self):
        self.w = {}
        self.r = {}


class T:
    def __init__(self, h):
        self.h = h
        self.b = Buf()
        self.dsem = None

    def __getitem__(self, k):
        return self.h[k]


class TV(T):
    def __init__(self, base, idx):
        self.h = base.h
        self.idx = idx
        self.b = Buf()
        self.dsem = None

    def __getitem__(self, k):
        if not isinstance(k, tuple):
            k = (k,)
        return self.h[(k[0], self.idx) + tuple(k[1:])]


class KB:
    def __init__(self, NL, debug=False):
        self.NL = NL
        self.NT = CTX + NL
        self.debug = debug
        nc = bass.Bass("TRN2", target_bir_lowering=False)
        self.nc = nc
        self.eng = {"pe": nc.tensor, "act": nc.scalar, "dve": nc.vector, "pool": nc.gpsimd, "sp": nc.sync}
        self.gstack = ExitStack()
        self.sem = {}
        self.cnt = {}
        for e in ("pe", "act", "dve", "pool"):
            self.sem[e] = self.gstack.enter_context(nc.semaphore("s_" + e))
            self.cnt[e] = 0
        self.known = {e: {} for e in self.eng}
        self.ccsem = self.gstack.enter_context(nc.semaphore('s_cc'))
        self.cccount = 0
        self.free_dsems = []
        self.all_dsems = []
        self.semcount = {}
        self.phase_tiles = []
        self.stack = None
        self.uid = 0
        self.dram = {}

    def name(self, n):
        self.uid += 1
        return f"{n}_{self.uid}"

    def sb(self, n, shape, dt):
        t = T(self.stack.enter_context(self.nc.sbuf_tensor(self.name(n), list(shape), dt)))
        self.phase_tiles.append(t)
        return t

    def ps(self, n, shape, dt=F32):
        t = T(self.stack.enter_context(self.nc.psum_tensor(self.name(n), list(shape), dt)))
        self.phase_tiles.append(t)
        return t

    def pool_tiles(self, n, shape, dt, k, psum=False):
        ts = [(self.ps if psum else self.sb)(n, shape, dt) for _ in range(k)]
        return Rot(ts)

    def dr(self, n, shape, dt, kind="Internal"):
        if self.debug and kind == "Internal" and n in self.debug:
            kind = "ExternalOutput"
        t = self.nc.dram_tensor(n, list(shape), dt, kind=kind).ap()
        self.dram[n] = t
        return t

    def get_dsem(self, t):
        if t.dsem is None:
            if self.free_dsems:
                t.dsem = self.free_dsems.pop()
            else:
                s = self.gstack.enter_context(self.nc.semaphore(self.name("d")))
                self.all_dsems.append(s)
                self.semcount[s] = 0
                t.dsem = s
        return t.dsem

    def _deps(self, e, reads, writes):
        need = {}
        for t in reads:
            for s, c in t.b.w.items():
                if need.get(s, 0) < c:
                    need[s] = c
        for t in writes:
            for s, c in t.b.w.items():
                if need.get(s, 0) < c:
                    need[s] = c
            for s, c in t.b.r.items():
                if need.get(s, 0) < c:
                    need[s] = c
        kn = self.known[e]
        for s, c in need.items():
            if kn.get(s, 0) < c:
                self.eng[e].wait_ge(s, c)
                kn[s] = c

    def op(self, e, fn, reads=(), writes=(), acc=False, sig=True):
        self._deps(e, reads, () if acc else writes)
        ins = fn(self.eng[e])
        s = self.sem[e]
        if sig:
            self.cnt[e] += 1
            ins.then_inc(s, 1)
            c = self.cnt[e]
        else:
            c = self.cnt[e] + 1
        for t in reads:
            t.b.r[s] = c
        for t in writes:
            if acc:
                t.b.w[s] = c
            else:
                t.b.w = {s: c}
                t.b.r = {}
        return ins

    def dma(self, q, out, in_, owner, reads=(), writes=(), slow=False):
        s = self.get_dsem(owner)
        self._deps(q, reads, writes)
        ins = self.eng[q].dma_start(out=out, in_=in_, allow_slow_non_contiguous=True) if slow else self.eng[q].dma_start(out=out, in_=in_)
        self.semcount[s] += 16
        ins.then_inc(s, 16)
        c = self.semcount[s]
        for t in reads:
            t.b.r[s] = c
        for t in writes:
            t.b.w = {s: c}
            t.b.r = {}
        return ins

    def barrier(self):
        targets = [(self.sem[e], self.cnt[e]) for e in self.sem if self.cnt[e] > 0]
        targets += [(s, self.semcount[s]) for s in self.all_dsems if self.semcount[s] > 0]
        for e in self.eng:
            kn = self.known[e]
            for s, c in targets:
                if kn.get(s, 0) < c:
                    self.eng[e].wait_ge(s, c)
                    kn[s] = c

    def begin(self):
        self.stack = ExitStack()
        self.phase_tiles = []

    def end(self):
        self.barrier()
        for t in self.phase_tiles:
            if t.dsem is not None:
                self.free_dsems.append(t.dsem)
                t.dsem = None
        self.stack.close()
        self.stack = None


class Rot:
    def __init__(self, ts):
        self.ts = ts
        self.i = 0

    def next(self):
        t = self.ts[self.i % len(self.ts)]
        self.i += 1
        return t


def fm_blocks():
    bl = []
    for h in range(8):
        bl.append(("naq", h))
    for h in range(8):
        bl.append(("nak", h))
    for h in range(4):
        bl += [("hgq", h), ("hgz1", h), ("hgz2", h)]
    for h in range(4):
        bl.append(("hgg", h))
    for h in range(4):
        bl += [("glq", h), ("glqs", h), ("glk", h), ("glks", h)]
    for h in range(4):
        bl.append(("glg", h))
    return bl


def _interleave(bl):
    na = [b for b in bl if b[0] in ("naq", "nak")]
    rest = [b for b in bl if b[0] not in ("naq", "nak")]
    out = []
    i = 0
    groups = []
    k = 0
    while k < len(rest):
        if rest[k][0] == "hgq":
            groups.append(rest[k:k + 3]); k += 3
        elif rest[k][0] == "glq":
            groups.append(rest[k:k + 4]); k += 4
        else:
            groups.append(rest[k:k + 1]); k += 1
    for g_ in groups:
        out += g_
        if len(g_) > 1 and i < len(na):
            out += na[i:i + 2]; i += 2
    out += na[i:]
    return out


FMB_ORDER = _interleave(fm_blocks())
FMB = fm_blocks()
NFMB = len(FMB)


def token_tiles(NL):
    tl = [(0, CTX)]
    for s in range(0, NL, 512):
        tl.append((CTX + s, min(512, NL - s)))
    return tl


def na_geometry(NL, SEQ):
    nqt = NL // 128
    nkt = nqt + 2
    rows = SEQ // GW
    plan = []
    for t in range(nqt):
        lo = min(max(t - 2, 0), nkt - 5)
        plan.append(lo)
    def struct(flip):
        out = np.full((nqt, 5, 128, 128), -1, np.int32)
        base = (rows // 2) * GW if flip else 0
        def glob(loc):
            loc = np.asarray(loc)
            own = loc < NL
            a = (loc - NL) // 128
            w = (loc - NL) % 128
            pl = (nqt - 1 - a) * 128 + w
            if not flip:
                g_own = loc
                g_par = SEQ - 1 - pl
            else:
                g_own = SEQ - 1 - loc
                g_par = pl
            return np.where(own, g_own, g_par)
        for t in range(nqt):
            qg = glob(t * 128 + np.arange(128))
            qr, qc = qg // GW, qg % GW
            rs = np.clip(qr - 4, 0, rows - 8)
            cs_ = np.clip(qc - 8, 0, GW - 16)
            for j in range(5):
                kg = glob((plan[t] + j) * 128 + np.arange(128))
                kr, kc = kg // GW, kg % GW
                ok = ((kr[:, None] >= rs[None, :]) & (kr[:, None] < rs[None, :] + 8) &
                      (kc[:, None] >= cs_[None, :]) & (kc[:, None] < cs_[None, :] + 16))
                dr = kr[:, None] - qr[None, :] + 7
                dc = kc[:, None] - qc[None, :] + 15
                idx = dr * 31 + dc
                out[t, j] = np.where(ok, idx, -1)
        return out
    s0, s1 = struct(False), struct(True)
    classes = []
    cls_of = []
    for t in range(nqt):
        key = (s0[t].tobytes(), s1[t].tobytes())
        if key in classes:
            cls_of.append(classes.index(key))
        else:
            classes.append(key)
            cls_of.append(len(classes) - 1)
    reps = [cls_of.index(c) for c in range(len(classes))]
    return plan, cls_of, reps, s0, s1


def build(NL, SEQ, ncls, plan, cls_of, debug=(), stop_after=None):
    kb = KB(NL, debug=set(debug))
    nc = kb.nc
    NT = kb.NT
    NCH = NT // CH
    NCC = CTX // CH
    TT = token_tiles(NL)
    nqt = NL // 128
    EI = "ExternalInput"
    xT = kb.dr("xT", [D, NT], F32, EI)
    cs_in = kb.dr("cs", [128, KC, 2], F32, EI)
    ada_w = kb.dr("ada_w", [DEPTH, D, 6 * D], F32, EI)
    ada_bt = kb.dr("ada_bt", [DEPTH, 128, 96], F32, EI)
    vecs = kb.dr("vecs", [128, 5, KC], F32, EI)
    hnorm = kb.dr("hnorm", [128, 2, DEPTH], F32, EI)
    lbp = kb.dr("lbp", [128, 2, DEPTH, 4], F32, EI)
    gbias = kb.dr("gbias", [128, DEPTH, 2, 4], F32, EI)
    cw_in = kb.dr("cw", [128, DEPTH, 3, 2 * NFB], F32, EI)
    cb_in = kb.dr("cb", [128, DEPTH, 2 * NFB], F32, EI)
    gup_in = kb.dr("gup", [16, DEPTH, 2, 4, 128], F32, EI)
    w_fm = kb.dr("w_fm", [DEPTH, NFMB, 128, KC, 128], F32, EI)
    w_r = kb.dr("w_r", [DEPTH, 2, 128, KC, 16], F32, EI)
    w_tm = kb.dr("w_tm", [DEPTH, 4, 128, KC, 512], F32, EI)
    w_out = kb.dr("w_out", [DEPTH, KC, 128, KC, 128], F32, EI)
    w_up = kb.dr("w_up", [DEPTH, 2 * NFB, 128, KC, 128], F32, EI)
    w_dn = kb.dr("w_dn", [DEPTH, KC, 128, NFB, 128], F32, EI)
    ropec = kb.dr("ropec", [128, NT], F32, EI)
    ropes = kb.dr("ropes", [128, NT], F32, EI)
    nabias = kb.dr("nabias", [DEPTH, ncls, 128, 8, 5, 128], F32, EI)
    pmask = kb.dr("pmask", [128, 2], F32, EI)
    outT = kb.dr("outT", [D, NL], F32, "ExternalOutput")

    xa = kb.dr("xa", [D, NT], F32)
    xb = kb.dr("xb", [D, NT], F32)
    qT = kb.dr("qT", [1024, NT], BF16)
    kT = kb.dr("kT", [1024, NT + 256], BF16)
    vN = kb.dr("vN", [NT + 256, 1024], BF16)
    sv = {"hg": kb.dr("hgv", [NT, 512], BF16), "gl": kb.dr("glv", [NT, 512], BF16)}
    gs = {"hg": kb.dr("hggs", [512, NT], BF16), "gl": kb.dr("glgs", [512, NT], BF16)}
    qt_d, kt_d, dec_d, o_d = {}, {}, {}, {}
    for kd in ("hg", "gl"):
        for s_ in (1, 2):
            qt_d[kd, s_] = kb.dr(f"qt_{kd}{s_}", [512, NT], BF16)
            kt_d[kd, s_] = kb.dr(f"kt_{kd}{s_}", [512, NT], BF16)
            dec_d[kd, s_] = kb.dr(f"dec_{kd}{s_}", [512, 3, NCH], F32)
            o_d[kd, s_] = kb.dr(f"o_{kd}{s_}", [512, NT], F32)
    mixT = kb.dr("mixT", [D, NT], BF16)
    h2T = kb.dr("h2T", [D, NL + 2], BF16)
    h2c = kb.dr("h2c", [D, CTX + 2], BF16)
    xs_st = kb.dr("xs_st", [128, 2, 512], F32)
    xg_st = kb.dr("xg_st", [256, 2, 512], F32)
    xs_kv = kb.dr("xs_kv", [2048, 256], BF16)
    xg_kv = kb.dr("xg_kv", [4096, 256], BF16)
    xs_h = kb.dr("xs_h", [128, KC], BF16)
    xg_h = kb.dr("xg_h", [256, KC], BF16)
    RG = [[0, 1], [2, 3], [4, 5], [6, 7]]

    def fm(ap, c0=0, n=None):
        return ap.rearrange("(c p) n -> p c n", p=128)

    kb.begin()
    pstack = kb.stack
    ident = kb.sb("ident", [128, 128], BF16)
    ones = kb.sb("ones", [128, 128], BF16)
    identf = kb.sb("identf", [128, 128], F32)
    modv = kb.sb("modv", [128, DEPTH, 96, 2], F32)
    Asc = kb.sb("Asc", [128, DEPTH, 2, KC, 2], F32)
    vec_t = kb.sb("vec_t", [128, 5, KC], F32)
    hn_t = kb.sb("hn_t", [128, 2, DEPTH], F32)
    lb_t = kb.sb("lb_t", [128, 2, 4], F32)
    oml_t = kb.sb("oml_t", [128, 2, 4], F32)
    nml_t = kb.sb("nml_t", [128, 2, 4], F32)
    zero_t = kb.sb("zero_t", [128, 8], F32)
    one_t = kb.sb("one_t", [128, 8], F32)
    gb_t = kb.sb("gb_t", [128, DEPTH, 2, 4], F32)
    cw_t = kb.sb("cw_t", [128, DEPTH, 3, 2 * NFB], F32)
    cb_t = kb.sb("cb_t", [128, DEPTH, 2 * NFB], F32)
    gup_t = kb.sb("gup_t", [16, DEPTH, 2, 4, 128], BF16)
    pm_t = kb.sb("pm_t", [128, 2], F32)
    rmask = kb.sb("rmask", [128, 512], F32)
    tri = {1: kb.sb("tri1", [32, 32], F32), 2: kb.sb("tri2", [32, 32], F32)}
    zbf = kb.sb("zbf", [128, KC], BF16)
    mEL = {0: kb.sb("mE", [128, 512], BF16), 1: kb.sb("mL", [128, 512], BF16)}
    persistent = list(kb.phase_tiles)
    kb.stack = None

    def setup():
        kb.begin()
        lbraw = kb.sb("lbraw", [128, 2, DEPTH, 4], F32)
        cst = kb.sb("cst", [128, KC, 2], F32)
        csb = kb.sb("csb", [128, KC, 2], BF16)
        abt = kb.sb("abt", [128, DEPTH, 96], F32)
        tmp = kb.sb("tmp", [128, 2, 4], F32)
        kb.op("pool", lambda e: e.memset(ones[:], 1.0), writes=[ones])
        kb.op("pool", lambda e: e.memset(identf[:], 0.0), writes=[identf])
        kb.op("pool", lambda e: e.affine_select(out=identf[:], in_=identf[:], pattern=[[-1, 128]],
                                                compare_op=ALU.not_equal, fill=1.0, base=0, channel_multiplier=1),
              reads=[identf], writes=[identf])
        kb.op("dve", lambda e: e.tensor_copy(out=ident[:], in_=identf[:]), reads=[identf], writes=[ident])
        kb.op("pool", lambda e: e.memset(zero_t[:], 0.0), writes=[zero_t])
        kb.op("pool", lambda e: e.memset(one_t[:], 1.0), writes=[one_t])
        kb.op("pool", lambda e: e.memset(zbf[:], 0.0), writes=[zbf])
        for hf in range(2):
            kb.op("pool", lambda e: e.memset(mEL[hf][:], 0.0), writes=[mEL[hf]])
            kb.op("pool", lambda e: e.memset(mEL[hf][:].rearrange("p (c t) -> p c t", t=CH)[:, :, hf * (CH // 2):(hf + 1) * (CH // 2)], 1.0),
                  reads=[mEL[hf]], writes=[mEL[hf]])
        kb.op("pool", lambda e: e.memset(rmask[:], 1.0), writes=[rmask])
        kb.op("pool", lambda e: e.memset(rmask[:].rearrange("p (c t) -> p c t", t=CH)[:, :, 0:1], 0.0),
              reads=[rmask], writes=[rmask])
        for k_, pstep, cmul in ((1, 1, -1), (2, -1, 1)):
            kb.op("pool", lambda e: e.memset(tri[k_][:], 1.0), writes=[tri[k_]])
            kb.op("pool", lambda e: e.affine_select(out=tri[k_][:], in_=tri[k_][:], pattern=[[pstep, 32]],
                                                    compare_op=ALU.is_ge, fill=0.0, base=0, channel_multiplier=cmul),
                  reads=[tri[k_]], writes=[tri[k_]])
        for (dst, src) in ((vec_t, vecs), (hn_t, hnorm), (lbraw, lbp), (gb_t, gbias), (cw_t, cw_in),
                           (cb_t, cb_in), (pm_t, pmask), (cst, cs_in), (abt, ada_bt.rearrange("l p c -> p l c"))):
            kb.dma("sp", dst[:], src, dst, writes=[dst])
        kb.dma("pool", gup_t[:], gup_in, gup_t, writes=[gup_t])
        kb.op("dve", lambda e: e.tensor_tensor(out=tmp[:], in0=lbraw[:, :, 1, :], in1=lbraw[:, :, 0, :], op=ALU.subtract),
              reads=[lbraw], writes=[tmp])
        kb.op("act", lambda e: e.activation(out=lb_t[:], in_=tmp[:], func=AF.Sigmoid), reads=[tmp], writes=[lb_t])
        kb.op("dve", lambda e: e.tensor_scalar(out=oml_t[:], in0=lb_t[:], scalar1=-1.0, scalar2=1.0, op0=ALU.mult, op1=ALU.add),
              reads=[lb_t], writes=[oml_t])
        kb.op("dve", lambda e: e.tensor_scalar(out=nml_t[:], in0=lb_t[:], scalar1=1.0, scalar2=-1.0, op0=ALU.mult, op1=ALU.add),
              reads=[lb_t], writes=[nml_t])
        kb.op("act", lambda e: e.activation(out=csb[:], in_=cst[:], func=AF.Silu), reads=[cst], writes=[csb])
        wpool = kb.pool_tiles("adaw", [128, KC, 512], BF16, 3)
        pp = kb.pool_tiles("adaps", [128, 4, 2], F32, 2, psum=True)
        for l in range(DEPTH):
            for fb in range(24):
                wt = wpool.next()
                kb.dma("pool", wt[:], ada_w[l].rearrange("(kc p) f -> p kc f", p=128)[:, :, fb * 512:(fb + 1) * 512],
                       wt, writes=[wt])
                pt = pp.next()
                for m in range(4):
                    for kc in range(KC):
                        kb.op("pe", lambda e: e.matmul(pt[:, m, :], lhsT=wt[:, kc, m * 128:(m + 1) * 128], rhs=csb[:, kc, :],
                                                       start=(kc == 0), stop=(kc == KC - 1)),
                              reads=[wt, csb], writes=[pt], acc=not (kc == 0 and m == 0), sig=(kc == KC - 1))
                kb.op("dve", lambda e: e.tensor_tensor(out=modv[:, l, fb * 4:(fb + 1) * 4, :], in0=pt[:],
                                                       in1=abt[:, l, fb * 4:(fb + 1) * 4].to_broadcast([128, 4, 2]) if False else
                                                       abt[:, l, fb * 4:(fb + 1) * 4].rearrange("p (c o) -> p c o", o=1).to_broadcast([128, 4, 2]),
                                                       op=ALU.add),
                      reads=[pt, abt], writes=[modv])
        for l in range(DEPTH):
            for wi in range(2):
                gvec = vec_t[:, l + 2 * wi, :]
                scl = modv[:, l, (3 * wi + 1) * KC:(3 * wi + 2) * KC, :]
                kb.op("dve", lambda e: e.tensor_scalar(out=Asc[:, l, wi, :, :], in0=scl, scalar1=1.0, scalar2=float(np.sqrt(D)),
                                                       op0=ALU.add, op1=ALU.mult), reads=[modv], writes=[Asc])
                kb.op("dve", lambda e: e.tensor_tensor(out=Asc[:, l, wi, :, :], in0=Asc[:, l, wi, :, :],
                                                       in1=gvec.rearrange("p (c o) -> p c o", o=1).to_broadcast([128, KC, 2]), op=ALU.mult),
                      reads=[Asc, vec_t], writes=[Asc])
        kb.phase_tiles += []
        kb.end()

    def modsel(l, part, c, j):
        return modv[:, l, part * KC + c, j:j + 1]

    def norm_mod(src, t0, n, l, wi, j, h_out, xpool, sqpool, sspool, rspool):
        xt = xpool.next()
        kb.dma("sp", xt[:, :, 0:n], fm(src)[:, :, t0:t0 + n], xt, writes=[xt])
        sq = sqpool.next()
        kb.op("act", lambda e: e.activation(out=sq[:, :, 0:n], in_=xt[:, :, 0:n], func=AF.Square), reads=[xt], writes=[sq])
        ss = sspool.next()
        for c in range(KC):
            kb.op("pe", lambda e: e.matmul(ss[:, 0:n], lhsT=ones[:], rhs=sq[:, c, 0:n], start=(c == 0), stop=(c == KC - 1)),
                  reads=[ones, sq], writes=[ss], acc=(c > 0), sig=(c == KC - 1))
        rs = rspool.next()
        kb.op("act", lambda e: e.activation(out=rs[:, 0:n], in_=ss[:, 0:n], func=AF.Ln, bias=float(D * EPS)), reads=[ss], writes=[rs])
        kb.op("act", lambda e: e.activation(out=rs[:, 0:n], in_=rs[:, 0:n], func=AF.Exp, scale=-0.5), reads=[rs], writes=[rs])
        xm = xt
        kb.op("dve", lambda e: e.tensor_tensor(out=xm[:, :, 0:n], in0=xt[:, :, 0:n],
                                               in1=rs[:, 0:n].rearrange("p (o n) -> p o n", o=1).to_broadcast([128, KC, n]), op=ALU.mult),
              reads=[rs], writes=[xm])
        for c in range(KC):
            kb.op("act", lambda e: e.activation(out=h_out[:, c, 0:n], in_=xm[:, c, 0:n], func=AF.Identity,
                                                scale=Asc[:, l, wi, c, j:j + 1], bias=modsel(l, 3 * wi, c, j)),
                  reads=[xm, Asc, modv], writes=[h_out])
        return xt

    def phase_A(l, src):
        kb.begin()
        xpool = kb.pool_tiles("xA", [128, KC, 512], F32, 1)
        sqpool = kb.pool_tiles("sqA", [128, KC, 512], BF16, 1)
        sspool = kb.pool_tiles("ssA", [128, 512], F32, 1, psum=True)
        rspool = kb.pool_tiles("rsA", [128, 512], F32, 1)
        hpool = kb.pool_tiles("hA", [128, KC, 512], BF16, 2)
        wfm = kb.pool_tiles("wfm", [128, KC, 128], BF16, 4)
        wtm = kb.pool_tiles("wtm", [128, KC, 512], BF16, 2)
        wr = kb.pool_tiles("wr", [128, KC, 16], BF16, 2)
        pfm = kb.pool_tiles("pfm", [128, 512], F32, 5, psum=True)
        ptm = kb.pool_tiles("ptm", [128, 512], F32, 2, psum=True)
        obf = kb.pool_tiles("obf", [128, 512], BF16, 4)
        of32 = kb.pool_tiles("of32", [128, 512], F32, 10)
        rT = [kb.sb("rT1", [16, 512], BF16), kb.sb("rT2", [16, 512], BF16)]
        keep = {k_: kb.sb("keep_" + k_, [128, 512], F32) for k_ in ("q", "k1", "k2", "lg1", "lg2", "qs")}
        decp = kb.pool_tiles("decp", [128, 3, 16], F32, 3)
        ropc = kb.sb("ropc", [128, 512], F32)
        rops = kb.sb("rops", [128, 512], F32)
        lbz = (l == 0)

        def store(q, dst, t, n, dtile):
            kb.dma(q, dst, t[:, 0:n], t, reads=[t])

        def gate_pipe(kd, h, s_, lg, qsrc, ksrc, t0, n):
            nch = n // CH
            c0 = t0 // CH
            cum = of32.next()
            kb.op("dve", lambda e: e.tensor_tensor_scan(out=cum[:, 0:n], data0=rmask[:, 0:n], data1=lg[:, 0:n], initial=0.0,
                                                        op0=ALU.mult, op1=ALU.add), reads=[rmask, lg], writes=[cum])
            c3 = cum[:, 0:n].rearrange("p (c t) -> p c t", t=CH)
            if s_ == 2:
                cb_ = of32.next()
                kb.op("dve", lambda e: e.tensor_tensor(out=cb_[:, 0:n].rearrange("p (c t) -> p c t", t=CH),
                                                       in0=c3[:, :, CH - 1:CH].to_broadcast([128, nch, CH]), in1=c3, op=ALU.subtract),
                      reads=[cum], writes=[cb_])
                kb.op("dve", lambda e: e.tensor_tensor(out=cb_[:, 0:n], in0=cb_[:, 0:n], in1=lg[:, 0:n], op=ALU.add),
                      reads=[cb_, lg], writes=[cb_])
                cum = cb_
                c3 = cum[:, 0:n].rearrange("p (c t) -> p c t", t=CH)
                mi, li = CH // 2, 0
            else:
                mi, li = CH // 2 - 1, CH - 1
            dc = decp.next()
            kb.op("act", lambda e: e.activation(out=dc[:, 0, 0:nch], in_=c3[:, :, li], func=AF.Exp), reads=[cum], writes=[dc])
            kb.op("act", lambda e: e.activation(out=dc[:, 2, 0:nch], in_=c3[:, :, mi], func=AF.Exp), reads=[cum], writes=[dc])
            kb.op("dve", lambda e: e.tensor_tensor(out=dc[:, 1, 0:nch], in0=c3[:, :, li], in1=c3[:, :, mi], op=ALU.subtract),
                  reads=[cum], writes=[dc])
            kb.op("act", lambda e: e.activation(out=dc[:, 1, 0:nch], in_=dc[:, 1, 0:nch], func=AF.Exp), reads=[dc], writes=[dc])
            kb.dma("act", dec_d[kd, s_][h * 128:(h + 1) * 128, :, c0:c0 + nch], dc[:, :, 0:nch], dc, reads=[dc])
            d1 = of32.next()
            kb.op("dve", lambda e: e.tensor_tensor(out=d1[:, 0:n].rearrange("p (c t) -> p c t", t=CH), in0=c3,
                                                   in1=c3[:, :, mi:mi + 1].to_broadcast([128, nch, CH]), op=ALU.subtract),
                  reads=[cum], writes=[d1])
            e1 = of32.next()
            kb.op("act", lambda e: e.activation(out=e1[:, 0:n], in_=d1[:, 0:n], func=AF.Exp), reads=[d1], writes=[e1])
            kb.op("act", lambda e: e.activation(out=d1[:, 0:n], in_=d1[:, 0:n], func=AF.Exp, scale=-1.0), reads=[d1], writes=[d1])
            qo = obf.next()
            kb.op("dve", lambda e: e.tensor_tensor(out=qo[:, 0:n], in0=qsrc[:, 0:n], in1=e1[:, 0:n], op=ALU.mult),
                  reads=[qsrc, e1], writes=[qo])
            kb.dma("sp", qt_d[kd, s_][h * 128:(h + 1) * 128, t0:t0 + n], qo[:, 0:n], qo, reads=[qo])
            ko = obf.next()
            kb.op("dve", lambda e: e.tensor_tensor(out=ko[:, 0:n], in0=ksrc[:, 0:n], in1=d1[:, 0:n], op=ALU.mult),
                  reads=[ksrc, d1], writes=[ko])
            kb.dma("sp", kt_d[kd, s_][h * 128:(h + 1) * 128, t0:t0 + n], ko[:, 0:n], ko, reads=[ko])

        hts = {}

        def do_norm(ti):
            t0_, n_ = TT[ti]
            hts[ti] = hpool.next()
            norm_mod(src, t0_, n_, l, 0, 1 if t0_ == 0 else 0, hts[ti], xpool, sqpool, sspool, rspool)

        do_norm(0)
        for ti, (t0, n) in enumerate(TT):
            j = 1 if t0 == 0 else 0
            ht = hts.pop(ti)
            if ti + 1 < len(TT):
                do_norm(ti + 1)
            for tb in range(4):
                wt = wtm.next()
                kb.dma("pool", wt[:], w_tm[l, tb], wt, writes=[wt])
                for m in range(n // 128):
                    pt = ptm.next()
                    for kc in range(KC):
                        kb.op("pe", lambda e: e.matmul(pt[:], lhsT=ht[:, kc, m * 128:(m + 1) * 128], rhs=wt[:, kc, :],
                                                       start=(kc == 0), stop=(kc == KC - 1)),
                              reads=[ht, wt], writes=[pt], acc=(kc > 0), sig=(kc == KC - 1))
                    ot = obf.next()
                    kb.op("act", lambda e: e.copy(out=ot[:], in_=pt[:]), reads=[pt], writes=[ot])
                    r0 = t0 + m * 128
                    if tb < 2:
                        dst = vN[r0:r0 + 128, tb * 512:(tb + 1) * 512]
                    else:
                        dst = sv["hg" if tb == 2 else "gl"][r0:r0 + 128, :]
                    kb.dma("act", dst, ot[:], ot, reads=[ot])
            for s_ in range(2):
                wt = wr.next()
                kb.dma("pool", wt[:], w_r[l, s_], wt, writes=[wt])
                pt = pfm.next()
                for kc in range(KC):
                    kb.op("pe", lambda e: e.matmul(pt[0:16, 0:n], lhsT=wt[:, kc, :], rhs=ht[:, kc, 0:n],
                                                   start=(kc == 0), stop=(kc == KC - 1)),
                          reads=[ht, wt], writes=[pt], acc=(kc > 0), sig=(kc == KC - 1))
                kb.op("act", lambda e: e.copy(out=rT[s_][:, 0:n], in_=pt[0:16, 0:n]), reads=[pt], writes=[rT[s_]])
            kb.dma("sp", ropc[:, 0:n], ropec[:, t0:t0 + n], ropc, writes=[ropc])
            kb.dma("sp", rops[:, 0:n], ropes[:, t0:t0 + n], rops, writes=[rops])
            for (kind, h) in FMB_ORDER:
                bi = FMB.index((kind, h))
                wt = wfm.next()
                kb.dma("pool", wt[:], w_fm[l, bi], wt, writes=[wt])
                pt = pfm.next()
                for kc in range(KC):
                    kb.op("pe", lambda e: e.matmul(pt[:, 0:n], lhsT=wt[:, kc, :], rhs=ht[:, kc, 0:n],
                                                   start=(kc == 0), stop=(kc == KC - 1)),
                          reads=[ht, wt], writes=[pt], acc=(kc > 0), sig=(kc == KC - 1))
                if kind == "naq":
                    ot = obf.next()
                    kb.op("act", lambda e: e.activation(out=ot[:, 0:n], in_=pt[:, 0:n], func=AF.Copy, scale=float(128 ** -0.5)),
                          reads=[pt], writes=[ot])
                    kb.dma("act", qT[h * 128:(h + 1) * 128, t0:t0 + n], ot[:, 0:n], ot, reads=[ot])
                elif kind == "nak":
                    ot = obf.next()
                    kb.op("act", lambda e: e.copy(out=ot[:, 0:n], in_=pt[:, 0:n]), reads=[pt], writes=[ot])
                    kb.dma("act", kT[h * 128:(h + 1) * 128, t0:t0 + n], ot[:, 0:n], ot, reads=[ot])
                elif kind in ("hgg", "glg"):
                    ot = obf.next()
                    kb.op("act", lambda e: e.activation(out=ot[:, 0:n], in_=pt[:, 0:n], func=AF.Silu), reads=[pt], writes=[ot])
                    kb.dma("act", gs[kind[:2]][h * 128:(h + 1) * 128, t0:t0 + n], ot[:, 0:n], ot, reads=[ot])
                elif kind == "hgq":
                    kb.op("act", lambda e: e.activation(out=keep["q"][:, 0:n], in_=pt[:, 0:n], func=AF.Silu),
                          reads=[pt], writes=[keep["q"]])
                elif kind in ("hgz1", "hgz2"):
                    s_ = 1 if kind == "hgz1" else 2
                    sg = of32.next()
                    kb.op("act", lambda e: e.activation(out=sg[:, 0:n], in_=pt[:, 0:n], func=AF.Sigmoid), reads=[pt], writes=[sg])
                    kk, lg = keep["k%d" % s_], keep["lg%d" % s_]
                    if lbz:
                        kb.op("dve", lambda e: e.tensor_scalar(out=kk[:, 0:n], in0=sg[:, 0:n], scalar1=-1.0, scalar2=1.0,
                                                               op0=ALU.mult, op1=ALU.add), reads=[sg], writes=[kk])
                        kb.op("act", lambda e: e.activation(out=lg[:, 0:n], in_=sg[:, 0:n], func=AF.Ln), reads=[sg], writes=[lg])
                    else:
                        kb.op("dve", lambda e: e.tensor_scalar(out=kk[:, 0:n], in0=sg[:, 0:n], scalar1=nml_t[:, s_ - 1, h:h + 1],
                                                               scalar2=oml_t[:, s_ - 1, h:h + 1], op0=ALU.mult, op1=ALU.add),
                              reads=[sg, nml_t, oml_t], writes=[kk])
                        kb.op("act", lambda e: e.activation(out=lg[:, 0:n], in_=sg[:, 0:n], func=AF.Ln,
                                                            scale=oml_t[:, s_ - 1, h:h + 1], bias=lb_t[:, s_ - 1, h:h + 1]),
                              reads=[sg, oml_t, lb_t], writes=[lg])
                    if s_ == 2:
                        for s2 in (1, 2):
                            gate_pipe("hg", h, s2, keep["lg%d" % s2], keep["q"], keep["k%d" % s2], t0, n)
                elif kind in ("glq", "glk"):
                    kb.op("act", lambda e: e.activation(out=keep["qs"][:, 0:n], in_=pt[:, 0:n], func=AF.Copy,
                                                        scale=(0.125 if kind == "glq" else 1.0)), reads=[pt], writes=[keep["qs"]])
                elif kind in ("glqs", "glks"):
                    dstk = keep["q"] if kind == "glqs" else keep["k1"]
                    tmp = of32.next()
                    kb.op("dve", lambda e: e.tensor_tensor(out=tmp[:, 0:n], in0=pt[:, 0:n], in1=rops[:, 0:n], op=ALU.mult),
                          reads=[pt, rops], writes=[tmp])
                    kb.op("dve", lambda e: e.tensor_tensor(out=dstk[:, 0:n], in0=keep["qs"][:, 0:n], in1=ropc[:, 0:n], op=ALU.mult),
                          reads=[keep["qs"], ropc], writes=[dstk])
                    kb.op("dve", lambda e: e.scalar_tensor_tensor(out=dstk[:, 0:n], in0=tmp[:, 0:n],
                                                                  scalar=(0.125 if kind == "glqs" else 1.0), in1=dstk[:, 0:n],
                                                                  op0=ALU.mult, op1=ALU.add), reads=[tmp, dstk], writes=[dstk])
                    if kind == "glks":
                        for s_ in (1, 2):
                            pg = pfm.next()
                            kb.op("pe", lambda e: e.matmul(pg[:, 0:n], lhsT=gup_t[:, l, s_ - 1, h, :], rhs=rT[s_ - 1][:, 0:n],
                                                           start=True, stop=True), reads=[gup_t, rT[s_ - 1]], writes=[pg])
                            sg = of32.next()
                            kb.op("act", lambda e: e.activation(out=sg[:, 0:n], in_=pg[:, 0:n], func=AF.Sigmoid,
                                                                bias=gb_t[:, l, s_ - 1, h:h + 1]), reads=[pg, gb_t], writes=[sg])
                            lg = keep["lg%d" % s_]
                            kb.op("act", lambda e: e.activation(out=lg[:, 0:n], in_=sg[:, 0:n], func=AF.Ln), reads=[sg], writes=[lg])
                            kb.op("dve", lambda e: e.tensor_scalar(out=lg[:, 0:n], in0=lg[:, 0:n], scalar1=1.0 / 16.0, scalar2=None,
                                                                   op0=ALU.mult), reads=[lg], writes=[lg])
                            gate_pipe("gl", h, s_, lg, keep["q"], keep["k1"], t0, n)
        kb.end()

    def scan_pass(l, s_, do_ctx_out):
        kb.begin()
        S = {kd: kb.sb("S" + kd, [128, 4, 128], F32) for kd in ("hg", "gl")}
        Sb = {kd: kb.sb("Sb" + kd, [128, 4, 128], BF16) for kd in ("hg", "gl")}
        dec = {kd: kb.sb("dec" + kd, [128, 4, 3, NCH], F32) for kd in ("hg", "gl")}
        qpool = kb.pool_tiles("sq", [128, 4, 256], BF16, 4)
        kpool = kb.pool_tiles("sk", [128, 4, 256], BF16, 4)
        kSp = kb.pool_tiles("skS", [128, 4, 256], BF16, 4)
        kRp = kb.pool_tiles("skR", [128, 4, 256], BF16, 4)
        qRp = kb.pool_tiles("sqR", [128, 4, 256], BF16, 4)
        vpool = kb.pool_tiles("svv", [32, 8, 512], BF16, 4)
        opool = kb.pool_tiles("so", [128, 4, 256], F32, 4)
        ktm = kb.pool_tiles("ktm", [32, 512], BF16, 4)
        attm = kb.pool_tiles("attm", [32, 4, 32], BF16, 4)
        tmpS = kb.pool_tiles("tmpS", [128, 4, 128], F32, 4)
        p_tr = kb.pool_tiles("p_tr", [32, 512], BF16, 2, psum=True)
        p_at = kb.pool_tiles("p_at", [32, 4, 32], F32, 2, psum=True)
        p_o = kb.pool_tiles("p_o", [128, 4, 32], F32, 2, psum=True)
        pend = {}
        p_u = kb.pool_tiles("p_u", [128, 4, 128], F32, 2, psum=True)
        for kd in ("hg", "gl"):
            kb.dma("sp", dec[kd][:], dec_d[kd, s_].rearrange("(h p) a c -> p h a c", p=128), dec[kd], writes=[dec[kd]])

        def init_state(zero):
            for ki, kd in enumerate(("hg", "gl")):
                if zero:
                    kb.op("pool", lambda e: e.memset(S[kd][:], 0.0), writes=[S[kd]])
                else:
                    g0 = tmpS.next()
                    g1 = tmpS.next()
                    kb.dma("sp", g0[:].rearrange("p h d -> p (h d)"), xg_st[0:128, ki, :], g0, writes=[g0])
                    kb.dma("sp", g1[:].rearrange("p h d -> p (h d)"), xg_st[128:256, ki, :], g1, writes=[g1])
                    kb.op("dve", lambda e: e.tensor_scalar(out=S[kd][:], in0=g0[:], scalar1=pm_t[:, 0:1], scalar2=None, op0=ALU.mult),
                          reads=[g0, pm_t], writes=[S[kd]])
                    kb.op("dve", lambda e: e.scalar_tensor_tensor(out=S[kd][:], in0=g1[:], scalar=pm_t[:, 1:2], in1=S[kd][:],
                                                                  op0=ALU.mult, op1=ALU.add), reads=[g1, pm_t, S[kd]], writes=[S[kd]])

        def run_blocks(blocks, rev):
            steps = []
            for bi, (b0, bn) in enumerate(blocks):
                cl = list(range(bn // CH))
                if rev:
                    cl = cl[::-1]
                for k_, ci in enumerate(cl):
                    steps.append((bi, ci, b0 // CH + ci, k_ == len(cl) - 1))
            loaded = {}

            def load_block(bi):
                if bi >= len(blocks) or bi in loaded:
                    return
                b0, bn = blocks[bi]
                tiles = {}
                for kd in ("hg", "gl"):
                    qt_ = qpool.next(); kt_ = kpool.next(); vt_ = vpool.next(); ot_ = opool.next()
                    kb.dma("sp", qt_[:, :, 0:bn], qt_d[kd, s_].rearrange("(h p) n -> p h n", p=128)[:, :, b0:b0 + bn], qt_, writes=[qt_])
                    kb.dma("sp", kt_[:, :, 0:bn], kt_d[kd, s_].rearrange("(h p) n -> p h n", p=128)[:, :, b0:b0 + bn], kt_, writes=[kt_])
                    kb.dma("sp", vt_[:, 0:bn // CH, :], sv[kd][b0:b0 + bn, :].rearrange("(c p) f -> p c f", p=CH), vt_, writes=[vt_])
                    safe, risky = (0, 1) if s_ == 1 else (1, 0)
                    kS = kSp.next(); kR = kRp.next(); qR = qRp.next()
                    for (dst_, src_, mk) in ((kS, kt_, safe), (kR, kt_, risky), (qR, qt_, risky)):
                        kb.op("dve", lambda e: e.tensor_tensor(out=dst_[:, :, 0:bn], in0=src_[:, :, 0:bn],
                                                                in1=mEL[mk][:, 0:bn].rearrange("p (o n) -> p o n", o=1).to_broadcast([128, 4, bn]),
                                                                op=ALU.mult), reads=[src_, mEL[mk]], writes=[dst_])
                    tiles[kd] = (qt_, kt_, vt_, ot_, kS, kR, qR)
                loaded[bi] = tiles

            def prep(st, kd):
                bi, ci, gc, _ = st
                qt_, kt_, vt_, ot_, kS, kR, qR = loaded[bi][kd]
                cs = slice(ci * CH, (ci + 1) * CH)
                ptr = p_tr.next()
                for h in range(4):
                    kb.op("pe", lambda e: e.transpose(ptr[:, h * 128:(h + 1) * 128], kt_[:, h, cs], ident[:]),
                          reads=[kt_, ident], writes=[ptr], acc=(h > 0), sig=(h == 3))
                km = ktm.next()
                kb.op("act", lambda e: e.copy(out=km[:], in_=ptr[:]), reads=[ptr], writes=[km])
                pa = p_at.next()
                for h in range(4):
                    kb.op("pe", lambda e: e.matmul(pa[:, h, :], lhsT=kS[:, h, cs], rhs=qt_[:, h, cs], start=True, stop=False),
                          reads=[kS, qt_], writes=[pa], acc=(h > 0), sig=False)
                    kb.op("pe", lambda e: e.matmul(pa[:, h, :], lhsT=kR[:, h, cs], rhs=qR[:, h, cs], start=False, stop=True),
                          reads=[kR, qR], writes=[pa], acc=True, sig=(h == 3))
                am = attm.next()
                kb.op("dve", lambda e: e.tensor_tensor(out=am[:], in0=pa[:],
                                                       in1=tri[s_][:].rearrange("s (o t) -> s o t", o=1).to_broadcast([32, 4, 32]),
                                                       op=ALU.mult), reads=[pa, tri[s_]], writes=[am])
                pu = p_u.next()
                for h in range(4):
                    kb.op("pe", lambda e: e.matmul(pu[:, h, :], lhsT=km[:, h * 128:(h + 1) * 128], rhs=vt_[:, ci, h * 128:(h + 1) * 128],
                                                   start=True, stop=True), reads=[km, vt_], writes=[pu], acc=(h > 0), sig=(h == 3))
                tu = tmpS.next()
                kb.op("dve", lambda e: e.tensor_tensor(out=tu[:], in0=pu[:],
                                                       in1=dec[kd][:, :, 1, gc:gc + 1].to_broadcast([128, 4, 128]), op=ALU.mult),
                      reads=[pu, dec[kd]], writes=[tu])
                return am, tu

            def state(st, kd, am, tu):
                bi, ci, gc, _ = st
                qt_, kt_, vt_, ot_, kS, kR, qR = loaded[bi][kd]
                cs = slice(ci * CH, (ci + 1) * CH)
                kb.op("pool", lambda e: e.tensor_tensor(out=Sb[kd][:], in0=S[kd][:],
                                                        in1=dec[kd][:, :, 2, gc:gc + 1].to_broadcast([128, 4, 128]), op=ALU.mult),
                      reads=[S[kd], dec[kd]], writes=[Sb[kd]])
                flush(kd)
                po = p_o.next()
                for h in range(4):
                    kb.op("pe", lambda e: e.matmul(po[:, h, :], lhsT=vt_[:, ci, h * 128:(h + 1) * 128], rhs=am[:, h, :],
                                                   start=True, stop=False), reads=[vt_, am], writes=[po], acc=(h > 0), sig=False)
                    kb.op("pe", lambda e: e.matmul(po[:, h, :], lhsT=Sb[kd][:, h, :], rhs=qt_[:, h, cs],
                                                   start=False, stop=True), reads=[Sb[kd], qt_], writes=[po], acc=True, sig=(h == 3))
                pend[kd] = (po, ot_, cs)
                for h in range(4):
                    kb.op("dve", lambda e: e.scalar_tensor_tensor(out=S[kd][:, h, :], in0=S[kd][:, h, :], scalar=dec[kd][:, h, 0, gc:gc + 1],
                                                                  in1=tu[:, h, :], op0=ALU.mult, op1=ALU.add),
                          reads=[S[kd], dec[kd], tu], writes=[S[kd]])

            def flush(kd):
                if kd in pend:
                    po_, ot2, cs2 = pend.pop(kd)
                    kb.op("act", lambda e: e.copy(out=ot2[:, :, cs2], in_=po_[:]), reads=[po_], writes=[ot2])

            load_block(0)
            load_block(1)
            pre = {kd: prep(steps[0], kd) for kd in ("hg", "gl")}
            for i, st in enumerate(steps):
                nxt = None
                if i + 1 < len(steps):
                    nxt = {kd: prep(steps[i + 1], kd) for kd in ("hg", "gl")}
                for kd in ("hg", "gl"):
                    state(st, kd, *pre[kd])
                if st[3]:
                    b0, bn = blocks[st[0]]
                    for kd in ("hg", "gl"):
                        flush(kd)
                        ot_ = loaded[st[0]][kd][3]
                        kb.dma("act", o_d[kd, s_].rearrange("(h p) n -> p h n", p=128)[:, :, b0:b0 + bn], ot_[:, :, 0:bn], ot_, reads=[ot_])
                    load_block(st[0] + 2)
                pre = nxt

        SB = [(b0, 256) for b0 in range(0, NT, 256)]
        if s_ == 1:
            init_state(True)
            run_blocks(SB, False)
            for ki, kd in enumerate(("hg", "gl")):
                kb.dma("pool", xs_st[:, ki, :], S[kd][:].rearrange("p h d -> p (h d)"), S[kd], reads=[S[kd]])
        else:
            init_state(False)
            run_blocks(SB[1:][::-1], True)
            init_state(True)
            run_blocks(SB[:1], True)
        kb.end()

    def exchange1():
        kb.begin()
        a = kb.sb("xa_k", [128, 8, 256], BF16)
        b_ = kb.sb("xa_v", [128, 2, 1024], BF16)
        kb.dma("pool", a[:], kT.rearrange("(h p) n -> p h n", p=128)[:, :, NT - 256:NT], a, writes=[a])
        kb.dma("pool", xs_kv[0:1024, :].rearrange("(h p) n -> p h n", p=128), a[:], a, reads=[a])
        kb.dma("pool", b_[:], vN[NT - 256:NT, :].rearrange("(c p) f -> p c f", p=128), b_, writes=[b_])
        kb.dma("pool", xs_kv[1024:2048, :].rearrange("(c p x) n -> p c (x n)", p=128, x=4), b_[:], b_, reads=[b_])
        kb.barrier()
        g = nc.gpsimd
        cc = kb.ccsem
        for (src, dst) in ((xs_kv, xg_kv), (xs_st.rearrange("p k f -> p (k f)"), xg_st.rearrange("p k f -> p (k f)"))):
            g.collective_compute("AllGather", ALU.bypass, replica_groups=RG, ins=[src], outs=[dst]).then_inc(cc)
            kb.cccount += 1
        g.wait_ge(cc, kb.cccount)
        for r in range(2):
            ka = kb.sb("ka%d" % r, [128, 8, 256], BF16)
            va = kb.sb("va%d" % r, [128, 2, 1024], BF16)
            kb.dma("pool", ka[:], xg_kv[r * 2048:r * 2048 + 1024, :].rearrange("(h p) n -> p h n", p=128), ka, writes=[ka])
            kb.dma("pool", va[:], xg_kv[r * 2048 + 1024:r * 2048 + 2048, :].rearrange("(c p x) n -> p c (x n)", p=128, x=4), va, writes=[va])
            if r == 0:
                ksel = kb.sb("ksel", [128, 8, 256], BF16)
                vsel = kb.sb("vsel", [128, 2, 1024], BF16)
                kb.op("dve", lambda e: e.tensor_scalar(out=ksel[:], in0=ka[:], scalar1=pm_t[:, 0:1], scalar2=None, op0=ALU.mult),
                      reads=[ka, pm_t], writes=[ksel])
                kb.op("dve", lambda e: e.tensor_scalar(out=vsel[:], in0=va[:], scalar1=pm_t[:, 0:1], scalar2=None, op0=ALU.mult),
                      reads=[va, pm_t], writes=[vsel])
            else:
                kb.op("dve", lambda e: e.scalar_tensor_tensor(out=ksel[:], in0=ka[:], scalar=pm_t[:, 1:2], in1=ksel[:],
                                                              op0=ALU.mult, op1=ALU.add), reads=[ka, pm_t, ksel], writes=[ksel])
                kb.op("dve", lambda e: e.scalar_tensor_tensor(out=vsel[:], in0=va[:], scalar=pm_t[:, 1:2], in1=vsel[:],
                                                              op0=ALU.mult, op1=ALU.add), reads=[va, pm_t, vsel], writes=[vsel])
        for a_ in range(2):
            kb.dma("sp", kT.rearrange("(h p) n -> p h n", p=128)[:, :, NT + a_ * 128:NT + (a_ + 1) * 128],
                   ksel[:, :, (1 - a_) * 128:(2 - a_) * 128], ksel, reads=[ksel])
            kb.dma("sp", vN[NT + a_ * 128:NT + (a_ + 1) * 128, :], vsel[:, 1 - a_, :], vsel, reads=[vsel])
        kb.end()

    def attention(l, with_ctx_self):
        kb.begin()
        kc_t = kb.sb("kc_t", [128, 8, CTX], BF16)
        vc_t = kb.sb("vc_t", [128, 2, 1024], BF16)
        kb.dma("sp", kc_t[:], kT.rearrange("(h p) n -> p h n", p=128)[:, :, 0:CTX], kc_t, writes=[kc_t])
        kb.dma("sp", vc_t[:], vN[0:CTX, :].rearrange("(c p) f -> p c f", p=128), vc_t, writes=[vc_t])
        bias_t = [kb.sb("bias%d" % c, [128, 8, 5, 128], BF16) for c in range(ncls)]
        for c in range(ncls):
            kb.dma("pool", bias_t[c][:], nabias[l, c], bias_t[c], writes=[bias_t[c]])
        qpool = kb.pool_tiles("aq", [128, 8, 128], BF16, 2)
        kpool = kb.pool_tiles("ak", [128, 8, 640], BF16, 2)
        vpool = kb.pool_tiles("av", [128, 5, 1024], BF16, 2)
        ppool = kb.pool_tiles("ap", [128, 7, 128], BF16, 3)
        opool = kb.pool_tiles("ao", [128, 8, 128], BF16, 2)
        rpool = kb.pool_tiles("ar", [128, 4, 128], F32, 2)
        p_s = kb.pool_tiles("p_s", [128, 8, 128], F32, 2, psum=True)
        p_n = kb.pool_tiles("p_n", [128, 4, 128], F32, 1, psum=True)
        p_d = kb.pool_tiles("p_d", [128, 4, 128], F32, 1, psum=True)
        qtiles = [(CTX + t * 128, plan[t], cls_of[t]) for t in range(nqt)]
        if with_ctx_self:
            qtiles = [(0, None, None), (128, None, None)] + qtiles
        for (q0, lo, cls) in qtiles:
            qt_ = qpool.next()
            kb.dma("sp", qt_[:], qT.rearrange("(h p) n -> p h n", p=128)[:, :, q0:q0 + 128], qt_, writes=[qt_])
            nloc = 0
            if lo is not None:
                nloc = 5
                kt_ = kpool.next(); vt_ = vpool.next()
                k0 = CTX + lo * 128
                kb.dma("sp", kt_[:], kT.rearrange("(h p) n -> p h n", p=128)[:, :, k0:k0 + 640], kt_, writes=[kt_])
                kb.dma("sp", vt_[:], vN[k0:k0 + 640, :].rearrange("(c p) f -> p c f", p=128), vt_, writes=[vt_])
            nk = nloc + 2
            ot_ = opool.next()
            for hg_ in range(2):
                pn = p_n.next(); pd = p_d.next()
                for hh in range(4):
                    h = hg_ * 4 + hh
                    ps_ = p_s.next()
                    for j in range(nk):
                        if j < nloc:
                            kap = kt_[:, h, j * 128:(j + 1) * 128]
                        else:
                            kap = kc_t[:, h, (j - nloc) * 128:(j - nloc + 1) * 128]
                        hasb = j < nloc
                        kb.op("pe", lambda e: e.matmul(ps_[:, j, :], lhsT=kap, rhs=qt_[:, h, :], start=True, stop=not hasb),
                              reads=[qt_, kc_t] + ([kt_] if nloc else []), writes=[ps_], acc=(j > 0), sig=False)
                        if hasb:
                            kb.op("pe", lambda e: e.matmul(ps_[:, j, :], lhsT=ident[:], rhs=bias_t[cls][:, h, j, :], start=False, stop=True),
                                  reads=[ident, bias_t[cls]], writes=[ps_], acc=True, sig=False)
                    kb.op("pe", lambda e: e.matmul(ps_[:, 7, 0:2], lhsT=ident[:], rhs=ident[:, 0:2], start=True, stop=True),
                          reads=[ident], writes=[ps_], acc=True, sig=True)
                    pt_ = ppool.next()
                    kb.op("act", lambda e: e.activation(out=pt_[:, 0:nk, :], in_=ps_[:, 0:nk, :], func=AF.Exp), reads=[ps_], writes=[pt_])
                    for j in range(nk):
                        if j < nloc:
                            vap = vt_[:, j, h * 128:(h + 1) * 128]
                        else:
                            vap = vc_t[:, j - nloc, h * 128:(h + 1) * 128]
                        kb.op("pe", lambda e: e.matmul(pn[:, hh, :], lhsT=vap, rhs=pt_[:, j, :], start=(j == 0), stop=(j == nk - 1)),
                              reads=[pt_, vc_t] + ([vt_] if nloc else []), writes=[pn], acc=not (j == 0 and hh == 0), sig=False)
                    for j in range(nk):
                        kb.op("pe", lambda e: e.matmul(pd[:, hh, :], lhsT=ones[:], rhs=pt_[:, j, :], start=(j == 0), stop=(j == nk - 1)),
                              reads=[pt_, ones], writes=[pd], acc=not (j == 0 and hh == 0), sig=(j == nk - 1))
                rt_ = rpool.next()
                kb.op("dve", lambda e: e.reciprocal(out=rt_[:], in_=pd[:]), reads=[pd], writes=[rt_])
                kb.op("dve", lambda e: e.tensor_tensor(out=ot_[:, hg_ * 4:(hg_ + 1) * 4, :], in0=pn[:], in1=rt_[:], op=ALU.mult),
                      reads=[pn, rt_], writes=[ot_])
            kb.dma("act", mixT[0:1024, :].rearrange("(h p) n -> p h n", p=128)[:, :, q0:q0 + 128], ot_[:], ot_, reads=[ot_])
        kb.end()

    def post_scan(l, t_list):
        kb.begin()
        o1p = kb.pool_tiles("o1p", [128, 512], F32, 4)
        o2p = kb.pool_tiles("o2p", [128, 512], F32, 4)
        gp = kb.pool_tiles("gp", [128, 512], BF16, 4)
        sqp = kb.pool_tiles("sqp", [128, 512], BF16, 3)
        rsp = kb.pool_tiles("rsp", [128, 512], F32, 3)
        mp = kb.pool_tiles("mp", [128, 512], BF16, 4)
        pss = kb.pool_tiles("pss", [128, 512], F32, 4, psum=True)
        for (t0, n) in t_list:
            for ki, kd in enumerate(("hg", "gl")):
                for h in range(4):
                    o1 = o1p.next(); o2 = o2p.next(); g_ = gp.next()
                    rows = slice(h * 128, (h + 1) * 128)
                    kb.dma("sp", o1[:, 0:n], o_d[kd, 1][rows, t0:t0 + n], o1, writes=[o1])
                    kb.dma("sp", o2[:, 0:n], o_d[kd, 2][rows, t0:t0 + n], o2, writes=[o2])
                    kb.dma("sp", g_[:, 0:n], gs[kd][rows, t0:t0 + n], g_, writes=[g_])
                    kb.op("dve", lambda e: e.tensor_tensor(out=o1[:, 0:n], in0=o1[:, 0:n], in1=o2[:, 0:n], op=ALU.add),
                          reads=[o1, o2], writes=[o1])
                    sq = sqp.next()
                    kb.op("act", lambda e: e.activation(out=sq[:, 0:n], in_=o1[:, 0:n], func=AF.Square), reads=[o1], writes=[sq])
                    ps_ = pss.next()
                    kb.op("pe", lambda e: e.matmul(ps_[:, 0:n], lhsT=ones[:], rhs=sq[:, 0:n], start=True, stop=True),
                          reads=[ones, sq], writes=[ps_])
                    rs = rsp.next()
                    kb.op("act", lambda e: e.activation(out=rs[:, 0:n], in_=ps_[:, 0:n], func=AF.Ln, bias=float(128 * EPS)),
                          reads=[ps_], writes=[rs])
                    kb.op("act", lambda e: e.activation(out=rs[:, 0:n], in_=rs[:, 0:n], func=AF.Exp, scale=-0.5), reads=[rs], writes=[rs])
                    kb.op("dve", lambda e: e.scalar_tensor_tensor(out=o1[:, 0:n], in0=o1[:, 0:n], scalar=hn_t[:, ki, l:l + 1], in1=rs[:, 0:n],
                                                                  op0=ALU.mult, op1=ALU.mult), reads=[o1, rs, hn_t], writes=[o1])
                    m_ = mp.next()
                    kb.op("dve", lambda e: e.scalar_tensor_tensor(out=m_[:, 0:n], in0=o1[:, 0:n], scalar=float(np.sqrt(128.0)), in1=g_[:, 0:n],
                                                                  op0=ALU.mult, op1=ALU.mult), reads=[o1, g_], writes=[m_])
                    r0 = 1024 + ki * 512 + h * 128
                    kb.dma("act", mixT[r0:r0 + 128, t0:t0 + n], m_[:, 0:n], m_, reads=[m_])
        kb.end()

    def out_proj(l, src, dst, t_list):
        kb.begin()
        mpool = kb.pool_tiles("mxt", [128, KC, 512], BF16, 2)
        xpool = kb.pool_tiles("xr", [128, 512], F32, 6)
        wpool = kb.pool_tiles("wo", [128, KC, 128], BF16, 5)
        pp = kb.pool_tiles("po", [128, 512], F32, 4, psum=True)
        for (t0, n) in t_list:
            j = 1 if t0 == 0 else 0
            mt = mpool.next()
            kb.dma("sp", mt[:, :, 0:n], fm(mixT)[:, :, t0:t0 + n], mt, writes=[mt])
            for ob in range(KC):
                wt = wpool.next()
                kb.dma("pool", wt[:], w_out[l, ob], wt, writes=[wt])
                xt = xpool.next()
                kb.dma("sp", xt[:, 0:n], src[ob * 128:(ob + 1) * 128, t0:t0 + n], xt, writes=[xt])
                pt = pp.next()
                for kc in range(KC):
                    kb.op("pe", lambda e: e.matmul(pt[:, 0:n], lhsT=wt[:, kc, :], rhs=mt[:, kc, 0:n], start=(kc == 0), stop=(kc == KC - 1)),
                          reads=[wt, mt], writes=[pt], acc=(kc > 0), sig=(kc == KC - 1))
                kb.op("dve", lambda e: e.scalar_tensor_tensor(out=xt[:, 0:n], in0=pt[:, 0:n], scalar=modsel(l, 2, ob, j), in1=xt[:, 0:n],
                                                              op0=ALU.mult, op1=ALU.add), reads=[pt, xt, modv], writes=[xt])
                kb.dma("act", dst[ob * 128:(ob + 1) * 128, t0:t0 + n], xt[:, 0:n], xt, reads=[xt])
        kb.end()

    def norm2_phase(l, src, t_list):
        kb.begin()
        xpool = kb.pool_tiles("xN", [128, KC, 512], F32, 2)
        sqpool = kb.pool_tiles("sqN", [128, KC, 512], BF16, 1)
        sspool = kb.pool_tiles("ssN", [128, 512], F32, 1, psum=True)
        rspool = kb.pool_tiles("rsN", [128, 512], F32, 1)
        hpool = kb.pool_tiles("hN", [128, KC, 512], BF16, 3)
        for (t0, n) in t_list:
            j = 1 if t0 == 0 else 0
            ht = hpool.next()
            norm_mod(src, t0, n, l, 1, j, ht, xpool, sqpool, sspool, rspool)
            if t0 == 0:
                kb.dma("act", fm(h2c)[:, :, 1:1 + n], ht[:, :, 0:n], ht, reads=[ht])
            else:
                c0 = t0 - CTX + 1
                kb.dma("act", fm(h2T)[:, :, c0:c0 + n], ht[:, :, 0:n], ht, reads=[ht])
                if t0 + n == NT:
                    kb.dma("pool", xs_h, ht[:, :, n - 1], ht, reads=[ht], slow=True)
        kb.dma("sp", fm(h2T)[:, :, 0], zbf[:], zbf, reads=[zbf], slow=True)
        kb.dma("sp", fm(h2c)[:, :, 0], zbf[:], zbf, reads=[zbf], slow=True)
        kb.dma("sp", fm(h2c)[:, :, CTX + 1], zbf[:], zbf, reads=[zbf], slow=True)
        kb.barrier()
        g = nc.gpsimd
        g.collective_compute("AllGather", ALU.bypass, replica_groups=RG, ins=[xs_h], outs=[xg_h]).then_inc(kb.ccsem)
        kb.cccount += 1
        g.wait_ge(kb.ccsem, kb.cccount)
        h0 = kb.sb("hx0", [128, KC], BF16)
        h1 = kb.sb("hx1", [128, KC], BF16)
        hs = kb.sb("hxs", [128, KC], BF16)
        kb.dma("pool", h0[:], xg_h[0:128, :], h0, writes=[h0])
        kb.dma("pool", h1[:], xg_h[128:256, :], h1, writes=[h1])
        kb.op("dve", lambda e: e.tensor_scalar(out=hs[:], in0=h0[:], scalar1=pm_t[:, 0:1], scalar2=None, op0=ALU.mult),
              reads=[h0, pm_t], writes=[hs])
        kb.op("dve", lambda e: e.scalar_tensor_tensor(out=hs[:], in0=h1[:], scalar=pm_t[:, 1:2], in1=hs[:], op0=ALU.mult, op1=ALU.add),
              reads=[h1, pm_t, hs], writes=[hs])
        kb.dma("sp", fm(h2T)[:, :, NL + 1], hs[:], hs, reads=[hs], slow=True)
        kb.end()

    def ffn(l, mid, dst, do_ctx):
        kb.begin()
        wins = []
        if do_ctx:
            wins.append((h2c, 0, CTX + 2, 0))
        s = 0
        while s < NL:
            w = min(512, NL + 2 - s)
            wins.append((h2T, s, w, CTX + s))
            s += w - 2
        hpool = kb.pool_tiles("hF", [128, KC, 512], BF16, 2)
        gpool = kb.pool_tiles("gF", [128, NFB, 512], BF16, 1)
        wup = kb.pool_tiles("wup", [128, KC, 128], BF16, 4)
        wdn = kb.pool_tiles("wdn", [128, NFB, 128], BF16, 2)
        pab = kb.pool_tiles("pab", [128, 512], F32, 4, psum=True)
        pdn = kb.pool_tiles("pdn", [128, 512], F32, 2, psum=True)
        cvp = kb.pool_tiles("cvp", [128, 512], F32, 6)
        xpool = kb.pool_tiles("xF", [128, 512], F32, 3)
        for (hsrc, s0, w, tok0) in wins:
            j = 1 if tok0 == 0 else 0
            no = w - 2
            ht = hpool.next()
            kb.dma("sp", ht[:, :, 0:w], fm(hsrc)[:, :, s0:s0 + w], ht, writes=[ht])
            gt = gpool.next()
            for jb in range(NFB):
                cv = []
                for half in range(2):
                    blk = half * NFB + jb
                    wt = wup.next()
                    kb.dma("pool", wt[:], w_up[l, blk], wt, writes=[wt])
                    pt = pab.next()
                    for kc in range(KC):
                        kb.op("pe", lambda e: e.matmul(pt[:, 0:w], lhsT=wt[:, kc, :], rhs=ht[:, kc, 0:w], start=(kc == 0), stop=(kc == KC - 1)),
                              reads=[wt, ht], writes=[pt], acc=(kc > 0), sig=(kc == KC - 1))
                    c_ = cvp.next()
                    kb.op("act", lambda e: e.activation(out=c_[:, 0:no], in_=pt[:, 1:1 + no], func=AF.Identity,
                                                        scale=cw_t[:, l, 1, blk:blk + 1], bias=cb_t[:, l, blk:blk + 1]),
                          reads=[pt, cw_t, cb_t], writes=[c_])
                    kb.op("dve", lambda e: e.scalar_tensor_tensor(out=c_[:, 0:no], in0=pt[:, 0:no], scalar=cw_t[:, l, 0, blk:blk + 1],
                                                                  in1=c_[:, 0:no], op0=ALU.mult, op1=ALU.add), reads=[pt, cw_t, c_], writes=[c_])
                    kb.op("dve", lambda e: e.scalar_tensor_tensor(out=c_[:, 0:no], in0=pt[:, 2:2 + no], scalar=cw_t[:, l, 2, blk:blk + 1],
                                                                  in1=c_[:, 0:no], op0=ALU.mult, op1=ALU.add), reads=[pt, cw_t, c_], writes=[c_])
                    cv.append(c_)
                sa = cvp.next()
                kb.op("act", lambda e: e.activation(out=sa[:, 0:no], in_=cv[0][:, 0:no], func=AF.Silu), reads=[cv[0]], writes=[sa])
                kb.op("dve", lambda e: e.tensor_tensor(out=gt[:, jb, 0:no], in0=sa[:, 0:no], in1=cv[1][:, 0:no], op=ALU.mult),
                      reads=[sa, cv[1]], writes=[gt])
            for ob in range(KC):
                wt = wdn.next()
                kb.dma("pool", wt[:], w_dn[l, ob], wt, writes=[wt])
                xt = xpool.next()
                kb.dma("sp", xt[:, 0:no], mid[ob * 128:(ob + 1) * 128, tok0:tok0 + no], xt, writes=[xt])
                pt = pdn.next()
                for kc in range(NFB):
                    kb.op("pe", lambda e: e.matmul(pt[:, 0:no], lhsT=wt[:, kc, :], rhs=gt[:, kc, 0:no], start=(kc == 0), stop=(kc == NFB - 1)),
                          reads=[wt, gt], writes=[pt], acc=(kc > 0), sig=(kc == NFB - 1))
                kb.op("dve", lambda e: e.scalar_tensor_tensor(out=xt[:, 0:no], in0=pt[:, 0:no], scalar=modsel(l, 5, ob, j), in1=xt[:, 0:no],
                                                              op0=ALU.mult, op1=ALU.add), reads=[pt, xt, modv], writes=[xt])
                kb.dma("act", dst[ob * 128:(ob + 1) * 128, tok0:tok0 + no], xt[:, 0:no], xt, reads=[xt])
        kb.end()

    def final_norm(src):
        kb.begin()
        xpool = kb.pool_tiles("xZ", [128, KC, 512], F32, 2)
        sqpool = kb.pool_tiles("sqZ", [128, KC, 512], BF16, 1)
        sspool = kb.pool_tiles("ssZ", [128, 512], F32, 1, psum=True)
        rspool = kb.pool_tiles("rsZ", [128, 512], F32, 1)
        fa = kb.sb("fa", [128, KC], F32)
        kb.op("dve", lambda e: e.tensor_scalar(out=fa[:], in0=vec_t[:, 4, :], scalar1=float(np.sqrt(D)), scalar2=None, op0=ALU.mult),
              reads=[vec_t], writes=[fa])
        for (t0, n) in TT[1:]:
            xt = xpool.next()
            kb.dma("sp", xt[:, :, 0:n], fm(src)[:, :, t0:t0 + n], xt, writes=[xt])
            sq = sqpool.next()
            kb.op("act", lambda e: e.activation(out=sq[:, :, 0:n], in_=xt[:, :, 0:n], func=AF.Square), reads=[xt], writes=[sq])
            ss = sspool.next()
            for c in range(KC):
                kb.op("pe", lambda e: e.matmul(ss[:, 0:n], lhsT=ones[:], rhs=sq[:, c, 0:n], start=(c == 0), stop=(c == KC - 1)),
                      reads=[ones, sq], writes=[ss], acc=(c > 0), sig=(c == KC - 1))
            rs = rspool.next()
            kb.op("act", lambda e: e.activation(out=rs[:, 0:n], in_=ss[:, 0:n], func=AF.Ln, bias=float(D * EPS)), reads=[ss], writes=[rs])
            kb.op("act", lambda e: e.activation(out=rs[:, 0:n], in_=rs[:, 0:n], func=AF.Exp, scale=-0.5), reads=[rs], writes=[rs])
            kb.op("dve", lambda e: e.tensor_tensor(out=xt[:, :, 0:n], in0=xt[:, :, 0:n],
                                                   in1=rs[:, 0:n].rearrange("p (o n) -> p o n", o=1).to_broadcast([128, KC, n]), op=ALU.mult),
                  reads=[rs], writes=[xt])
            for c in range(KC):
                kb.op("act", lambda e: e.activation(out=xt[:, c, 0:n], in_=xt[:, c, 0:n], func=AF.Copy, scale=fa[:, c:c + 1]),
                      reads=[xt, fa], writes=[xt])
            kb.dma("act", fm(outT)[:, :, t0 - CTX:t0 - CTX + n], xt[:, :, 0:n], xt, reads=[xt])
        kb.end()

    setup()
    bufs = [xT, xa, xb]
    src = xT
    for l in range(DEPTH):
        mid, dst = xa, xb
        phase_A(l, src)
        if stop_after == ("A", l):
            break
        scan_pass(l, 1, l == 0)
        exchange1()
        scan_pass(l, 2, l == 0)
        if stop_after == ("S", l):
            break
        attention(l, l == 0)
        tl = TT if l == 0 else TT[1:]
        post_scan(l, tl)
        if stop_after == ("M", l):
            break
        out_proj(l, src, mid, tl)
        norm2_phase(l, mid, tl)
        ffn(l, mid, dst, l == 0)
        src = dst
        if stop_after == ("L", l):
            break
    if stop_after is None:
        final_norm(xb)
    kb.stack = pstack
    kb.phase_tiles = persistent
    kb.end()
    kb.gstack.close()
    return nc


def _pc(v, nchunk):
    return np.ascontiguousarray(v.reshape(nchunk, 128).T)


def _tile_w(w, cols):
    cols = np.asarray(cols)
    sub = w[:, np.where(cols < 0, 0, cols)]
    if (cols < 0).any():
        sub = sub.copy()
        sub[:, cols < 0] = 0.0
    k = w.shape[0] // 128
    return np.ascontiguousarray(sub.reshape(k, 128, len(cols)).transpose(1, 0, 2))


def prep(inp, NL, SEQ, B):
    f32 = np.float32
    g = {k: np.asarray(v, dtype=f32) for k, v in inp.items()}
    plan, cls_of, reps, s0, s1 = na_geometry(NL, SEQ)
    ncls = len(reps)
    NT = CTX + NL
    ar = np.arange
    swap64 = np.concatenate([ar(16, 32), ar(0, 16), ar(48, 64), ar(32, 48)])
    pad = -np.ones(64, np.int64)
    shared = {}
    for half in range(2):
        zf, zb = (4096, 4608) if half == 0 else (4608, 4096)
        rf, rb = (7168, 7184) if half == 0 else (7184, 7168)
        d1, d2 = (0, 1) if half == 0 else (1, 0)
        w_fm = np.zeros((DEPTH, NFMB, 128, KC, 128), f32)
        w_r = np.zeros((DEPTH, 2, 128, KC, 16), f32)
        w_tm = np.zeros((DEPTH, 4, 128, KC, 512), f32)
        for l in range(DEPTH):
            w = g["w_in"][l]
            for bi, (kind, h) in enumerate(FMB):
                if kind == "naq": cols = h * 128 + ar(128)
                elif kind == "nak": cols = 1024 + h * 128 + ar(128)
                elif kind == "hgq": cols = 3072 + h * 128 + ar(128)
                elif kind == "hgz1": cols = zf + h * 128 + ar(128)
                elif kind == "hgz2": cols = zb + h * 128 + ar(128)
                elif kind == "hgg": cols = 5120 + h * 128 + ar(128)
                elif kind == "glq": cols = np.concatenate([5632 + h * 64 + ar(64), pad])
                elif kind == "glqs": cols = np.concatenate([5632 + h * 64 + swap64, pad])
                elif kind == "glk": cols = np.concatenate([5888 + h * 64 + ar(64), pad])
                elif kind == "glks": cols = np.concatenate([5888 + h * 64 + swap64, pad])
                elif kind == "glg": cols = 6656 + h * 128 + ar(128)
                w_fm[l, bi] = _tile_w(w, cols)
            w_r[l, 0] = _tile_w(w, rf + ar(16))
            w_r[l, 1] = _tile_w(w, rb + ar(16))
            for tb, c0 in enumerate((2048, 2560, 3584, 6144)):
                w_tm[l, tb] = _tile_w(w, c0 + ar(512))
        lbp = np.zeros((128, 2, DEPTH, 4), f32)
        gbias = np.zeros((128, DEPTH, 2, 4), f32)
        gup = np.zeros((16, DEPTH, 2, 4, 128), f32)
        cw = np.zeros((128, DEPTH, 3, 2 * NFB), f32)
        for l in range(DEPTH):
            for si, dd in enumerate((d1, d2)):
                lbp[:, si, l, :] = _pc(g["hg_lower_bounds"][dd, l], 4)
                for h in range(4):
                    gbias[0:64, l, si, h] = g["gla_gate_b"][l, dd, h * 64:(h + 1) * 64]
                    gup[:, l, si, h, 0:64] = g["gla_gate_up"][l, dd][:, h * 64:(h + 1) * 64]
            for tap in range(3):
                cw[:, l, tap, :] = _pc(g["conv_w"][l, (2 - tap) if half else tap], 2 * NFB)
        shared[half] = dict(w_fm=w_fm, w_r=w_r, w_tm=w_tm, lbp=lbp, gbias=gbias, gup=gup, cw=cw,
                            pmask=np.tile(np.array([[0.0, 1.0]] if half == 0 else [[1.0, 0.0]], f32), (128, 1)))
        st = s1 if half else s0
        nb = np.zeros((DEPTH, ncls, 128, 8, 5, 128), f32)
        for l in range(DEPTH):
            tab = g["na_rpb"][l].reshape(8, -1)
            for c, t in enumerate(reps):
                idx = st[t]
                val = np.where(idx[None] >= 0, tab[:, np.maximum(idx, 0)], f32(NEG))
                nb[l, c] = val.transpose(2, 0, 1, 3)
        shared[half]["nabias"] = nb
        cosT = np.ones((128, NT), f32)
        sinT = np.zeros((128, NT), f32)
        i = ar(NL)
        gpos = (SEQ - 1 - i) if half else i
        inv = (10000.0 ** (-ar(0, 32, 2, dtype=f32) / f32(32))).astype(f32)
        for base, pos in ((0, gpos // GW), (32, gpos % GW)):
            ang = pos.astype(f32)[None, :] * inv[:, None]
            cosT[base:base + 16, CTX:] = np.cos(ang); cosT[base + 16:base + 32, CTX:] = np.cos(ang)
            sinT[base:base + 16, CTX:] = -np.sin(ang); sinT[base + 16:base + 32, CTX:] = np.sin(ang)
        shared[half]["ropec"] = cosT
        shared[half]["ropes"] = sinT
    common = dict(
        ada_w=g["ada_w"],
        ada_bt=np.ascontiguousarray(g["ada_b"].reshape(DEPTH, 96, 128).transpose(0, 2, 1)),
        vecs=np.ascontiguousarray(np.stack([_pc(g["norm1_g"][0], KC), _pc(g["norm1_g"][1], KC), _pc(g["norm2_g"][0], KC),
                                            _pc(g["norm2_g"][1], KC), _pc(g["final_g"], KC)], axis=1)),
        hnorm=np.ascontiguousarray(np.stack([g["hg_norm_g"].T, g["gla_norm_g"].T], axis=1)),
        cb=np.ascontiguousarray(np.stack([_pc(g["conv_b"][l], 2 * NFB) for l in range(DEPTH)], axis=1)),
        w_out=np.ascontiguousarray(np.stack([np.stack([_tile_w(g["w_out"][l], ob * 128 + ar(128)) for ob in range(KC)]) for l in range(DEPTH)])),
        w_up=np.ascontiguousarray(np.stack([np.stack([_tile_w(g["w_up"][l], bk * 128 + ar(128)) for bk in range(2 * NFB)]) for l in range(DEPTH)])),
        w_dn=np.ascontiguousarray(np.stack([np.stack([_tile_w(g["w_down"][l], ob * 128 + ar(128)) for ob in range(KC)]) for l in range(DEPTH)])),
    )
    in_maps = []
    for r in range(2 * B):
        b, half = r // 2, r % 2
        xl = g["x"][b, half * NL:(half + 1) * NL]
        cx = g["ctx"][b]
        if half:
            xl, cx = xl[::-1], cx[::-1]
        m = dict(common)
        m.update(shared[half])
        m["xT"] = np.ascontiguousarray(np.concatenate([cx, xl], axis=0).T)
        m["cs"] = np.ascontiguousarray(np.stack([_pc(g["c"][b], KC), _pc(g["c_ctx"], KC)], axis=2))
        in_maps.append(m)
    return in_maps, ncls, plan, cls_of


_CACHE = {}


def run(inp, NL, SEQ, B, debug=(), stop_after=None):
    in_maps, ncls, plan, cls_of = prep(inp, NL, SEQ, B)
    nc = build(NL, SEQ, ncls, plan, cls_of, debug=debug, stop_after=stop_after)
    res = run_bass_kernel_spmd(nc, in_maps, core_ids=list(range(2 * B)))
    return res


def kernel(**inputs):
    B, SEQ, _ = inputs["x"].shape
    NL = SEQ // 2
    res = run(inputs, NL, SEQ, B)
    out = np.zeros((B, SEQ, D), np.float32)
    for r in range(2 * B):
        b, half = r // 2, r % 2
        o = np.asarray(res.results[r]["outT"]).T
        if half:
            out[b, NL:] = o[::-1]
        else:
            out[b, :NL] = o
    return out
```

```python
import numpy as np
from contextlib import ExitStack
import concourse.bass as bass
import concourse.mybir as mybir
from concourse.bass_utils import run_bass_kernel_spmd

F32 = mybir.dt.float32
BF16 = mybir.dt.bfloat16
AF = mybir.ActivationFunctionType
ALU = mybir.AluOpType

D = 2048
KC = 16
CTX = 256
DFF = 5632
NFB = DFF // 128
GW = 64
CH = 32
EPS = 1e-6
NEG = -30000.0
DEPTH = 2


class Buf:
    __slots__ = ("w", "r")

    def __init__(self):
        self.w = {}
        self.r = {}


class T:
    def __init__(self, h):
        self.h = h
        self.b = Buf()
        self.dsem = None

    def __getitem__(self, k):
        return self.h[k]


class TV(T):
    def __init__(self, base, idx):
        self.h = base.h
        self.idx = idx
        self.b = Buf()
        self.dsem = None

    def __getitem__(self, k):
        if not isinstance(k, tuple):
            k = (k,)
        return self.h[(k[0], self.idx) + tuple(k[1:])]


class KB:
    def __init__(self, NL, debug=False):
        self.NL = NL
        self.NT = CTX + NL
        self.debug = debug
        nc = bass.Bass("TRN2", target_bir_lowering=False)
        self.nc = nc
        self.eng = {"pe": nc.tensor, "act": nc.scalar, "dve": nc.vector, "pool": nc.gpsimd, "sp": nc.sync}
        self.gstack = ExitStack()
        self.sem = {}
        self.cnt = {}
        for e in ("pe", "act", "dve", "pool"):
            self.sem[e] = self.gstack.enter_context(nc.semaphore("s_" + e))
            self.cnt[e] = 0
        self.known = {e: {} for e in self.eng}
        self.ccsem = self.gstack.enter_context(nc.semaphore('s_cc'))
        self.cccount = 0
        self.free_dsems = []
        self.all_dsems = []
        self.semcount = {}
        self.phase_tiles = []
        self.stack = None
        self.uid = 0
        self.dram = {}

    def name(self, n):
        self.uid += 1
        return f"{n}_{self.uid}"

    def sb(self, n, shape, dt):
        t = T(self.stack.enter_context(self.nc.sbuf_tensor(self.name(n), list(shape), dt)))
        self.phase_tiles.append(t)
        return t

    def ps(self, n, shape, dt=F32):
        t = T(self.stack.enter_context(self.nc.psum_tensor(self.name(n), list(shape), dt)))
        self.phase_tiles.append(t)
        return t

    def pool_tiles(self, n, shape, dt, k, psum=False):
        ts = [(self.ps if psum else self.sb)(n, shape, dt) for _ in range(k)]
        return Rot(ts)

    def dr(self, n, shape, dt, kind="Internal"):
        if self.debug and kind == "Internal" and n in self.debug:
            kind = "ExternalOutput"
        t = self.nc.dram_tensor(n, list(shape), dt, kind=kind).ap()
        self.dram[n] = t
        return t

    def get_dsem(self, t):
        if t.dsem is None:
            if self.free_dsems:
                t.dsem = self.free_dsems.pop()
            else:
                s = self.gstack.enter_context(self.nc.semaphore(self.name("d")))
                self.all_dsems.append(s)
                self.semcount[s] = 0
                t.dsem = s
        return t.dsem

    def _deps(self, e, reads, writes):
        need = {}
        for t in reads:
            for s, c in t.b.w.items():
                if need.get(s, 0) < c:
                    need[s] = c
        for t in writes:
            for s, c in t.b.w.items():
                if need.get(s, 0) < c:
                    need[s] = c
            for s, c in t.b.r.items():
                if need.get(s, 0) < c:
                    need[s] = c
        kn = self.known[e]
        for s, c in need.items():
            if kn.get(s, 0) < c:
                self.eng[e].wait_ge(s, c)
                kn[s] = c

    def op(self, e, fn, reads=(), writes=(), acc=False, sig=True):
        self._deps(e, reads, () if acc else writes)
        ins = fn(self.eng[e])
        s = self.sem[e]
        if sig:
            self.cnt[e] += 1
            ins.then_inc(s, 1)
            c = self.cnt[e]
        else:
            c = self.cnt[e] + 1
        for t in reads:
            t.b.r[s] = c
        for t in writes:
            if acc:
                t.b.w[s] = c
            else:
                t.b.w = {s: c}
                t.b.r = {}
        return ins

    def dma(self, q, out, in_, owner, reads=(), writes=(), slow=False):
        s = self.get_dsem(owner)
        self._deps(q, reads, writes)
        ins = self.eng[q].dma_start(out=out, in_=in_, allow_slow_non_contiguous=True) if slow else self.eng[q].dma_start(out=out, in_=in_)
        self.semcount[s] += 16
        ins.then_inc(s, 16)
        c = self.semcount[s]
        for t in reads:
            t.b.r[s] = c
        for t in writes:
            t.b.w = {s: c}
            t.b.r = {}
        return ins

    def barrier(self):
        targets = [(self.sem[e], self.cnt[e]) for e in self.sem if self.cnt[e] > 0]
        targets += [(s, self.semcount[s]) for s in self.all_dsems if self.semcount[s] > 0]
        for e in self.eng:
            kn = self.known[e]
            for s, c in targets:
                if kn.get(s, 0) < c:
                    self.eng[e].wait_ge(s, c)
                    kn[s] = c

    def begin(self):
        self.stack = ExitStack()
        self.phase_tiles = []

    def end(self):
        self.barrier()
        for t in self.phase_tiles:
            if t.dsem is not None:
                self.free_dsems.append(t.dsem)
                t.dsem = None
        self.stack.close()
        self.stack = None


class Rot:
    def __init__(self, ts):
        self.ts = ts
        self.i = 0

    def next(self):
        t = self.ts[self.i % len(self.ts)]
        self.i += 1
        return t


def fm_blocks():
    bl = []
    for h in range(8):
        bl.append(("naq", h))
    for h in range(8):
        bl.append(("nak", h))
    for h in range(4):
        bl += [("hgq", h), ("hgz1", h), ("hgz2", h)]
    for h in range(4):
        bl.append(("hgg", h))
    for h in range(4):
        bl += [("glq", h), ("glqs", h), ("glk", h), ("glks", h)]
    for h in range(4):
        bl.append(("glg", h))
    return bl


def _interleave(bl):
    na = [b for b in bl if b[0] in ("naq", "nak")]
    rest = [b for b in bl if b[0] not in ("naq", "nak")]
    out = []
    i = 0
    groups = []
    k = 0
    while k < len(rest):
        if rest[k][0] == "hgq":
            groups.append(rest[k:k + 3]); k += 3
        elif rest[k][0] == "glq":
            groups.append(rest[k:k + 4]); k += 4
        else:
            groups.append(rest[k:k + 1]); k += 1
    for g_ in groups:
        out += g_
        if len(g_) > 1 and i < len(na):
            out += na[i:i + 2]; i += 2
    out += na[i:]
    return out


FMB_ORDER = _interleave(fm_blocks())
FMB = fm_blocks()
NFMB = len(FMB)


def token_tiles(NL):
    tl = [(0, CTX)]
    for s in range(0, NL, 512):
        tl.append((CTX + s, min(512, NL - s)))
    return tl


def na_geometry(NL, SEQ):
    nqt = NL // 128
    nkt = nqt + 2
    rows = SEQ // GW
    plan = []
    for t in range(nqt):
        lo = min(max(t - 2, 0), nkt - 5)
        plan.append(lo)
    def struct(flip):
        out = np.full((nqt, 5, 128, 128), -1, np.int32)
        base = (rows // 2) * GW if flip else 0
        def glob(loc):
            loc = np.asarray(loc)
            own = loc < NL
            a = (loc - NL) // 128
            w = (loc - NL) % 128
            pl = (nqt - 1 - a) * 128 + w
            if not flip:
                g_own = loc
                g_par = SEQ - 1 - pl
            else:
                g_own = SEQ - 1 - loc
                g_par = pl
            return np.where(own, g_own, g_par)
        for t in range(nqt):
            qg = glob(t * 128 + np.arange(128))
            qr, qc = qg // GW, qg % GW
            rs = np.clip(qr - 4, 0, rows - 8)
            cs_ = np.clip(qc - 8, 0, GW - 16)
            for j in range(5):
                kg = glob((plan[t] + j) * 128 + np.arange(128))
                kr, kc = kg // GW, kg % GW
                ok = ((kr[:, None] >= rs[None, :]) & (kr[:, None] < rs[None, :] + 8) &
                      (kc[:, None] >= cs_[None, :]) & (kc[:, None] < cs_[None, :] + 16))
                dr = kr[:, None] - qr[None, :] + 7
                dc = kc[:, None] - qc[None, :] + 15
                idx = dr * 31 + dc
                out[t, j] = np.where(ok, idx, -1)
        return out
    s0, s1 = struct(False), struct(True)
    classes = []
    cls_of = []
    for t in range(nqt):
        key = (s0[t].tobytes(), s1[t].tobytes())
        if key in classes:
            cls_of.append(classes.index(key))
        else:
            classes.append(key)
            cls_of.append(len(classes) - 1)
    reps = [cls_of.index(c) for c in range(len(classes))]
    return plan, cls_of, reps, s0, s1


def build(NL, SEQ, ncls, plan, cls_of, debug=(), stop_after=None):
    kb = KB(NL, debug=set(debug))
    nc = kb.nc
    NT = kb.NT
    NCH = NT // CH
    NCC = CTX // CH
    TT = token_tiles(NL)
    nqt = NL // 128
    EI = "ExternalInput"
    xT = kb.dr("xT", [D, NT], F32, EI)
    cs_in = kb.dr("cs", [128, KC, 2], F32, EI)
    ada_w = kb.dr("ada_w", [DEPTH, D, 6 * D], F32, EI)
    ada_bt = kb.dr("ada_bt", [DEPTH, 128, 96], F32, EI)
    vecs = kb.dr("vecs", [128, 5, KC], F32, EI)
    hnorm = kb.dr("hnorm", [128, 2, DEPTH], F32, EI)
    lbp = kb.dr("lbp", [128, 2, DEPTH, 4], F32, EI)
    gbias = kb.dr("gbias", [128, DEPTH, 2, 4], F32, EI)
    cw_in = kb.dr("cw", [128, DEPTH, 3, 2 * NFB], F32, EI)
    cb_in = kb.dr("cb", [128, DEPTH, 2 * NFB], F32, EI)
    gup_in = kb.dr("gup", [16, DEPTH, 2, 4, 128], F32, EI)
    w_fm = kb.dr("w_fm", [DEPTH, NFMB, 128, KC, 128], F32, EI)
    w_r = kb.dr("w_r", [DEPTH, 2, 128, KC, 16], F32, EI)
    w_tm = kb.dr("w_tm", [DEPTH, 4, 128, KC, 512], F32, EI)
    w_out = kb.dr("w_out", [DEPTH, KC, 128, KC, 128], F32, EI)
    w_up = kb.dr("w_up", [DEPTH, 2 * NFB, 128, KC, 128], F32, EI)
    w_dn = kb.dr("w_dn", [DEPTH, KC, 128, NFB, 128], F32, EI)
    ropec = kb.dr("ropec", [128, NT], F32, EI)
    ropes = kb.dr("ropes", [128, NT], F32, EI)
    nabias = kb.dr("nabias", [DEPTH, ncls, 128, 8, 5, 128], F32, EI)
    pmask = kb.dr("pmask", [128, 2], F32, EI)
    outT = kb.dr("outT", [D, NL], F32, "ExternalOutput")

    xa = kb.dr("xa", [D, NT], F32)
    xb = kb.dr("xb", [D, NT], F32)
    qT = kb.dr("qT", [1024, NT], BF16)
    kT = kb.dr("kT", [1024, NT + 256], BF16)
    vN = kb.dr("vN", [NT + 256, 1024], BF16)
    sv = {"hg": kb.dr("hgv", [NT, 512], BF16), "gl": kb.dr("glv", [NT, 512], BF16)}
    gs = {"hg": kb.dr("hggs", [512, NT], BF16), "gl": kb.dr("glgs", [512, NT], BF16)}
    qt_d, kt_d, dec_d, o_d = {}, {}, {}, {}
    for kd in ("hg", "gl"):
        for s_ in (1, 2):
            qt_d[kd, s_] = kb.dr(f"qt_{kd}{s_}", [512, NT], BF16)
            kt_d[kd, s_] = kb.dr(f"kt_{kd}{s_}", [512, NT], BF16)
            dec_d[kd, s_] = kb.dr(f"dec_{kd}{s_}", [512, 3, NCH], F32)
            o_d[kd, s_] = kb.dr(f"o_{kd}{s_}", [512, NT], F32)
    mixT = kb.dr("mixT", [D, NT], BF16)
    h2T = kb.dr("h2T", [D, NL + 2], BF16)
    h2c = kb.dr("h2c", [D, CTX + 2], BF16)
    xs_st = kb.dr("xs_st", [128, 2, 512], F32)
    xg_st = kb.dr("xg_st", [256, 2, 512], F32)
    xs_kv = kb.dr("xs_kv", [2048, 256], BF16)
    xg_kv = kb.dr("xg_kv", [4096, 256], BF16)
    xs_h = kb.dr("xs_h", [128, KC], BF16)
    xg_h = kb.dr("xg_h", [256, KC], BF16)
    RG = [[0, 1], [2, 3], [4, 5], [6, 7]]

    def fm(ap, c0=0, n=None):
        return ap.rearrange("(c p) n -> p c n", p=128)

    kb.begin()
    pstack = kb.stack
    ident = kb.sb("ident", [128, 128], BF16)
    ones = kb.sb("ones", [128, 128], BF16)
    identf = kb.sb("identf", [128, 128], F32)
    modv = kb.sb("modv", [128, DEPTH, 96, 2], F32)
    Asc = kb.sb("Asc", [128, DEPTH, 2, KC, 2], F32)
    vec_t = kb.sb("vec_t", [128, 5, KC], F32)
    hn_t = kb.sb("hn_t", [128, 2, DEPTH], F32)
    lb_t = kb.sb("lb_t", [128, 2, 4], F32)
    oml_t = kb.sb("oml_t", [128, 2, 4], F32)
    nml_t = kb.sb("nml_t", [128, 2, 4], F32)
    zero_t = kb.sb("zero_t", [128, 8], F32)
    one_t = kb.sb("one_t", [128, 8], F32)
    gb_t = kb.sb("gb_t", [128, DEPTH, 2, 4], F32)
    cw_t = kb.sb("cw_t", [128, DEPTH, 3, 2 * NFB], F32)
    cb_t = kb.sb("cb_t", [128, DEPTH, 2 * NFB], F32)
    gup_t = kb.sb("gup_t", [16, DEPTH, 2, 4, 128], BF16)
    pm_t = kb.sb("pm_t", [128, 2], F32)
    rmask = kb.sb("rmask", [128, 512], F32)
    tri = {1: kb.sb("tri1", [32, 32], F32), 2: kb.sb("tri2", [32, 32], F32)}
    zbf = kb.sb("zbf", [128, KC], BF16)
    mEL = {0: kb.sb("mE", [128, 512], BF16), 1: kb.sb("mL", [128, 512], BF16)}
    persistent = list(kb.phase_tiles)
    kb.stack = None

    def setup():
        kb.begin()
        lbraw = kb.sb("lbraw", [128, 2, DEPTH, 4], F32)
        cst = kb.sb("cst", [128, KC, 2], F32)
        csb = kb.sb("csb", [128, KC, 2], BF16)
        abt = kb.sb("abt", [128, DEPTH, 96], F32)
        tmp = kb.sb("tmp", [128, 2, 4], F32)
        kb.op("pool", lambda e: e.memset(ones[:], 1.0), writes=[ones])
        kb.op("pool", lambda e: e.memset(identf[:], 0.0), writes=[identf])
        kb.op("pool", lambda e: e.affine_select(out=identf[:], in_=identf[:], pattern=[[-1, 128]],
                                                compare_op=ALU.not_equal, fill=1.0, base=0, channel_multiplier=1),
              reads=[identf], writes=[identf])
        kb.op("dve", lambda e: e.tensor_copy(out=ident[:], in_=identf[:]), reads=[identf], writes=[ident])
        kb.op("pool", lambda e: e.memset(zero_t[:], 0.0), writes=[zero_t])
        kb.op("pool", lambda e: e.memset(one_t[:], 1.0), writes=[one_t])
        kb.op("pool", lambda e: e.memset(zbf[:], 0.0), writes=[zbf])
        for hf in range(2):
            kb.op("pool", lambda e: e.memset(mEL[hf][:], 0.0), writes=[mEL[hf]])
            kb.op("pool", lambda e: e.memset(mEL[hf][:].rearrange("p (c t) -> p c t", t=CH)[:, :, hf * (CH // 2):(hf + 1) * (CH // 2)], 1.0),
                  reads=[mEL[hf]], writes=[mEL[hf]])
        kb.op("pool", lambda e: e.memset(rmask[:], 1.0), writes=[rmask])
        kb.op("pool", lambda e: e.memset(rmask[:].rearrange("p (c t) -> p c t", t=CH)[:, :, 0:1], 0.0),
              reads=[rmask], writes=[rmask])
        for k_, pstep, cmul in ((1, 1, -1), (2, -1, 1)):
            kb.op("pool", lambda e: e.memset(tri[k_][:], 1.0), writes=[tri[k_]])
            kb.op("pool", lambda e: e.affine_select(out=tri[k_][:], in_=tri[k_][:], pattern=[[pstep, 32]],
                                                    compare_op=ALU.is_ge, fill=0.0, base=0, channel_multiplier=cmul),
                  reads=[tri[k_]], writes=[tri[k_]])
        for (dst, src) in ((vec_t, vecs), (hn_t, hnorm), (lbraw, lbp), (gb_t, gbias), (cw_t, cw_in),
                           (cb_t, cb_in), (pm_t, pmask), (cst, cs_in), (abt, ada_bt.rearrange("l p c -> p l c"))):
            kb.dma("sp", dst[:], src, dst, writes=[dst])
        kb.dma("pool", gup_t[:], gup_in, gup_t, writes=[gup_t])
        kb.op("dve", lambda e: e.tensor_tensor(out=tmp[:], in0=lbraw[:, :, 1, :], in1=lbraw[:, :, 0, :], op=ALU.subtract),
              reads=[lbraw], writes=[tmp])
        kb.op("act", lambda e: e.activation(out=lb_t[:], in_=tmp[:], func=AF.Sigmoid), reads=[tmp], writes=[lb_t])
        kb.op("dve", lambda e: e.tensor_scalar(out=oml_t[:], in0=lb_t[:], scalar1=-1.0, scalar2=1.0, op0=ALU.mult, op1=ALU.add),
              reads=[lb_t], writes=[oml_t])
        kb.op("dve", lambda e: e.tensor_scalar(out=nml_t[:], in0=lb_t[:], scalar1=1.0, scalar2=-1.0, op0=ALU.mult, op1=ALU.add),
              reads=[lb_t], writes=[nml_t])
        kb.op("act", lambda e: e.activation(out=csb[:], in_=cst[:], func=AF.Silu), reads=[cst], writes=[csb])
        wpool = kb.pool_tiles("adaw", [128, KC, 512], BF16, 3)
        pp = kb.pool_tiles("adaps", [128, 4, 2], F32, 2, psum=True)
        for l in range(DEPTH):
            for fb in range(24):
                wt = wpool.next()
                kb.dma("pool", wt[:], ada_w[l].rearrange("(kc p) f -> p kc f", p=128)[:, :, fb * 512:(fb + 1) * 512],
                       wt, writes=[wt])
                pt = pp.next()
                for m in range(4):
                    for kc in range(KC):
                        kb.op("pe", lambda e: e.matmul(pt[:, m, :], lhsT=wt[:, kc, m * 128:(m + 1) * 128], rhs=csb[:, kc, :],
                                                       start=(kc == 0), stop=(kc == KC - 1)),
                              reads=[wt, csb], writes=[pt], acc=not (kc == 0 and m == 0), sig=(kc == KC - 1))
                kb.op("dve", lambda e: e.tensor_tensor(out=modv[:, l, fb * 4:(fb + 1) * 4, :], in0=pt[:],
                                                       in1=abt[:, l, fb * 4:(fb + 1) * 4].to_broadcast([128, 4, 2]) if False else
                                                       abt[:, l, fb * 4:(fb + 1) * 4].rearrange("p (c o) -> p c o", o=1).to_broadcast([128, 4, 2]),
                                                       op=ALU.add),
                      reads=[pt, abt], writes=[modv])
        for l in range(DEPTH):
            for wi in range(2):
                gvec = vec_t[:, l + 2 * wi, :]
                scl = modv[:, l, (3 * wi + 1) * KC:(3 * wi + 2) * KC, :]
                kb.op("dve", lambda e: e.tensor_scalar(out=Asc[:, l, wi, :, :], in0=scl, scalar1=1.0, scalar2=float(np.sqrt(D)),
                                                       op0=ALU.add, op1=ALU.mult), reads=[modv], writes=[Asc])
                kb.op("dve", lambda e: e.tensor_tensor(out=Asc[:, l, wi, :, :], in0=Asc[:, l, wi, :, :],
                                                       in1=gvec.rearrange("p (c o) -> p c o", o=1).to_broadcast([128, KC, 2]), op=ALU.mult),
                      reads=[Asc, vec_t], writes=[Asc])
        kb.phase_tiles += []
        kb.end()

    def modsel(l, part, c, j):
        return modv[:, l, part * KC + c, j:j + 1]

    def norm_mod(src, t0, n, l, wi, j, h_out, xpool, sqpool, sspool, rspool):
        xt = xpool.next()
        kb.dma("sp", xt[:, :, 0:n], fm(src)[:, :, t0:t0 + n], xt, writes=[xt])
        sq = sqpool.next()
        kb.op("act", lambda e: e.activation(out=sq[:, :, 0:n], in_=xt[:, :, 0:n], func=AF.Square), reads=[xt], writes=[sq])
        ss = sspool.next()
        for c in range(KC):
            kb.op("pe", lambda e: e.matmul(ss[:, 0:n], lhsT=ones[:], rhs=sq[:, c, 0:n], start=(c == 0), stop=(c == KC - 1)),
                  reads=[ones, sq], writes=[ss], acc=(c > 0), sig=(c == KC - 1))
        rs = rspool.next()
        kb.op("act", lambda e: e.activation(out=rs[:, 0:n], in_=ss[:, 0:n], func=AF.Ln, bias=float(D * EPS)), reads=[ss], writes=[rs])
        kb.op("act", lambda e: e.activation(out=rs[:, 0:n], in_=rs[:, 0:n], func=AF.Exp, scale=-0.5), reads=[rs], writes=[rs])
        xm = xt
        kb.op("dve", lambda e: e.tensor_tensor(out=xm[:, :, 0:n], in0=xt[:, :, 0:n],
                                               in1=rs[:, 0:n].rearrange("p (o n) -> p o n", o=1).to_broadcast([128, KC, n]), op=ALU.mult),
              reads=[rs], writes=[xm])
        for c in range(KC):
            kb.op("act", lambda e: e.activation(out=h_out[:, c, 0:n], in_=xm[:, c, 0:n], func=AF.Identity,
                                                scale=Asc[:, l, wi, c, j:j + 1], bias=modsel(l, 3 * wi, c, j)),
                  reads=[xm, Asc, modv], writes=[h_out])
        return xt

    def phase_A(l, src):
        kb.begin()
        xpool = kb.pool_tiles("xA", [128, KC, 512], F32, 1)
        sqpool = kb.pool_tiles("sqA", [128, KC, 512], BF16, 1)
        sspool = kb.pool_tiles("ssA", [128, 512], F32, 1, psum=True)
        rspool = kb.pool_tiles("rsA", [128, 512], F32, 1)
        hpool = kb.pool_tiles("hA", [128, KC, 512], BF16, 2)
        wfm = kb.pool_tiles("wfm", [128, KC, 128], BF16, 4)
        wtm = kb.pool_tiles("wtm", [128, KC, 512], BF16, 2)
        wr = kb.pool_tiles("wr", [128, KC, 16], BF16, 2)
        pfm = kb.pool_tiles("pfm", [128, 512], F32, 5, psum=True)
        ptm = kb.pool_tiles("ptm", [128, 512], F32, 2, psum=True)
        obf = kb.pool_tiles("obf", [128, 512], BF16, 4)
        of32 = kb.pool_tiles("of32", [128, 512], F32, 10)
        rT = [kb.sb("rT1", [16, 512], BF16), kb.sb("rT2", [16, 512], BF16)]
        keep = {k_: kb.sb("keep_" + k_, [128, 512], F32) for k_ in ("q", "k1", "k2", "lg1", "lg2", "qs")}
        decp = kb.pool_tiles("decp", [128, 3, 16], F32, 3)
        ropc = kb.sb("ropc", [128, 512], F32)
        rops = kb.sb("rops", [128, 512], F32)
        lbz = (l == 0)

        def store(q, dst, t, n, dtile):
            kb.dma(q, dst, t[:, 0:n], t, reads=[t])

        def gate_pipe(kd, h, s_, lg, qsrc, ksrc, t0, n):
            nch = n // CH
            c0 = t0 // CH
            cum = of32.next()
            kb.op("dve", lambda e: e.tensor_tensor_scan(out=cum[:, 0:n], data0=rmask[:, 0:n], data1=lg[:, 0:n], initial=0.0,
                                                        op0=ALU.mult, op1=ALU.add), reads=[rmask, lg], writes=[cum])
            c3 = cum[:, 0:n].rearrange("p (c t) -> p c t", t=CH)
            if s_ == 2:
                cb_ = of32.next()
                kb.op("dve", lambda e: e.tensor_tensor(out=cb_[:, 0:n].rearrange("p (c t) -> p c t", t=CH),
                                                       in0=c3[:, :, CH - 1:CH].to_broadcast([128, nch, CH]), in1=c3, op=ALU.subtract),
                      reads=[cum], writes=[cb_])
                kb.op("dve", lambda e: e.tensor_tensor(out=cb_[:, 0:n], in0=cb_[:, 0:n], in1=lg[:, 0:n], op=ALU.add),
                      reads=[cb_, lg], writes=[cb_])
                cum = cb_
                c3 = cum[:, 0:n].rearrange("p (c t) -> p c t", t=CH)
                mi, li = CH // 2, 0
            else:
                mi, li = CH // 2 - 1, CH - 1
            dc = decp.next()
            kb.op("act", lambda e: e.activation(out=dc[:, 0, 0:nch], in_=c3[:, :, li], func=AF.Exp), reads=[cum], writes=[dc])
            kb.op("act", lambda e: e.activation(out=dc[:, 2, 0:nch], in_=c3[:, :, mi], func=AF.Exp), reads=[cum], writes=[dc])
            kb.op("dve", lambda e: e.tensor_tensor(out=dc[:, 1, 0:nch], in0=c3[:, :, li], in1=c3[:, :, mi], op=ALU.subtract),
                  reads=[cum], writes=[dc])
            kb.op("act", lambda e: e.activation(out=dc[:, 1, 0:nch], in_=dc[:, 1, 0:nch], func=AF.Exp), reads=[dc], writes=[dc])
            kb.dma("act", dec_d[kd, s_][h * 128:(h + 1) * 128, :, c0:c0 + nch], dc[:, :, 0:nch], dc, reads=[dc])
            d1 = of32.next()
            kb.op("dve", lambda e: e.tensor_tensor(out=d1[:, 0:n].rearrange("p (c t) -> p c t", t=CH), in0=c3,
                                                   in1=c3[:, :, mi:mi + 1].to_broadcast([128, nch, CH]), op=ALU.subtract),
                  reads=[cum], writes=[d1])
            e1 = of32.next()
            kb.op("act", lambda e: e.activation(out=e1[:, 0:n], in_=d1[:, 0:n], func=AF.Exp), reads=[d1], writes=[e1])
            kb.op("act", lambda e: e.activation(out=d1[:, 0:n], in_=d1[:, 0:n], func=AF.Exp, scale=-1.0), reads=[d1], writes=[d1])
            qo = obf.next()
            kb.op("dve", lambda e: e.tensor_tensor(out=qo[:, 0:n], in0=qsrc[:, 0:n], in1=e1[:, 0:n], op=ALU.mult),
                  reads=[qsrc, e1], writes=[qo])
            kb.dma("sp", qt_d[kd, s_][h * 128:(h + 1) * 128, t0:t0 + n], qo[:, 0:n], qo, reads=[qo])
            ko = obf.next()
            kb.op("dve", lambda e: e.tensor_tensor(out=ko[:, 0:n], in0=ksrc[:, 0:n], in1=d1[:, 0:n], op=ALU.mult),
                  reads=[ksrc, d1], writes=[ko])
            kb.dma("sp", kt_d[kd, s_][h * 128:(h + 1) * 128, t0:t0 + n], ko[:, 0:n], ko, reads=[ko])

        hts = {}

        def do_norm(ti):
            t0_, n_ = TT[ti]
            hts[ti] = hpool.next()
            norm_mod(src, t0_, n_, l, 0, 1 if t0_ == 0 else 0, hts[ti], xpool, sqpool, sspool, rspool)

        do_norm(0)
        for ti, (t0, n) in enumerate(TT):
            j = 1 if t0 == 0 else 0
            ht = hts.pop(ti)
            if ti + 1 < len(TT):
                do_norm(ti + 1)
            for tb in range(4):
                wt = wtm.next()
                kb.dma("pool", wt[:], w_tm[l, tb], wt, writes=[wt])
                for m in range(n // 128):
                    pt = ptm.next()
                    for kc in range(KC):
                        kb.op("pe", lambda e: e.matmul(pt[:], lhsT=ht[:, kc, m * 128:(m + 1) * 128], rhs=wt[:, kc, :],
                                                       start=(kc == 0), stop=(kc == KC - 1)),
                              reads=[ht, wt], writes=[pt], acc=(kc > 0), sig=(kc == KC - 1))
                    ot = obf.next()
                    kb.op("act", lambda e: e.copy(out=ot[:], in_=pt[:]), reads=[pt], writes=[ot])
                    r0 = t0 + m * 128
                    if tb < 2:
                        dst = vN[r0:r0 + 128, tb * 512:(tb + 1) * 512]
                    else:
                        dst = sv["hg" if tb == 2 else "gl"][r0:r0 + 128, :]
                    kb.dma("act", dst, ot[:], ot, reads=[ot])
            for s_ in range(2):
                wt = wr.next()
                kb.dma("pool", wt[:], w_r[l, s_], wt, writes=[wt])
                pt = pfm.next()
                for kc in range(KC):
                    kb.op("pe", lambda e: e.matmul(pt[0:16, 0:n], lhsT=wt[:, kc, :], rhs=ht[:, kc, 0:n],
                                                   start=(kc == 0), stop=(kc == KC - 1)),
                          reads=[ht, wt], writes=[pt], acc=(kc > 0), sig=(kc == KC - 1))
                kb.op("act", lambda e: e.copy(out=rT[s_][:, 0:n], in_=pt[0:16, 0:n]), reads=[pt], writes=[rT[s_]])
            kb.dma("sp", ropc[:, 0:n], ropec[:, t0:t0 + n], ropc, writes=[ropc])
            kb.dma("sp", rops[:, 0:n], ropes[:, t0:t0 + n], rops, writes=[rops])
            for (kind, h) in FMB_ORDER:
                bi = FMB.index((kind, h))
                wt = wfm.next()
                kb.dma("pool", wt[:], w_fm[l, bi], wt, writes=[wt])
                pt = pfm.next()
                for kc in range(KC):
                    kb.op("pe", lambda e: e.matmul(pt[:, 0:n], lhsT=wt[:, kc, :], rhs=ht[:, kc, 0:n],
                                                   start=(kc == 0), stop=(kc == KC - 1)),
                          reads=[ht, wt], writes=[pt], acc=(kc > 0), sig=(kc == KC - 1))
                if kind == "naq":
                    ot = obf.next()
                    kb.op("act", lambda e: e.activation(out=ot[:, 0:n], in_=pt[:, 0:n], func=AF.Copy, scale=float(128 ** -0.5)),
                          reads=[pt], writes=[ot])
                    kb.dma("act", qT[h * 128:(h + 1) * 128, t0:t0 + n], ot[:, 0:n], ot, reads=[ot])
                elif kind == "nak":
                    ot = obf.next()
                    kb.op("act", lambda e: e.copy(out=ot[:, 0:n], in_=pt[:, 0:n]), reads=[pt], writes=[ot])
                    kb.dma("act", kT[h * 128:(h + 1) * 128, t0:t0 + n], ot[:, 0:n], ot, reads=[ot])
                elif kind in ("hgg", "glg"):
                    ot = obf.next()
                    kb.op("act", lambda e: e.activation(out=ot[:, 0:n], in_=pt[:, 0:n], func=AF.Silu), reads=[pt], writes=[ot])
                    kb.dma("act", gs[kind[:2]][h * 128:(h + 1) * 128, t0:t0 + n], ot[:, 0:n], ot, reads=[ot])
                elif kind == "hgq":
                    kb.op("act", lambda e: e.activation(out=keep["q"][:, 0:n], in_=pt[:, 0:n], func=AF.Silu),
                          reads=[pt], writes=[keep["q"]])
                elif kind in ("hgz1", "hgz2"):
                    s_ = 1 if kind == "hgz1" else 2
                    sg = of32.next()
                    kb.op("act", lambda e: e.activation(out=sg[:, 0:n], in_=pt[:, 0:n], func=AF.Sigmoid), reads=[pt], writes=[sg])
                    kk, lg = keep["k%d" % s_], keep["lg%d" % s_]
                    if lbz:
                        kb.op("dve", lambda e: e.tensor_scalar(out=kk[:, 0:n], in0=sg[:, 0:n], scalar1=-1.0, scalar2=1.0,
                                                               op0=ALU.mult, op1=ALU.add), reads=[sg], writes=[kk])
                        kb.op("act", lambda e: e.activation(out=lg[:, 0:n], in_=sg[:, 0:n], func=AF.Ln), reads=[sg], writes=[lg])
                    else:
                        kb.op("dve", lambda e: e.tensor_scalar(out=kk[:, 0:n], in0=sg[:, 0:n], scalar1=nml_t[:, s_ - 1, h:h + 1],
                                                               scalar2=oml_t[:, s_ - 1, h:h + 1], op0=ALU.mult, op1=ALU.add),
                              reads=[sg, nml_t, oml_t], writes=[kk])
                        kb.op("act", lambda e: e.activation(out=lg[:, 0:n], in_=sg[:, 0:n], func=AF.Ln,
                                                            scale=oml_t[:, s_ - 1, h:h + 1], bias=lb_t[:, s_ - 1, h:h + 1]),
                              reads=[sg, oml_t, lb_t], writes=[lg])
                    if s_ == 2:
                        for s2 in (1, 2):
                            gate_pipe("hg", h, s2, keep["lg%d" % s2], keep["q"], keep["k%d" % s2], t0, n)
                elif kind in ("glq", "glk"):
                    kb.op("act", lambda e: e.activation(out=keep["qs"][:, 0:n], in_=pt[:, 0:n], func=AF.Copy,
                                                        scale=(0.125 if kind == "glq" else 1.0)), reads=[pt], writes=[keep["qs"]])
                elif kind in ("glqs", "glks"):
                    dstk = keep["q"] if kind == "glqs" else keep["k1"]
                    tmp = of32.next()
                    kb.op("dve", lambda e: e.tensor_tensor(out=tmp[:, 0:n], in0=pt[:, 0:n], in1=rops[:, 0:n], op=ALU.mult),
                          reads=[pt, rops], writes=[tmp])
                    kb.op("dve", lambda e: e.tensor_tensor(out=dstk[:, 0:n], in0=keep["qs"][:, 0:n], in1=ropc[:, 0:n], op=ALU.mult),
                          reads=[keep["qs"], ropc], writes=[dstk])
                    kb.op("dve", lambda e: e.scalar_tensor_tensor(out=dstk[:, 0:n], in0=tmp[:, 0:n],
                                                                  scalar=(0.125 if kind == "glqs" else 1.0), in1=dstk[:, 0:n],
                                                                  op0=ALU.mult, op1=ALU.add), reads=[tmp, dstk], writes=[dstk])
                    if kind == "glks":
                        for s_ in (1, 2):
                            pg = pfm.next()
                            kb.op("pe", lambda e: e.matmul(pg[:, 0:n], lhsT=gup_t[:, l, s_ - 1, h, :], rhs=rT[s_ - 1][:, 0:n],
                                                           start=True, stop=True), reads=[gup_t, rT[s_ - 1]], writes=[pg])
                            sg = of32.next()
                            kb.op("act", lambda e: e.activation(out=sg[:, 0:n], in_=pg[:, 0:n], func=AF.Sigmoid,
                                                                bias=gb_t[:, l, s_ - 1, h:h + 1]), reads=[pg, gb_t], writes=[sg])
                            lg = keep["lg%d" % s_]
                            kb.op("act", lambda e: e.activation(out=lg[:, 0:n], in_=sg[:, 0:n], func=AF.Ln), reads=[sg], writes=[lg])
                            kb.op("dve", lambda e: e.tensor_scalar(out=lg[:, 0:n], in0=lg[:, 0:n], scalar1=1.0 / 16.0, scalar2=None,
                                                                   op0=ALU.mult), reads=[lg], writes=[lg])
                            gate_pipe("gl", h, s_, lg, keep["q"], keep["k1"], t0, n)
        kb.end()

    def scan_pass(l, s_, do_ctx_out):
        kb.begin()
        S = {kd: kb.sb("S" + kd, [128, 4, 128], F32) for kd in ("hg", "gl")}
        Sb = {kd: kb.sb("Sb" + kd, [128, 4, 128], BF16) for kd in ("hg", "gl")}
        dec = {kd: kb.sb("dec" + kd, [128, 4, 3, NCH], F32) for kd in ("hg", "gl")}
        qpool = kb.pool_tiles("sq", [128, 4, 256], BF16, 4)
        kpool = kb.pool_tiles("sk", [128, 4, 256], BF16, 4)
        kSp = kb.pool_tiles("skS", [128, 4, 256], BF16, 4)
        kRp = kb.pool_tiles("skR", [128, 4, 256], BF16, 4)
        qRp = kb.pool_tiles("sqR", [128, 4, 256], BF16, 4)
        vpool = kb.pool_tiles("svv", [32, 8, 512], BF16, 4)
        opool = kb.pool_tiles("so", [128, 4, 256], F32, 4)
        ktm = kb.pool_tiles("ktm", [32, 512], BF16, 4)
        attm = kb.pool_tiles("attm", [32, 4, 32], BF16, 4)
        tmpS = kb.pool_tiles("tmpS", [128, 4, 128], F32, 4)
        p_tr = kb.pool_tiles("p_tr", [32, 512], BF16, 2, psum=True)
        p_at = kb.pool_tiles("p_at", [32, 4, 32], F32, 2, psum=True)
        p_o = kb.pool_tiles("p_o", [128, 4, 32], F32, 2, psum=True)
        pend = {}
        p_u = kb.pool_tiles("p_u", [128, 4, 128], F32, 2, psum=True)
        for kd in ("hg", "gl"):
            kb.dma("sp", dec[kd][:], dec_d[kd, s_].rearrange("(h p) a c -> p h a c", p=128), dec[kd], writes=[dec[kd]])

        def init_state(zero):
            for ki, kd in enumerate(("hg", "gl")):
                if zero:
                    kb.op("pool", lambda e: e.memset(S[kd][:], 0.0), writes=[S[kd]])
                else:
                    g0 = tmpS.next()
                    g1 = tmpS.next()
                    kb.dma("sp", g0[:].rearrange("p h d -> p (h d)"), xg_st[0:128, ki, :], g0, writes=[g0])
                    kb.dma("sp", g1[:].rearrange("p h d -> p (h d)"), xg_st[128:256, ki, :], g1, writes=[g1])
                    kb.op("dve", lambda e: e.tensor_scalar(out=S[kd][:], in0=g0[:], scalar1=pm_t[:, 0:1], scalar2=None, op0=ALU.mult),
                          reads=[g0, pm_t], writes=[S[kd]])
                    kb.op("dve", lambda e: e.scalar_tensor_tensor(out=S[kd][:], in0=g1[:], scalar=pm_t[:, 1:2], in1=S[kd][:],
                                                                  op0=ALU.mult, op1=ALU.add), reads=[g1, pm_t, S[kd]], writes=[S[kd]])

        def run_blocks(blocks, rev):
            steps = []
            for bi, (b0, bn) in enumerate(blocks):
                cl = list(range(bn // CH))
                if rev:
                    cl = cl[::-1]
                for k_, ci in enumerate(cl):
                    steps.append((bi, ci, b0 // CH + ci, k_ == len(cl) - 1))
            loaded = {}

            def load_block(bi):
                if bi >= len(blocks) or bi in loaded:
                    return
                b0, bn = blocks[bi]
                tiles = {}
                for kd in ("hg", "gl"):
                    qt_ = qpool.next(); kt_ = kpool.next(); vt_ = vpool.next(); ot_ = opool.next()
                    kb.dma("sp", qt_[:, :, 0:bn], qt_d[kd, s_].rearrange("(h p) n -> p h n", p=128)[:, :, b0:b0 + bn], qt_, writes=[qt_])
                    kb.dma("sp", kt_[:, :, 0:bn], kt_d[kd, s_].rearrange("(h p) n -> p h n", p=128)[:, :, b0:b0 + bn], kt_, writes=[kt_])
                    kb.dma("sp", vt_[:, 0:bn // CH, :], sv[kd][b0:b0 + bn, :].rearrange("(c p) f -> p c f", p=CH), vt_, writes=[vt_])
                    safe, risky = (0, 1) if s_ == 1 else (1, 0)
                    kS = kSp.next(); kR = kRp.next(); qR = qRp.next()
                    for (dst_, src_, mk) in ((kS, kt_, safe), (kR, kt_, risky), (qR, qt_, risky)):
                        kb.op("dve", lambda e: e.tensor_tensor(out=dst_[:, :, 0:bn], in0=src_[:, :, 0:bn],
                                                                in1=mEL[mk][:, 0:bn].rearrange("p (o n) -> p o n", o=1).to_broadcast([128, 4, bn]),
                                                                op=ALU.mult), reads=[src_, mEL[mk]], writes=[dst_])
                    tiles[kd] = (qt_, kt_, vt_, ot_, kS, kR, qR)
                loaded[bi] = tiles

            def prep(st, kd):
                bi, ci, gc, _ = st
                qt_, kt_, vt_, ot_, kS, kR, qR = loaded[bi][kd]
                cs = slice(ci * CH, (ci + 1) * CH)
                ptr = p_tr.next()
                for h in range(4):
                    kb.op("pe", lambda e: e.transpose(ptr[:, h * 128:(h + 1) * 128], kt_[:, h, cs], ident[:]),
                          reads=[kt_, ident], writes=[ptr], acc=(h > 0), sig=(h == 3))
                km = ktm.next()
                kb.op("act", lambda e: e.copy(out=km[:], in_=ptr[:]), reads=[ptr], writes=[km])
                pa = p_at.next()
                for h in range(4):
                    kb.op("pe", lambda e: e.matmul(pa[:, h, :], lhsT=kS[:, h, cs], rhs=qt_[:, h, cs], start=True, stop=False),
                          reads=[kS, qt_], writes=[pa], acc=(h > 0), sig=False)
                    kb.op("pe", lambda e: e.matmul(pa[:, h, :], lhsT=kR[:, h, cs], rhs=qR[:, h, cs], start=False, stop=True),
                          reads=[kR, qR], writes=[pa], acc=True, sig=(h == 3))
                am = attm.next()
                kb.op("dve", lambda e: e.tensor_tensor(out=am[:], in0=pa[:],
                                                       in1=tri[s_][:].rearrange("s (o t) -> s o t", o=1).to_broadcast([32, 4, 32]),
                                                       op=ALU.mult), reads=[pa, tri[s_]], writes=[am])
                pu = p_u.next()
                for h in range(4):
                    kb.op("pe", lambda e: e.matmul(pu[:, h, :], lhsT=km[:, h * 128:(h + 1) * 128], rhs=vt_[:, ci, h * 128:(h + 1) * 128],
                                                   start=True, stop=True), reads=[km, vt_], writes=[pu], acc=(h > 0), sig=(h == 3))
                tu = tmpS.next()
                kb.op("dve", lambda e: e.tensor_tensor(out=tu[:], in0=pu[:],
                                                       in1=dec[kd][:, :, 1, gc:gc + 1].to_broadcast([128, 4, 128]), op=ALU.mult),
                      reads=[pu, dec[kd]], writes=[tu])
                return am, tu

            def state(st, kd, am, tu):
                bi, ci, gc, _ = st
                qt_, kt_, vt_, ot_, kS, kR, qR = loaded[bi][kd]
                cs = slice(ci * CH, (ci + 1) * CH)
                kb.op("pool", lambda e: e.tensor_tensor(out=Sb[kd][:], in0=S[kd][:],
                                                        in1=dec[kd][:, :, 2, gc:gc + 1].to_broadcast([128, 4, 128]), op=ALU.mult),
                      reads=[S[kd], dec[kd]], writes=[Sb[kd]])
                flush(kd)
                po = p_o.next()
                for h in range(4):
                    kb.op("pe", lambda e: e.matmul(po[:, h, :], lhsT=vt_[:, ci, h * 128:(h + 1) * 128], rhs=am[:, h, :],
                                                   start=True, stop=False), reads=[vt_, am], writes=[po], acc=(h > 0), sig=False)
                    kb.op("pe", lambda e: e.matmul(po[:, h, :], lhsT=Sb[kd][:, h, :], rhs=qt_[:, h, cs],
                                                   start=False, stop=True), reads=[Sb[kd], qt_], writes=[po], acc=True, sig=(h == 3))
                pend[kd] = (po, ot_, cs)
                kb.op("pool", lambda e: e.tensor_tensor(out=S[kd][:], in0=S[kd][:],
                                                        in1=dec[kd][:, :, 0, gc:gc + 1].to_broadcast([128, 4, 128]), op=ALU.mult),
                      reads=[S[kd], dec[kd]], writes=[S[kd]])
                kb.op("dve", lambda e: e.tensor_tensor(out=S[kd][:], in0=S[kd][:], in1=tu[:], op=ALU.add),
                      reads=[S[kd], tu], writes=[S[kd]])

            def flush(kd):
                if kd in pend:
                    po_, ot2, cs2 = pend.pop(kd)
                    kb.op("act", lambda e: e.copy(out=ot2[:, :, cs2], in_=po_[:]), reads=[po_], writes=[ot2])

            load_block(0)
            load_block(1)
            pre = {kd: prep(steps[0], kd) for kd in ("hg", "gl")}
            for i, st in enumerate(steps):
                nxt = None
                if i + 1 < len(steps):
                    nxt = {kd: prep(steps[i + 1], kd) for kd in ("hg", "gl")}
                for kd in ("hg", "gl"):
                    state(st, kd, *pre[kd])
                if st[3]:
                    b0, bn = blocks[st[0]]
                    for kd in ("hg", "gl"):
                        flush(kd)
                        ot_ = loaded[st[0]][kd][3]
                        kb.dma("act", o_d[kd, s_].rearrange("(h p) n -> p h n", p=128)[:, :, b0:b0 + bn], ot_[:, :, 0:bn], ot_, reads=[ot_])
                    load_block(st[0] + 2)
                pre = nxt

        SB = [(b0, 256) for b0 in range(0, NT, 256)]
        if s_ == 1:
            init_state(True)
            run_blocks(SB, False)
            for ki, kd in enumerate(("hg", "gl")):
                kb.dma("pool", xs_st[:, ki, :], S[kd][:].rearrange("p h d -> p (h d)"), S[kd], reads=[S[kd]])
        else:
            init_state(False)
            run_blocks(SB[1:][::-1], True)
            init_state(True)
            run_blocks(SB[:1], True)
        kb.end()

    def exchange1():
        kb.begin()
        a = kb.sb("xa_k", [128, 8, 256], BF16)
        b_ = kb.sb("xa_v", [128, 2, 1024], BF16)
        kb.dma("pool", a[:], kT.rearrange("(h p) n -> p h n", p=128)[:, :, NT - 256:NT], a, writes=[a])
        kb.dma("pool", xs_kv[0:1024, :].rearrange("(h p) n -> p h n", p=128), a[:], a, reads=[a])
        kb.dma("pool", b_[:], vN[NT - 256:NT, :].rearrange("(c p) f -> p c f", p=128), b_, writes=[b_])
        kb.dma("pool", xs_kv[1024:2048, :].rearrange("(c p x) n -> p c (x n)", p=128, x=4), b_[:], b_, reads=[b_])
        kb.barrier()
        g = nc.gpsimd
        cc = kb.ccsem
        for (src, dst) in ((xs_kv, xg_kv), (xs_st.rearrange("p k f -> p (k f)"), xg_st.rearrange("p k f -> p (k f)"))):
            g.collective_compute("AllGather", ALU.bypass, replica_groups=RG, ins=[src], outs=[dst]).then_inc(cc)
            kb.cccount += 1
        g.wait_ge(cc, kb.cccount)
        for r in range(2):
            ka = kb.sb("ka%d" % r, [128, 8, 256], BF16)
            va = kb.sb("va%d" % r, [128, 2, 1024], BF16)
            kb.dma("pool", ka[:], xg_kv[r * 2048:r * 2048 + 1024, :].rearrange("(h p) n -> p h n", p=128), ka, writes=[ka])
            kb.dma("pool", va[:], xg_kv[r * 2048 + 1024:r * 2048 + 2048, :].rearrange("(c p x) n -> p c (x n)", p=128, x=4), va, writes=[va])
            if r == 0:
                ksel = kb.sb("ksel", [128, 8, 256], BF16)
                vsel = kb.sb("vsel", [128, 2, 1024], BF16)
                kb.op("dve", lambda e: e.tensor_scalar(out=ksel[:], in0=ka[:], scalar1=pm_t[:, 0:1], scalar2=None, op0=ALU.mult),
                      reads=[ka, pm_t], writes=[ksel])
                kb.op("dve", lambda e: e.tensor_scalar(out=vsel[:], in0=va[:], scalar1=pm_t[:, 0:1], scalar2=None, op0=ALU.mult),
                      reads=[va, pm_t], writes=[vsel])
            else:
                kb.op("dve", lambda e: e.scalar_tensor_tensor(out=ksel[:], in0=ka[:], scalar=pm_t[:, 1:2], in1=ksel[:],
                                                              op0=ALU.mult, op1=ALU.add), reads=[ka, pm_t, ksel], writes=[ksel])
                kb.op("dve", lambda e: e.scalar_tensor_tensor(out=vsel[:], in0=va[:], scalar=pm_t[:, 1:2], in1=vsel[:],
                                                              op0=ALU.mult, op1=ALU.add), reads=[va, pm_t, vsel], writes=[vsel])
        for a_ in range(2):
            kb.dma("sp", kT.rearrange("(h p) n -> p h n", p=128)[:, :, NT + a_ * 128:NT + (a_ + 1) * 128],
                   ksel[:, :, (1 - a_) * 128:(2 - a_) * 128], ksel, reads=[ksel])
            kb.dma("sp", vN[NT + a_ * 128:NT + (a_ + 1) * 128, :], vsel[:, 1 - a_, :], vsel, reads=[vsel])
        kb.end()

    def attention(l, with_ctx_self):
        kb.begin()
        kc_t = kb.sb("kc_t", [128, 8, CTX], BF16)
        vc_t = kb.sb("vc_t", [128, 2, 1024], BF16)
        kb.dma("sp", kc_t[:], kT.rearrange("(h p) n -> p h n", p=128)[:, :, 0:CTX], kc_t, writes=[kc_t])
        kb.dma("sp", vc_t[:], vN[0:CTX, :].rearrange("(c p) f -> p c f", p=128), vc_t, writes=[vc_t])
        bias_t = [kb.sb("bias%d" % c, [128, 8, 5, 128], BF16) for c in range(ncls)]
        for c in range(ncls):
            kb.dma("pool", bias_t[c][:], nabias[l, c], bias_t[c], writes=[bias_t[c]])
        qpool = kb.pool_tiles("aq", [128, 8, 128], BF16, 2)
        kpool = kb.pool_tiles("ak", [128, 8, 640], BF16, 2)
        vpool = kb.pool_tiles("av", [128, 5, 1024], BF16, 2)
        ppool = kb.pool_tiles("ap", [128, 7, 128], BF16, 3)
        opool = kb.pool_tiles("ao", [128, 8, 128], BF16, 2)
        rpool = kb.pool_tiles("ar", [128, 4, 128], F32, 2)
        p_s = kb.pool_tiles("p_s", [128, 8, 128], F32, 2, psum=True)
        p_n = kb.pool_tiles("p_n", [128, 4, 128], F32, 1, psum=True)
        p_d = kb.pool_tiles("p_d", [128, 4, 128], F32, 1, psum=True)
        qtiles = [(CTX + t * 128, plan[t], cls_of[t]) for t in range(nqt)]
        if with_ctx_self:
            qtiles = [(0, None, None), (128, None, None)] + qtiles
        for (q0, lo, cls) in qtiles:
            qt_ = qpool.next()
            kb.dma("sp", qt_[:], qT.rearrange("(h p) n -> p h n", p=128)[:, :, q0:q0 + 128], qt_, writes=[qt_])
            nloc = 0
            if lo is not None:
                nloc = 5
                kt_ = kpool.next(); vt_ = vpool.next()
                k0 = CTX + lo * 128
                kb.dma("sp", kt_[:], kT.rearrange("(h p) n -> p h n", p=128)[:, :, k0:k0 + 640], kt_, writes=[kt_])
                kb.dma("sp", vt_[:], vN[k0:k0 + 640, :].rearrange("(c p) f -> p c f", p=128), vt_, writes=[vt_])
            nk = nloc + 2
            ot_ = opool.next()
            for hg_ in range(2):
                pn = p_n.next(); pd = p_d.next()
                for hh in range(4):
                    h = hg_ * 4 + hh
                    ps_ = p_s.next()
                    for j in range(nk):
                        if j < nloc:
                            kap = kt_[:, h, j * 128:(j + 1) * 128]
                        else:
                            kap = kc_t[:, h, (j - nloc) * 128:(j - nloc + 1) * 128]
                        hasb = j < nloc
                        kb.op("pe", lambda e: e.matmul(ps_[:, j, :], lhsT=kap, rhs=qt_[:, h, :], start=True, stop=not hasb),
                              reads=[qt_, kc_t] + ([kt_] if nloc else []), writes=[ps_], acc=(j > 0), sig=False)
                        if hasb:
                            kb.op("pe", lambda e: e.matmul(ps_[:, j, :], lhsT=ident[:], rhs=bias_t[cls][:, h, j, :], start=False, stop=True),
                                  reads=[ident, bias_t[cls]], writes=[ps_], acc=True, sig=False)
                    kb.op("pe", lambda e: e.matmul(ps_[:, 7, 0:2], lhsT=ident[:], rhs=ident[:, 0:2], start=True, stop=True),
                          reads=[ident], writes=[ps_], acc=True, sig=True)
                    pt_ = ppool.next()
                    kb.op("act", lambda e: e.activation(out=pt_[:, 0:nk, :], in_=ps_[:, 0:nk, :], func=AF.Exp), reads=[ps_], writes=[pt_])
                    for j in range(nk):
                        if j < nloc:
                            vap = vt_[:, j, h * 128:(h + 1) * 128]
                        else:
                            vap = vc_t[:, j - nloc, h * 128:(h + 1) * 128]
                        kb.op("pe", lambda e: e.matmul(pn[:, hh, :], lhsT=vap, rhs=pt_[:, j, :], start=(j == 0), stop=(j == nk - 1)),
                              reads=[pt_, vc_t] + ([vt_] if nloc else []), writes=[pn], acc=not (j == 0 and hh == 0), sig=False)
                    for j in range(nk):
                        kb.op("pe", lambda e: e.matmul(pd[:, hh, :], lhsT=ones[:], rhs=pt_[:, j, :], start=(j == 0), stop=(j == nk - 1)),
                              reads=[pt_, ones], writes=[pd], acc=not (j == 0 and hh == 0), sig=(j == nk - 1))
                rt_ = rpool.next()
                kb.op("dve", lambda e: e.reciprocal(out=rt_[:], in_=pd[:]), reads=[pd], writes=[rt_])
                kb.op("dve", lambda e: e.tensor_tensor(out=ot_[:, hg_ * 4:(hg_ + 1) * 4, :], in0=pn[:], in1=rt_[:], op=ALU.mult),
                      reads=[pn, rt_], writes=[ot_])
            kb.dma("act", mixT[0:1024, :].rearrange("(h p) n -> p h n", p=128)[:, :, q0:q0 + 128], ot_[:], ot_, reads=[ot_])
        kb.end()

    def post_scan(l, t_list):
        kb.begin()
        o1p = kb.pool_tiles("o1p", [128, 512], F32, 4)
        o2p = kb.pool_tiles("o2p", [128, 512], F32, 4)
        gp = kb.pool_tiles("gp", [128, 512], BF16, 4)
        sqp = kb.pool_tiles("sqp", [128, 512], BF16, 3)
        rsp = kb.pool_tiles("rsp", [128, 512], F32, 3)
        mp = kb.pool_tiles("mp", [128, 512], BF16, 4)
        pss = kb.pool_tiles("pss", [128, 512], F32, 4, psum=True)
        for (t0, n) in t_list:
            for ki, kd in enumerate(("hg", "gl")):
                for h in range(4):
                    o1 = o1p.next(); o2 = o2p.next(); g_ = gp.next()
                    rows = slice(h * 128, (h + 1) * 128)
                    kb.dma("sp", o1[:, 0:n], o_d[kd, 1][rows, t0:t0 + n], o1, writes=[o1])
                    kb.dma("sp", o2[:, 0:n], o_d[kd, 2][rows, t0:t0 + n], o2, writes=[o2])
                    kb.dma("sp", g_[:, 0:n], gs[kd][rows, t0:t0 + n], g_, writes=[g_])
                    kb.op("dve", lambda e: e.tensor_tensor(out=o1[:, 0:n], in0=o1[:, 0:n], in1=o2[:, 0:n], op=ALU.add),
                          reads=[o1, o2], writes=[o1])
                    sq = sqp.next()
                    kb.op("act", lambda e: e.activation(out=sq[:, 0:n], in_=o1[:, 0:n], func=AF.Square), reads=[o1], writes=[sq])
                    ps_ = pss.next()
                    kb.op("pe", lambda e: e.matmul(ps_[:, 0:n], lhsT=ones[:], rhs=sq[:, 0:n], start=True, stop=True),
                          reads=[ones, sq], writes=[ps_])
                    rs = rsp.next()
                    kb.op("act", lambda e: e.activation(out=rs[:, 0:n], in_=ps_[:, 0:n], func=AF.Ln, bias=float(128 * EPS)),
                          reads=[ps_], writes=[rs])
                    kb.op("act", lambda e: e.activation(out=rs[:, 0:n], in_=rs[:, 0:n], func=AF.Exp, scale=-0.5), reads=[rs], writes=[rs])
                    kb.op("dve", lambda e: e.scalar_tensor_tensor(out=o1[:, 0:n], in0=o1[:, 0:n], scalar=hn_t[:, ki, l:l + 1], in1=rs[:, 0:n],
                                                                  op0=ALU.mult, op1=ALU.mult), reads=[o1, rs, hn_t], writes=[o1])
                    m_ = mp.next()
                    kb.op("dve", lambda e: e.scalar_tensor_tensor(out=m_[:, 0:n], in0=o1[:, 0:n], scalar=float(np.sqrt(128.0)), in1=g_[:, 0:n],
                                                                  op0=ALU.mult, op1=ALU.mult), reads=[o1, g_], writes=[m_])
                    r0 = 1024 + ki * 512 + h * 128
                    kb.dma("act", mixT[r0:r0 + 128, t0:t0 + n], m_[:, 0:n], m_, reads=[m_])
        kb.end()

    def out_proj(l, src, dst, t_list):
        kb.begin()
        mpool = kb.pool_tiles("mxt", [128, KC, 512], BF16, 2)
        xpool = kb.pool_tiles("xr", [128, 512], F32, 6)
        wpool = kb.pool_tiles("wo", [128, KC, 128], BF16, 5)
        pp = kb.pool_tiles("po", [128, 512], F32, 4, psum=True)
        for (t0, n) in t_list:
            j = 1 if t0 == 0 else 0
            mt = mpool.next()
            kb.dma("sp", mt[:, :, 0:n], fm(mixT)[:, :, t0:t0 + n], mt, writes=[mt])
            for ob in range(KC):
                wt = wpool.next()
                kb.dma("pool", wt[:], w_out[l, ob], wt, writes=[wt])
                xt = xpool.next()
                kb.dma("sp", xt[:, 0:n], src[ob * 128:(ob + 1) * 128, t0:t0 + n], xt, writes=[xt])
                pt = pp.next()
                for kc in range(KC):
                    kb.op("pe", lambda e: e.matmul(pt[:, 0:n], lhsT=wt[:, kc, :], rhs=mt[:, kc, 0:n], start=(kc == 0), stop=(kc == KC - 1)),
                          reads=[wt, mt], writes=[pt], acc=(kc > 0), sig=(kc == KC - 1))
                kb.op("dve", lambda e: e.scalar_tensor_tensor(out=xt[:, 0:n], in0=pt[:, 0:n], scalar=modsel(l, 2, ob, j), in1=xt[:, 0:n],
                                                              op0=ALU.mult, op1=ALU.add), reads=[pt, xt, modv], writes=[xt])
                kb.dma("act", dst[ob * 128:(ob + 1) * 128, t0:t0 + n], xt[:, 0:n], xt, reads=[xt])
        kb.end()

    def norm2_phase(l, src, t_list):
        kb.begin()
        xpool = kb.pool_tiles("xN", [128, KC, 512], F32, 2)
        sqpool = kb.pool_tiles("sqN", [128, KC, 512], BF16, 1)
        sspool = kb.pool_tiles("ssN", [128, 512], F32, 1, psum=True)
        rspool = kb.pool_tiles("rsN", [128, 512], F32, 1)
        hpool = kb.pool_tiles("hN", [128, KC, 512], BF16, 3)
        for (t0, n) in t_list:
            j = 1 if t0 == 0 else 0
            ht = hpool.next()
            norm_mod(src, t0, n, l, 1, j, ht, xpool, sqpool, sspool, rspool)
            if t0 == 0:
                kb.dma("act", fm(h2c)[:, :, 1:1 + n], ht[:, :, 0:n], ht, reads=[ht])
            else:
                c0 = t0 - CTX + 1
                kb.dma("act", fm(h2T)[:, :, c0:c0 + n], ht[:, :, 0:n], ht, reads=[ht])
                if t0 + n == NT:
                    kb.dma("pool", xs_h, ht[:, :, n - 1], ht, reads=[ht], slow=True)
        kb.dma("sp", fm(h2T)[:, :, 0], zbf[:], zbf, reads=[zbf], slow=True)
        kb.dma("sp", fm(h2c)[:, :, 0], zbf[:], zbf, reads=[zbf], slow=True)
        kb.dma("sp", fm(h2c)[:, :, CTX + 1], zbf[:], zbf, reads=[zbf], slow=True)
        kb.barrier()
        g = nc.gpsimd
        g.collective_compute("AllGather", ALU.bypass, replica_groups=RG, ins=[xs_h], outs=[xg_h]).then_inc(kb.ccsem)
        kb.cccount += 1
        g.wait_ge(kb.ccsem, kb.cccount)
        h0 = kb.sb("hx0", [128, KC], BF16)
        h1 = kb.sb("hx1", [128, KC], BF16)
        hs = kb.sb("hxs", [128, KC], BF16)
        kb.dma("pool", h0[:], xg_h[0:128, :], h0, writes=[h0])
        kb.dma("pool", h1[:], xg_h[128:256, :], h1, writes=[h1])
        kb.op("dve", lambda e: e.tensor_scalar(out=hs[:], in0=h0[:], scalar1=pm_t[:, 0:1], scalar2=None, op0=ALU.mult),
              reads=[h0, pm_t], writes=[hs])
        kb.op("dve", lambda e: e.scalar_tensor_tensor(out=hs[:], in0=h1[:], scalar=pm_t[:, 1:2], in1=hs[:], op0=ALU.mult, op1=ALU.add),
              reads=[h1, pm_t, hs], writes=[hs])
        kb.dma("sp", fm(h2T)[:, :, NL + 1], hs[:], hs, reads=[hs], slow=True)
        kb.end()

    def ffn(l, mid, dst, do_ctx):
        kb.begin()
        wins = []
        if do_ctx:
            wins.append((h2c, 0, CTX + 2, 0))
        s = 0
        while s < NL:
            w = min(512, NL + 2 - s)
            wins.append((h2T, s, w, CTX + s))
            s += w - 2
        hpool = kb.pool_tiles("hF", [128, KC, 512], BF16, 2)
        gpool = kb.pool_tiles("gF", [128, NFB, 512], BF16, 1)
        wup = kb.pool_tiles("wup", [128, KC, 128], BF16, 4)
        wdn = kb.pool_tiles("wdn", [128, NFB, 128], BF16, 2)
        pab = kb.pool_tiles("pab", [128, 512], F32, 4, psum=True)
        pdn = kb.pool_tiles("pdn", [128, 512], F32, 2, psum=True)
        cvp = kb.pool_tiles("cvp", [128, 512], F32, 6)
        xpool = kb.pool_tiles("xF", [128, 512], F32, 3)
        for (hsrc, s0, w, tok0) in wins:
            j = 1 if tok0 == 0 else 0
            no = w - 2
            ht = hpool.next()
            kb.dma("sp", ht[:, :, 0:w], fm(hsrc)[:, :, s0:s0 + w], ht, writes=[ht])
            gt = gpool.next()
            for jb in range(NFB):
                cv = []
                for half in range(2):
                    blk = half * NFB + jb
                    wt = wup.next()
                    kb.dma("pool", wt[:], w_up[l, blk], wt, writes=[wt])
                    pt = pab.next()
                    for kc in range(KC):
                        kb.op("pe", lambda e: e.matmul(pt[:, 0:w], lhsT=wt[:, kc, :], rhs=ht[:, kc, 0:w], start=(kc == 0), stop=(kc == KC - 1)),
                              reads=[wt, ht], writes=[pt], acc=(kc > 0), sig=(kc == KC - 1))
                    c_ = cvp.next()
                    kb.op("act", lambda e: e.activation(out=c_[:, 0:no], in_=pt[:, 1:1 + no], func=AF.Identity,
                                                        scale=cw_t[:, l, 1, blk:blk + 1], bias=cb_t[:, l, blk:blk + 1]),
                          reads=[pt, cw_t, cb_t], writes=[c_])
                    kb.op("dve", lambda e: e.scalar_tensor_tensor(out=c_[:, 0:no], in0=pt[:, 0:no], scalar=cw_t[:, l, 0, blk:blk + 1],
                                                                  in1=c_[:, 0:no], op0=ALU.mult, op1=ALU.add), reads=[pt, cw_t, c_], writes=[c_])
                    kb.op("dve", lambda e: e.scalar_tensor_tensor(out=c_[:, 0:no], in0=pt[:, 2:2 + no], scalar=cw_t[:, l, 2, blk:blk + 1],
                                                                  in1=c_[:, 0:no], op0=ALU.mult, op1=ALU.add), reads=[pt, cw_t, c_], writes=[c_])
                    cv.append(c_)
                sa = cvp.next()
                kb.op("act", lambda e: e.activation(out=sa[:, 0:no], in_=cv[0][:, 0:no], func=AF.Silu), reads=[cv[0]], writes=[sa])
                kb.op("dve", lambda e: e.tensor_tensor(out=gt[:, jb, 0:no], in0=sa[:, 0:no], in1=cv[1][:, 0:no], op=ALU.mult),
                      reads=[sa, cv[1]], writes=[gt])
            for ob in range(KC):
                wt = wdn.next()
                kb.dma("pool", wt[:], w_dn[l, ob], wt, writes=[wt])
                xt = xpool.next()
                kb.dma("sp", xt[:, 0:no], mid[ob * 128:(ob + 1) * 128, tok0:tok0 + no], xt, writes=[xt])
                pt = pdn.next()
                for kc in range(NFB):
                    kb.op("pe", lambda e: e.matmul(pt[:, 0:no], lhsT=wt[:, kc, :], rhs=gt[:, kc, 0:no], start=(kc == 0), stop=(kc == NFB - 1)),
                          reads=[wt, gt], writes=[pt], acc=(kc > 0), sig=(kc == NFB - 1))
                kb.op("dve", lambda e: e.scalar_tensor_tensor(out=xt[:, 0:no], in0=pt[:, 0:no], scalar=modsel(l, 5, ob, j), in1=xt[:, 0:no],
                                                              op0=ALU.mult, op1=ALU.add), reads=[pt, xt, modv], writes=[xt])
                kb.dma("act", dst[ob * 128:(ob + 1) * 128, tok0:tok0 + no], xt[:, 0:no], xt, reads=[xt])
        kb.end()

    def final_norm(src):
        kb.begin()
        xpool = kb.pool_tiles("xZ", [128, KC, 512], F32, 2)
        sqpool = kb.pool_tiles("sqZ", [128, KC, 512], BF16, 1)
        sspool = kb.pool_tiles("ssZ", [128, 512], F32, 1, psum=True)
        rspool = kb.pool_tiles("rsZ", [128, 512], F32, 1)
        fa = kb.sb("fa", [128, KC], F32)
        kb.op("dve", lambda e: e.tensor_scalar(out=fa[:], in0=vec_t[:, 4, :], scalar1=float(np.sqrt(D)), scalar2=None, op0=ALU.mult),
              reads=[vec_t], writes=[fa])
        for (t0, n) in TT[1:]:
            xt = xpool.next()
            kb.dma("sp", xt[:, :, 0:n], fm(src)[:, :, t0:t0 + n], xt, writes=[xt])
            sq = sqpool.next()
            kb.op("act", lambda e: e.activation(out=sq[:, :, 0:n], in_=xt[:, :, 0:n], func=AF.Square), reads=[xt], writes=[sq])
            ss = sspool.next()
            for c in range(KC):
                kb.op("pe", lambda e: e.matmul(ss[:, 0:n], lhsT=ones[:], rhs=sq[:, c, 0:n], start=(c == 0), stop=(c == KC - 1)),
                      reads=[ones, sq], writes=[ss], acc=(c > 0), sig=(c == KC - 1))
            rs = rspool.next()
            kb.op("act", lambda e: e.activation(out=rs[:, 0:n], in_=ss[:, 0:n], func=AF.Ln, bias=float(D * EPS)), reads=[ss], writes=[rs])
            kb.op("act", lambda e: e.activation(out=rs[:, 0:n], in_=rs[:, 0:n], func=AF.Exp, scale=-0.5), reads=[rs], writes=[rs])
            kb.op("dve", lambda e: e.tensor_tensor(out=xt[:, :, 0:n], in0=xt[:, :, 0:n],
                                                   in1=rs[:, 0:n].rearrange("p (o n) -> p o n", o=1).to_broadcast([128, KC, n]), op=ALU.mult),
                  reads=[rs], writes=[xt])
            for c in range(KC):
                kb.op("act", lambda e: e.activation(out=xt[:, c, 0:n], in_=xt[:, c, 0:n], func=AF.Copy, scale=fa[:, c:c + 1]),
                      reads=[xt, fa], writes=[xt])
            kb.dma("act", fm(outT)[:, :, t0 - CTX:t0 - CTX + n], xt[:, :, 0:n], xt, reads=[xt])
        kb.end()

    setup()
    bufs = [xT, xa, xb]
    src = xT
    for l in range(DEPTH):
        mid, dst = xa, xb
        phase_A(l, src)
        if stop_after == ("A", l):
            break
        scan_pass(l, 1, l == 0)
        exchange1()
        scan_pass(l, 2, l == 0)
        if stop_after == ("S", l):
            break
        attention(l, l == 0)
        tl = TT if l == 0 else TT[1:]
        post_scan(l, tl)
        if stop_after == ("M", l):
            break
        out_proj(l, src, mid, tl)
        norm2_phase(l, mid, tl)
        ffn(l, mid, dst, l == 0)
        src = dst
        if stop_after == ("L", l):
            break
    if stop_after is None:
        final_norm(xb)
    kb.stack = pstack
    kb.phase_tiles = persistent
    kb.end()
    kb.gstack.close()
    return nc


def _pc(v, nchunk):
    return np.ascontiguousarray(v.reshape(nchunk, 128).T)


def _tile_w(w, cols):
    cols = np.asarray(cols)
    sub = w[:, np.where(cols < 0, 0, cols)]
    if (cols < 0).any():
        sub = sub.copy()
        sub[:, cols < 0] = 0.0
    k = w.shape[0] // 128
    return np.ascontiguousarray(sub.reshape(k, 128, len(cols)).transpose(1, 0, 2))


def prep(inp, NL, SEQ, B):
    f32 = np.float32
    g = {k: np.asarray(v, dtype=f32) for k, v in inp.items()}
    plan, cls_of, reps, s0, s1 = na_geometry(NL, SEQ)
    ncls = len(reps)
    NT = CTX + NL
    ar = np.arange
    swap64 = np.concatenate([ar(16, 32), ar(0, 16), ar(48, 64), ar(32, 48)])
    pad = -np.ones(64, np.int64)
    shared = {}
    for half in range(2):
        zf, zb = (4096, 4608) if half == 0 else (4608, 4096)
        rf, rb = (7168, 7184) if half == 0 else (7184, 7168)
        d1, d2 = (0, 1) if half == 0 else (1, 0)
        w_fm = np.zeros((DEPTH, NFMB, 128, KC, 128), f32)
        w_r = np.zeros((DEPTH, 2, 128, KC, 16), f32)
        w_tm = np.zeros((DEPTH, 4, 128, KC, 512), f32)
        for l in range(DEPTH):
            w = g["w_in"][l]
            for bi, (kind, h) in enumerate(FMB):
                if kind == "naq": cols = h * 128 + ar(128)
                elif kind == "nak": cols = 1024 + h * 128 + ar(128)
                elif kind == "hgq": cols = 3072 + h * 128 + ar(128)
                elif kind == "hgz1": cols = zf + h * 128 + ar(128)
                elif kind == "hgz2": cols = zb + h * 128 + ar(128)
                elif kind == "hgg": cols = 5120 + h * 128 + ar(128)
                elif kind == "glq": cols = np.concatenate([5632 + h * 64 + ar(64), pad])
                elif kind == "glqs": cols = np.concatenate([5632 + h * 64 + swap64, pad])
                elif kind == "glk": cols = np.concatenate([5888 + h * 64 + ar(64), pad])
                elif kind == "glks": cols = np.concatenate([5888 + h * 64 + swap64, pad])
                elif kind == "glg": cols = 6656 + h * 128 + ar(128)
                w_fm[l, bi] = _tile_w(w, cols)
            w_r[l, 0] = _tile_w(w, rf + ar(16))
            w_r[l, 1] = _tile_w(w, rb + ar(16))
            for tb, c0 in enumerate((2048, 2560, 3584, 6144)):
                w_tm[l, tb] = _tile_w(w, c0 + ar(512))
        lbp = np.zeros((128, 2, DEPTH, 4), f32)
        gbias = np.zeros((128, DEPTH, 2, 4), f32)
        gup = np.zeros((16, DEPTH, 2, 4, 128), f32)
        cw = np.zeros((128, DEPTH, 3, 2 * NFB), f32)
        for l in range(DEPTH):
            for si, dd in enumerate((d1, d2)):
                lbp[:, si, l, :] = _pc(g["hg_lower_bounds"][dd, l], 4)
                for h in range(4):
                    gbias[0:64, l, si, h] = g["gla_gate_b"][l, dd, h * 64:(h + 1) * 64]
                    gup[:, l, si, h, 0:64] = g["gla_gate_up"][l, dd][:, h * 64:(h + 1) * 64]
            for tap in range(3):
                cw[:, l, tap, :] = _pc(g["conv_w"][l, (2 - tap) if half else tap], 2 * NFB)
        shared[half] = dict(w_fm=w_fm, w_r=w_r, w_tm=w_tm, lbp=lbp, gbias=gbias, gup=gup, cw=cw,
                            pmask=np.tile(np.array([[0.0, 1.0]] if half == 0 else [[1.0, 0.0]], f32), (128, 1)))
        st = s1 if half else s0
        nb = np.zeros((DEPTH, ncls, 128, 8, 5, 128), f32)
        for l in range(DEPTH):
            tab = g["na_rpb"][l].reshape(8, -1)
            for c, t in enumerate(reps):
                idx = st[t]
                val = np.where(idx[None] >= 0, tab[:, np.maximum(idx, 0)], f32(NEG))
                nb[l, c] = val.transpose(2, 0, 1, 3)
        shared[half]["nabias"] = nb
        cosT = np.ones((128, NT), f32)
        sinT = np.zeros((128, NT), f32)
        i = ar(NL)
        gpos = (SEQ - 1 - i) if half else i
        inv = (10000.0 ** (-ar(0, 32, 2, dtype=f32) / f32(32))).astype(f32)
        for base, pos in ((0, gpos // GW), (32, gpos % GW)):
            ang = pos.astype(f32)[None, :] * inv[:, None]
            cosT[base:base + 16, CTX:] = np.cos(ang); cosT[base + 16:base + 32, CTX:] = np.cos(ang)
            sinT[base:base + 16, CTX:] = -np.sin(ang); sinT[base + 16:base + 32, CTX:] = np.sin(ang)
        shared[half]["ropec"] = cosT
        shared[half]["ropes"] = sinT
    common = dict(
        ada_w=g["ada_w"],
        ada_bt=np.ascontiguousarray(g["ada_b"].reshape(DEPTH, 96, 128).transpose(0, 2, 1)),
        vecs=np.ascontiguousarray(np.stack([_pc(g["norm1_g"][0], KC), _pc(g["norm1_g"][1], KC), _pc(g["norm2_g"][0], KC),
                                            _pc(g["norm2_g"][1], KC), _pc(g["final_g"], KC)], axis=1)),
        hnorm=np.ascontiguousarray(np.stack([g["hg_norm_g"].T, g["gla_norm_g"].T], axis=1)),
        cb=np.ascontiguousarray(np.stack([_pc(g["conv_b"][l], 2 * NFB) for l in range(DEPTH)], axis=1)),
        w_out=np.ascontiguousarray(np.stack([np.stack([_tile_w(g["w_out"][l], ob * 128 + ar(128)) for ob in range(KC)]) for l in range(DEPTH)])),
        w_up=np.ascontiguousarray(np.stack([np.stack([_tile_w(g["w_up"][l], bk * 128 + ar(128)) for bk in range(2 * NFB)]) for l in range(DEPTH)])),
        w_dn=np.ascontiguousarray(np.stack([np.stack([_tile_w(g["w_down"][l], ob * 128 + ar(128)) for ob in range(KC)]) for l in range(DEPTH)])),
    )
    in_maps = []
    for r in range(2 * B):
        b, half = r // 2, r % 2
        xl = g["x"][b, half * NL:(half + 1) * NL]
        cx = g["ctx"][b]
        if half:
            xl, cx = xl[::-1], cx[::-1]
        m = dict(common)
        m.update(shared[half])
        m["xT"] = np.ascontiguousarray(np.concatenate([cx, xl], axis=0).T)
        m["cs"] = np.ascontiguousarray(np.stack([_pc(g["c"][b], KC), _pc(g["c_ctx"], KC)], axis=2))
        in_maps.append(m)
    return in_maps, ncls, plan, cls_of


_CACHE = {}


def run(inp, NL, SEQ, B, debug=(), stop_after=None):
    in_maps, ncls, plan, cls_of = prep(inp, NL, SEQ, B)
    nc = build(NL, SEQ, ncls, plan, cls_of, debug=debug, stop_after=stop_after)
    res = run_bass_kernel_spmd(nc, in_maps, core_ids=list(range(2 * B)))
    return res


def kernel(**inputs):
    B, SEQ, _ = inputs["x"].shape
    NL = SEQ // 2
    res = run(inputs, NL, SEQ, B)
    out = np.zeros((B, SEQ, D), np.float32)
    for r in range(2 * B):
        b, half = r // 2, r % 2
        o = np.asarray(res.results[r]["outT"]).T
        if half:
            out[b, NL:] = o[::-1]
        else:
            out[b, :NL] = o
    return out
```
